# Optimizing a Trainium2 kernel written in Bass

```python
import jax
import jax.numpy as jnp
from jax import lax
import numpy as np

D_MODEL = 1024
BATCH = 4
SEQ = 4096
DEPTH = 2
DEC_BATCH = 128
DEC_SEQ = 1
PAST_LEN = 2048
PAGE_SIZE = 128

D_MIX = D_MODEL
HEAD_DIM = 64
D_NSA = D_MIX // 2
N_HEADS = D_NSA // HEAD_DIM
N_KV = 2
GROUP = N_HEADS // N_KV
D_KV = 2 * N_KV * HEAD_DIM
CMP_BLOCK = 64
N_SELECT = 16
WINDOW = 512
Q_BLOCK = 64
D_POOL = D_MIX // 4
POOL_WINDOWS = (2, 4, 8, 16)
N_POOL_GROUPS = 4
POOL_GROUP_DIM = D_POOL // N_POOL_GROUPS
POOL_MAX = 16
D_GMLP = D_MIX - D_NSA - D_POOL
N_GMLP_GROUPS = 4
GMLP_GROUP_DIM = D_GMLP // N_GMLP_GROUPS
CHUNK = 128
D_IN = 2 * D_NSA + 3 * D_KV + 3 * N_HEADS + 2 * D_POOL + 3 * D_GMLP
EPS = 1e-6
NEG_INF = -1e30
SEL_FORCED = 1e4
SEL_INVALID = -1e4

kernel_name = 'hymba_nsa_pool_gmlp_step'


def rms_norm(x, g):
    xf = x.astype(jnp.float32)
    y = xf * lax.rsqrt(jnp.mean(xf * xf, axis=-1, keepdims=True) + EPS)
    return (y * g.astype(jnp.float32)).astype(x.dtype)


def masked_softmax(s, mask):
    s = jnp.where(mask, s, NEG_INF)
    p = jnp.exp(s - jnp.max(s, axis=-1, keepdims=True)) * mask
    return p / jnp.maximum(jnp.sum(p, axis=-1, keepdims=True), 1e-30)


def alibi_slopes():
    h = jnp.arange(1, N_HEADS + 1, dtype=jnp.float32)
    return jnp.exp2(-8.0 * h / N_HEADS).reshape(N_KV, GROUP)


def project(h, w):
    p = jnp.einsum('btd,de->bte', h, w)
    sizes = (D_NSA, D_KV, D_KV, D_KV, 3 * N_HEADS, D_NSA, D_POOL, D_POOL, D_GMLP, D_GMLP, D_GMLP)
    return jnp.split(p, np.cumsum(sizes)[:-1].tolist(), axis=-1)


def compress_blocks(kv, pe, w1, w2):
    b, L = kv.shape[0], kv.shape[1]
    n = L // CMP_BLOCK
    blocks = kv[:, :n * CMP_BLOCK].astype(jnp.float32).reshape(b, n, CMP_BLOCK, 2, N_KV, HEAD_DIM)
    blocks = blocks + jnp.swapaxes(pe, 0, 1)[None, None, :, :, None, :]
    hid = jax.nn.silu(jnp.einsum('bncrkd,rde->bncrke', blocks, w1))
    return jnp.einsum('bnrkd,rde->bnrke', jnp.mean(hid, axis=2), w2)


def nsa_attend(q, gates, kv_cmp, kv_sel, kv_win, q_pos0, win_pos0, pe, w1, w2):
    f32 = jnp.float32
    b, tq = q.shape[0], q.shape[1]
    L = kv_sel.shape[1]
    slopes = alibi_slopes()
    sl5 = slopes[None, None, :, :, None]
    qf = q.astype(f32).reshape(b, tq, N_KV, GROUP, HEAD_DIM) * (HEAD_DIM ** -0.5)
    t_pos = q_pos0 + jnp.arange(tq)
    kvc = compress_blocks(kv_cmp, pe.astype(f32), w1.astype(f32), w2.astype(f32))
    n_cmp = kvc.shape[1]
    dist_c = t_pos[:, None] - ((jnp.arange(n_cmp) + 1) * CMP_BLOCK - 1)[None, :]
    s_c = jnp.einsum('bqkgd,bnkd->bqkgn', qf, kvc[:, :, 0]) - sl5 * dist_c.astype(f32)[None, :, None, None, :]
    p_c = masked_softmax(s_c, (dist_c >= 0)[None, :, None, None, :])
    o_c = jnp.einsum('bqkgn,bnkd->bqkgd', p_c, kvc[:, :, 1])
    n_blk = -(-L // CMP_BLOCK)
    k_eff = min(N_SELECT, n_blk)
    imp = jnp.pad(jnp.sum(p_c, axis=3), ((0, 0), (0, 0), (0, 0), (0, n_blk - n_cmp)))
    blk = jnp.arange(n_blk)
    cur = t_pos // CMP_BLOCK
    forced = (blk[None, :] == cur[:, None]) | (blk[None, :] == 0)
    started = blk[None, :] <= cur[:, None]
    score = jnp.where(forced[None, :, None, :], SEL_FORCED,
                      jnp.where(started[None, :, None, :], imp, SEL_INVALID))
    top_val, top_idx = lax.top_k(score, k_eff)
    top_ok = top_val > SEL_INVALID * 0.5
    kb = jnp.pad(kv_sel.astype(f32), ((0, 0), (0, n_blk * CMP_BLOCK - L), (0, 0), (0, 0), (0, 0)))
    kb = kb.reshape(b, n_blk, CMP_BLOCK, 2, N_KV, HEAD_DIM).transpose(0, 4, 1, 2, 3, 5)
    kvw = jnp.pad(kv_win.astype(f32), ((0, 0), (WINDOW, 0), (0, 0), (0, 0), (0, 0)))
    qblk = Q_BLOCK if tq % Q_BLOCK == 0 else tq
    n_qb = tq // qblk

    def to_blocks(a):
        return a.reshape((b, n_qb, qblk) + a.shape[2:]).swapaxes(0, 1)

    def from_blocks(a):
        return a.swapaxes(0, 1).reshape((b, tq) + a.shape[3:])

    bi = jnp.arange(b)[:, None, None, None]
    ki = jnp.arange(N_KV)[None, None, :, None]
    w_off = jnp.arange(WINDOW + qblk)
    in_blk = jnp.arange(CMP_BLOCK)

    def sweep(args):
        qi, idx, ok, start = args
        tqb = q_pos0 + start + jnp.arange(qblk)
        g = kb[bi, ki, idx].reshape(b, qblk, N_KV, k_eff * CMP_BLOCK, 2, HEAD_DIM)
        s_pos = (idx[..., None] * CMP_BLOCK + in_blk).reshape(b, qblk, N_KV, k_eff * CMP_BLOCK)
        dist_s = tqb[None, :, None, None] - s_pos
        mask_s = (dist_s >= 0) & jnp.repeat(ok, CMP_BLOCK, axis=-1)
        s_s = jnp.einsum('bqkgd,bqksd->bqkgs', qi, g[..., 0, :]) - sl5 * dist_s.astype(f32)[:, :, :, None, :]
        o_s = jnp.einsum('bqkgs,bqksd->bqkgd', masked_softmax(s_s, mask_s[:, :, :, None, :]), g[..., 1, :])
        wk = lax.dynamic_slice_in_dim(kvw, q_pos0 + start - win_pos0, WINDOW + qblk, axis=1)
        w_pos = q_pos0 + start - WINDOW + w_off
        dist_w = tqb[:, None] - w_pos[None, :]
        mask_w = (dist_w >= 0) & (dist_w < WINDOW) & (w_pos >= win_pos0)[None, :]
        s_w = jnp.einsum('bqkgd,bskd->bqkgs', qi, wk[:, :, 0]) - sl5 * dist_w.astype(f32)[None, :, None, None, :]
        o_w = jnp.einsum('bqkgs,bskd->bqkgd', masked_softmax(s_w, mask_w[None, :, None, None, :]), wk[:, :, 1])
        return o_s, o_w

    o_s, o_w = lax.map(sweep, (to_blocks(qf), to_blocks(top_idx), to_blocks(top_ok), jnp.arange(n_qb) * qblk))
    o_s = from_blocks(o_s)
    o_w = from_blocks(o_w)
    gf = gates.astype(f32).reshape(b, tq, N_KV, GROUP, 3)
    o = gf[..., 0:1] * o_c + gf[..., 1:2] * o_s + gf[..., 2:3] * o_w
    return o.reshape(b, tq, D_NSA)


def pool_mix(xin, prev, pos0, w_pool, scale):
    b, t = xin.shape[0], xin.shape[1]
    xf = jnp.concatenate([prev, xin], axis=1).astype(jnp.float32)
    cs = jnp.cumsum(jnp.pad(xf, ((0, 0), (1, 0), (0, 0))), axis=1)
    hi = cs[:, POOL_MAX:]
    pos = pos0 + jnp.arange(t)
    pooled = []
    for gi, w in enumerate(POOL_WINDOWS):
        sl = slice(gi * POOL_GROUP_DIM, (gi + 1) * POOL_GROUP_DIM)
        lo = cs[:, POOL_MAX - w:POOL_MAX - w + t, sl]
        cnt = jnp.minimum(w, pos + 1).astype(jnp.float32)[None, :, None]
        pooled.append((hi[:, :, sl] - lo) / cnt)
    diff = (jnp.concatenate(pooled, axis=-1) - xf[:, POOL_MAX - 1:]).reshape(b, t, N_POOL_GROUPS, POOL_GROUP_DIM)
    y = jnp.einsum('btgc,gce->btge', diff, w_pool).reshape(b, t, D_POOL)
    return y * scale


def gmlp_mix(u, v, g_norm, ws, bs, rows):
    b, t = u.shape[0], u.shape[1]
    vg = v.astype(jnp.float32).reshape(b, t, N_GMLP_GROUPS, GMLP_GROUP_DIM)
    vn = vg * lax.rsqrt(jnp.mean(vg * vg, axis=-1, keepdims=True) + EPS) \
        * g_norm.astype(jnp.float32).reshape(N_GMLP_GROUPS, GMLP_GROUP_DIM)
    n = t // rows
    w = ws[:, :rows, :rows] * jnp.tril(jnp.ones((rows, rows), dtype=ws.dtype))
    s = jnp.einsum('gij,bnjgc->bnigc', w, vn.reshape(b, n, rows, N_GMLP_GROUPS, GMLP_GROUP_DIM))
    s = s + jnp.swapaxes(bs[:, :rows], 0, 1)[None, None, :, :, None]
    return u * s.reshape(b, t, D_GMLP), vn.reshape(b, t, D_GMLP).astype(v.dtype)


def merge_groups(x, ya, za, yb, zb, yc, zc, w_out, g_post):
    mix = jnp.concatenate([ya * jax.nn.silu(za), yb * jax.nn.silu(zb), yc * jax.nn.silu(zc)], axis=-1)
    out = jnp.einsum('bte,ed->btd', mix, w_out)
    return x + rms_norm(out, g_post).astype(x.dtype)


def setup_inputs(seed: int = 0) -> dict:
    key = jax.random.key(seed)
    k = jax.random.split(key, 20)
    f32 = jnp.float32
    n_pages = PAST_LEN // PAGE_SIZE
    n_used = DEC_BATCH * n_pages
    n_phys = n_used + max(1, n_used // 4)
    win_keep = min(WINDOW, PAST_LEN)

    def nrm(kk, shape, s=1.0):
        return s * jax.random.normal(kk, shape, f32)

    page_table = jax.random.permutation(k[0], n_phys)[:n_used].reshape(DEC_BATCH, n_pages).astype(jnp.int32)
    return {
        'x_prompt': nrm(k[1], (BATCH, SEQ, D_MODEL)),
        'x_sample': nrm(k[2], (DEC_BATCH, DEC_SEQ, D_MODEL)),
        'cache_kv_cmp': nrm(k[3], (DEPTH, n_phys, PAGE_SIZE, 2, N_KV, HEAD_DIM)),
        'cache_kv_sel': nrm(k[4], (DEPTH, n_phys, PAGE_SIZE, 2, N_KV, HEAD_DIM)),
        'cache_kv_win': nrm(k[5], (DEPTH, DEC_BATCH, win_keep, 2, N_KV, HEAD_DIM)),
        'state_pool': nrm(k[6], (DEPTH, DEC_BATCH, POOL_MAX - 1, D_POOL)),
        'page_table': page_table,
        'norm_pre': 1.0 + nrm(k[7], (DEPTH, D_MODEL), 0.02),
        'w_in': nrm(k[8], (DEPTH, D_MODEL, D_IN), D_MODEL ** -0.5),
        'cmp_pe': nrm(k[9], (DEPTH, 2, CMP_BLOCK, HEAD_DIM), 0.1),
        'cmp_w1': nrm(k[10], (DEPTH, 2, HEAD_DIM, HEAD_DIM), HEAD_DIM ** -0.5),
        'cmp_w2': nrm(k[11], (DEPTH, 2, HEAD_DIM, HEAD_DIM), HEAD_DIM ** -0.5),
        'pool_w': nrm(k[12], (DEPTH, N_POOL_GROUPS, POOL_GROUP_DIM, POOL_GROUP_DIM), POOL_GROUP_DIM ** -0.5),
        'pool_scale': 1.0 + nrm(k[13], (DEPTH, D_POOL), 0.02),
        'gmlp_norm': 1.0 + nrm(k[14], (DEPTH, D_GMLP), 0.02),
        'gmlp_ws': nrm(k[15], (DEPTH, N_GMLP_GROUPS, CHUNK, CHUNK), CHUNK ** -0.5),
        'gmlp_bs': 1.0 + nrm(k[16], (DEPTH, N_GMLP_GROUPS, CHUNK), 0.02),
        'w_out': nrm(k[17], (DEPTH, D_MIX, D_MODEL), D_MIX ** -0.5),
        'norm_post': 1.0 + nrm(k[18], (DEPTH, D_MODEL), 0.02),
    }


def reference(x_prompt, x_sample, cache_kv_cmp, cache_kv_sel, cache_kv_win, state_pool, page_table,
              norm_pre, w_in, cmp_pe, cmp_w1, cmp_w2, pool_w, pool_scale, gmlp_norm, gmlp_ws, gmlp_bs,
              w_out, norm_post):
    bp, tp = x_prompt.shape[0], x_prompt.shape[1]
    bs_, ts = x_sample.shape[0], x_sample.shape[1]
    past_len = page_table.shape[1] * PAGE_SIZE
    win_keep = cache_kv_win.shape[2]
    xp, xs = x_prompt, x_sample
    kvc_p, kvc_s, kvs_p, kvs_s, kvw_p, kvw_s, pool_p, pool_s, gv_p, gv_s = ([] for _ in range(10))
    for l in range(DEPTH):
        q, kc, ks, kw, gl, za, pin, zb, u, v, zc = project(rms_norm(xp, norm_pre[l]), w_in[l])
        kc, ks, kw = [a.reshape(bp, tp, 2, N_KV, HEAD_DIM) for a in (kc, ks, kw)]
        gates = jax.nn.sigmoid(gl.astype(jnp.float32)).reshape(bp, tp, N_HEADS, 3)
        ya = nsa_attend(q.reshape(bp, tp, N_HEADS, HEAD_DIM), gates, kc, ks, kw, 0, 0,
                        cmp_pe[l], cmp_w1[l], cmp_w2[l])
        yb = pool_mix(pin, jnp.zeros((bp, POOL_MAX - 1, D_POOL), pin.dtype), 0, pool_w[l], pool_scale[l])
        yc, vn = gmlp_mix(u, v, gmlp_norm[l], gmlp_ws[l], gmlp_bs[l], CHUNK)
        xp = merge_groups(xp, ya, za, yb, zb, yc, zc, w_out[l], norm_post[l])
        kvc_p.append(kc)
        kvs_p.append(ks)
        kvw_p.append(kw[:, tp - min(WINDOW, tp):])
        pool_p.append(pin[:, tp - (POOL_MAX - 1):])
        gv_p.append(vn[:, tp - CHUNK:])
        q, kc, ks, kw, gl, za, pin, zb, u, v, zc = project(rms_norm(xs, norm_pre[l]), w_in[l])
        kc, ks, kw = [a.reshape(bs_, ts, 2, N_KV, HEAD_DIM) for a in (kc, ks, kw)]
        gates = jax.nn.sigmoid(gl.astype(jnp.float32)).reshape(bs_, ts, N_HEADS, 3)
        kc_full = jnp.concatenate(
            [cache_kv_cmp[l][page_table].reshape(bs_, past_len, 2, N_KV, HEAD_DIM), kc], axis=1)
        ks_full = jnp.concatenate(
            [cache_kv_sel[l][page_table].reshape(bs_, past_len, 2, N_KV, HEAD_DIM), ks], axis=1)
        kw_full = jnp.concatenate([cache_kv_win[l], kw], axis=1)
        ya = nsa_attend(q.reshape(bs_, ts, N_HEADS, HEAD_DIM), gates, kc_full, ks_full, kw_full,
                        past_len, past_len - win_keep, cmp_pe[l], cmp_w1[l], cmp_w2[l])
        yb = pool_mix(pin, state_pool[l], past_len, pool_w[l], pool_scale[l])
        yc, vn = gmlp_mix(u, v, gmlp_norm[l], gmlp_ws[l], gmlp_bs[l], ts)
        xs = merge_groups(xs, ya, za, yb, zb, yc, zc, w_out[l], norm_post[l])
        kvc_s.append(kc)
        kvs_s.append(ks)
        kvw_s.append(kw_full[:, ts:])
        pool_s.append(jnp.concatenate([state_pool[l], pin], axis=1)[:, ts:])
        gv_s.append(vn)
    return (xp, xs,
            jnp.stack(kvc_p), jnp.stack(kvc_s),
            jnp.stack(kvs_p), jnp.stack(kvs_s),
            jnp.stack(kvw_p), jnp.stack(kvw_s),
            jnp.stack(pool_p), jnp.stack(pool_s),
            jnp.stack(gv_p), jnp.stack(gv_s))
```

```python
import numpy as np
import ml_dtypes
from contextlib import ExitStack
import concourse.bass as bass
import concourse.mybir as mybir
from concourse.bass_utils import run_bass_kernel_spmd

F32 = mybir.dt.float32
BF16 = mybir.dt.bfloat16
I32 = mybir.dt.int32
AF = mybir.ActivationFunctionType
ALU = mybir.AluOpType
AX = mybir.AxisListType

NCORES = 8
T = 4096
NT = T // 128
D = 1024
DIN = 3096
NS = 16
NPG = 16
NPHYS = 2560
EPS = 1e-6
SLOPES = [2.0 ** (-(h + 1)) for h in range(8)]
NEG = -30000.0

PG = [(0, 512), (512, 1024), (1024, 1304), (1304, 1816), (1816, 2328), (2328, 2840), (2840, 3096)]


class Buf:
    __slots__ = ("name", "w", "r", "excl")

    def __init__(self, name, excl=False):
        self.name = name
        self.w = None
        self.r = {}
        self.excl = excl


class Sched:
    ENG = ["pe", "act", "dve", "pool", "sp"]
    ATTR = {"pe": "tensor", "act": "scalar", "dve": "vector", "pool": "gpsimd", "sp": "sync"}

    def __init__(self, nc, es):
        self.nc = nc
        self.es = es
        self.sem = {e: es.enter_context(nc.semaphore("s_" + e)) for e in self.ENG}
        self.cnt = {e: 0 for e in self.ENG}
        self.prog = {e: [] for e in self.ENG}
        self.waited = {e: {} for e in self.ENG}
        self.streams = {}
        self.pending = {e: [] for e in self.ENG}

    def stream(self, name):
        if name not in self.streams:
            self.streams[name] = [self.es.enter_context(self.nc.semaphore("d_" + name)), 0]
        return self.streams[name]

    def fence(self):
        snap = [(self.sem[e], self.cnt[e]) for e in self.ENG if self.cnt[e] > 0]
        snap += [(st[0], st[1]) for st in self.streams.values() if st[1] > 0]
        for e in self.ENG:
            self.pending[e] = list(snap)

    def op(self, eng, fn, reads=(), writes=(), dma=None):
        writes = list(writes) + [b for b in reads if b.excl]
        reads = [b for b in reads if not b.excl]
        deps = {}

        def add(ev):
            if ev is None:
                return
            k = id(ev[0])
            if k not in deps or deps[k][1] < ev[1]:
                deps[k] = ev

        for b in reads:
            add(b.w)
        for b in writes:
            if not b.r:
                add(b.w)
            for ev in b.r.values():
                add(ev)
        if dma is None:
            self.cnt[eng] += 1
            ev = (self.sem[eng], self.cnt[eng], eng)
            inc = (self.sem[eng], 1)
        else:
            st = self.stream(dma)
            st[1] += 16
            ev = (st[0], st[1], "dma")
            inc = (st[0], 16)
        waits = []
        for (sem, val) in self.pending[eng]:
            k = id(sem)
            if sem is self.sem.get(eng):
                continue
            if self.waited[eng].get(k, 0) >= val:
                continue
            self.waited[eng][k] = val
            waits.append((sem, val))
        self.pending[eng] = []
        for k, (sem, val, src) in deps.items():
            if src == eng and eng == "pe":
                continue
            if self.waited[eng].get(k, 0) >= val:
                continue
            self.waited[eng][k] = val
            waits.append((sem, val))
        self.prog[eng].append((waits, fn, inc))
        k = id(ev[0])
        for b in reads:
            if k not in b.r or b.r[k][1] < ev[1]:
                b.r[k] = ev
        for b in writes:
            b.w = ev
            b.r = {}
        return ev

    def emit(self):
        nc = self.nc
        finals = [(self.sem[e], self.cnt[e]) for e in self.ENG if e != "sp" and self.cnt[e] > 0]
        finals += [(st[0], st[1]) for st in self.streams.values() if st[1] > 0]
        with nc.Block() as block:
            for e in self.ENG:
                prog = self.prog[e]

                def body(engine, prog=prog, e=e):
                    for waits, fn, inc in prog:
                        for sem, val in waits:
                            engine.wait_ge(sem, val)
                        ins = fn(engine)
                        ins.then_inc(inc[0], inc[1])
                    if e == "sp":
                        for sem, val in finals:
                            engine.wait_ge(sem, val)

                getattr(block, self.ATTR[e])(body)


class TB:
    def __init__(self, t, name):
        self.t = t
        self.b = Buf(name)

    def __getitem__(self, idx):
        return self.t[idx]


def build_program(layers=(0, 1), do_prompt=True, do_sample=True, nt_limit=NT, ns_limit=NS, dbg_y0=False, stage=99):
    nc = bass.Bass("TRN2", target_bir_lowering=False)

    def din(name, shape, dt=F32):
        return nc.dram_tensor(name, list(shape), dt, kind="ExternalInput").ap()

    def dout(name, shape, dt=F32):
        return nc.dram_tensor(name, list(shape), dt, kind="ExternalOutput").ap()

    xp = din("xp", [T, D])
    xs = din("xs", [NS, D])
    ncache = 2 * NPHYS * 128 if do_sample else 128
    ckc = din("ckc", [ncache, 256])
    cks = din("cks", [ncache, 256])
    ckw = din("ckw", [2, NS, 512, 256])
    spool = din("spool", [2, NS, 15, 256])
    ptab = din("ptab", [128, NS * NPG], I32)
    w_in = din("w_in", [2, D, DIN])
    w_out = din("w_out", [2, D, D])
    npre_t = din("npre_t", [2, 128, 8])
    gpost_rep = din("gpost_rep", [2, 128, D])
    pe_tok = din("pe_tok", [2, 128, 256])
    w1bd = din("w1bd", [2, 128, 2, 128])
    w2kv = din("w2kv", [2, 128, 4, 64])
    wpool = din("wpool", [2, 64, 4, 64])
    pscale_rep = din("pscale_rep", [2, 128, 256])
    gnorm_rep = din("gnorm_rep", [2, 128, 256])
    wsT = din("wsT", [2, 128, 4, 128])
    bsT = din("bsT", [2, 128, 4])
    ws00_rep = din("ws00_rep", [2, 128, 4])
    bs0_rep = din("bs0_rep", [2, 128, 4])
    c_ident = din("c_ident", [128, 128], BF16)
    c_identf = din("c_identf", [128, 128])
    c_trile = din("c_trile", [128, 128], BF16)
    c_trigt = din("c_trigt", [128, 128], BF16)
    c_kaug = din("c_kaug", [4, T], BF16)
    c_qaug = din("c_qaug", [NT, 4, 2048], BF16)
    c_kmask = din("c_kmask", [32, T], BF16)
    c_expe = din("c_expe", [64, T], BF16)
    c_alb = din("c_alb", [128, 512])
    c_iota = din("c_iota", [128, 64])
    c_misc = din("c_misc", [128, 8])
    c_force0 = din("c_force0", [128, 64])
    c_poolb = din("c_poolb", [128, 3, 4, 128], BF16)
    c_trilT = din("c_trilT", [128, 128])
    c_salb = din("c_salb", [128, 17, 8])
    c_swalb = din("c_swalb", [128, 5, 8])
    c_swmask = din("c_swmask", [128, 5])
    c_scalb = din("c_scalb", [4, 2, 32])
    c_sexpe = din("c_sexpe", [33, 17 * 128], BF16)
    c_sforce = din("c_sforce", [32, 33])
    c_spool = din("c_spool", [16, 4, 16])
    c_smask0 = din("c_smask0", [128, 17])

    y_p = dout("y_p", [T, D])
    kvc_p = dout("kvc_p", [2, T, 256])
    kvs_p = dout("kvs_p", [2, T, 256])
    kvw_p = dout("kvw_p", [2, 512, 256])
    pool_p = dout("pool_p", [2, 15, 256])
    gv_p = dout("gv_p", [2, 128, 256])
    y_s = dout("y_s", [NS, D])
    kvc_s = dout("kvc_s", [2, NS, 256])
    kvs_s = dout("kvs_s", [2, NS, 256])
    kvw_s = dout("kvw_s", [2, NS, 512, 256])
    pool_s = dout("pool_s", [2, NS, 15, 256])
    gv_s = dout("gv_s", [2, NS, 256])
    xp1 = nc.dram_tensor("xp1_scratch", [T, D], F32, kind="Internal").ap()
    bounce = nc.dram_tensor("bounce_scratch", [8, 4096], F32, kind="Internal").ap()
    xs1 = nc.dram_tensor("xs1_scratch", [NS, D], F32, kind="Internal").ap()

    with ExitStack() as es:
        S = Sched(nc, es)
        uid = [0]

        def sb(shape, dt, name, stack=None):
            uid[0] += 1
            nm = f"{name}_{uid[0]}"
            t = (stack or es).enter_context(nc.sbuf_tensor(nm, list(shape), dt))
            return TB(t, nm)

        def ps(shape, dt, name):
            uid[0] += 1
            nm = f"{name}_{uid[0]}"
            t = es.enter_context(nc.psum_tensor(nm, list(shape), dt))
            tb = TB(t, nm)
            tb.b.excl = True
            return tb

        def load(dst_ap, src_ap, bufs, stream=None, reads=(), eng="sp"):
            stream = "L_" + bufs[0].name
            S.op(eng, lambda e: e.dma_start(out=dst_ap, in_=src_ap), reads=reads, writes=bufs, dma=stream)

        def store(dst_ap, src_ap, bufs, stream, writes=(), eng="pool"):
            S.op(eng, lambda e: e.dma_start(out=dst_ap, in_=src_ap), reads=bufs, writes=writes, dma=stream)

        TP = ps([128, 1024], BF16, "TP")
        PJ = [ps([128, 512], F32, "PJ0"), ps([128, 512], F32, "PJ1")]
        ST = [ps([128, 512], F32, "ST0"), ps([128, 512], F32, "ST1")]
        OA = ps([128, 512], F32, "OA")
        MK = ps([128, 4, 128], F32, "MK")
        MKb = [MK.b for j in range(4)]
        MS = ps([128, 512], F32, "MS")
        TPF = TP

        ident = sb([128, 128], BF16, "ident")
        identf = sb([128, 128], F32, "identf")
        trile = sb([128, 128], BF16, "trile")
        trigt = sb([128, 128], BF16, "trigt")
        alb = sb([128, 512], F32, "alb")
        iota64 = sb([128, 64], F32, "iota64")
        cmisc = sb([128, 8], F32, "cmisc")
        force0 = sb([128, 64], F32, "force0")
        trilT = sb([128, 128], F32, "trilT")
        for tb, src in [(ident, c_ident), (identf, c_identf), (trile, c_trile), (trigt, c_trigt), (alb, c_alb),
                        (iota64, c_iota), (cmisc, c_misc), (force0, c_force0), (trilT, c_trilT)]:
            load(tb[:], src, [tb.b], "const")

        WIN = sb([128, 8, DIN], BF16, "WIN")
        WOUT = sb([128, 8, D], BF16, "WOUT")
        GP = sb([128, D], F32, "GP")
        NPRE = sb([128, 8], F32, "NPRE")
        PETOK = sb([128, 256], F32, "PETOK")
        W1BD = sb([128, 2, 128], BF16, "W1BD")
        W2KV = sb([128, 4, 64], BF16, "W2KV")
        WPOOL = sb([64, 4, 64], BF16, "WPOOL")
        PSC = sb([128, 256], F32, "PSC")
        GNR = sb([128, 256], F32, "GNR")
        WT = sb([128, 4, 128], BF16, "WT")
        BST = sb([128, 4], F32, "BST")
        WS00 = sb([128, 4], F32, "WS00")
        BS0 = sb([128, 4], F32, "BS0")
        WSTG0 = sb([128, 1032], F32, "WSTG0")
        WSTG = [WSTG0, WSTG0]
        SMALLF = WSTG0

        def load_weights(l):
            load(NPRE[:], npre_t[l], [NPRE.b], "wsmall")
            load(GP[:], gpost_rep[l], [GP.b], "wsmall")
            load(PETOK[:], pe_tok[l], [PETOK.b], "wsmall")
            load(PSC[:], pscale_rep[l], [PSC.b], "wsmall")
            load(GNR[:], gnorm_rep[l], [GNR.b], "wsmall")
            load(BST[:], bsT[l], [BST.b], "wsmall")
            load(WS00[:], ws00_rep[l], [WS00.b], "wsmall")
            load(BS0[:], bs0_rep[l], [BS0.b], "wsmall")
            load(SMALLF[:, 0:256], w1bd[l].rearrange("p r e -> p (r e)"), [SMALLF.b], "wsmall2")
            S.op("dve", lambda e: e.tensor_copy(out=W1BD[:].rearrange("p r e -> p (r e)"), in_=SMALLF[:, 0:256]),
                 reads=[SMALLF.b], writes=[W1BD.b])
            load(SMALLF[:, 0:256], w2kv[l].rearrange("p r e -> p (r e)"), [SMALLF.b], "wsmall2")
            S.op("dve", lambda e: e.tensor_scalar(out=W2KV[:].rearrange("p r e -> p (r e)"), in0=SMALLF[:, 0:256],
                                                  scalar1=1.0 / 128.0, scalar2=None, op0=ALU.mult),
                 reads=[SMALLF.b], writes=[W2KV.b])
            load(SMALLF[0:64, 0:256], wpool[l].rearrange("p r e -> p (r e)"), [SMALLF.b], "wsmall2")
            S.op("dve", lambda e: e.tensor_copy(out=WPOOL[:].rearrange("p r e -> p (r e)"), in_=SMALLF[0:64, 0:256]),
                 reads=[SMALLF.b], writes=[WPOOL.b])
            load(SMALLF[:, 0:512], wsT[l].rearrange("p g i -> p (g i)"), [SMALLF.b], "wsmall2")
            S.op("dve", lambda e: e.tensor_tensor(out=WT[:], in0=SMALLF[:, 0:512].rearrange("p (g i) -> p g i", g=4),
                                                  in1=trilT[:].unsqueeze(1).to_broadcast([128, 4, 128]), op=ALU.mult),
                 reads=[SMALLF.b, trilT.b], writes=[WT.b])
            k = 0
            for c in range(8):
                for (c0, c1) in [(0, 1032), (1032, 2064), (2064, 3096)]:
                    stg = WSTG[k % 2]
                    k += 1
                    load(stg[:, 0:c1 - c0], w_in[l, c * 128:(c + 1) * 128, c0:c1], [stg.b], "wstg" + str(k % 2))
                    if c0 == 0:
                        S.op("dve", lambda e, stg=stg, c=c: e.tensor_scalar(
                            out=WIN[:, c, 0:512], in0=stg[:, 0:512], scalar1=NPRE[:, c:c + 1], scalar2=0.125,
                            op0=ALU.mult, op1=ALU.mult), reads=[stg.b, NPRE.b], writes=[WIN.b])
                        S.op("dve", lambda e, stg=stg, c=c: e.tensor_scalar(
                            out=WIN[:, c, 512:1032], in0=stg[:, 512:1032], scalar1=NPRE[:, c:c + 1], scalar2=None,
                            op0=ALU.mult), reads=[stg.b, NPRE.b], writes=[WIN.b])
                    else:
                        S.op("dve", lambda e, stg=stg, c=c, c0=c0, c1=c1: e.tensor_scalar(
                            out=WIN[:, c, c0:c1], in0=stg[:, 0:c1 - c0], scalar1=NPRE[:, c:c + 1], scalar2=None,
                            op0=ALU.mult), reads=[stg.b, NPRE.b], writes=[WIN.b])
                stg = WSTG[k % 2]
                k += 1
                load(stg[:, 0:1024], w_out[l, c * 128:(c + 1) * 128, :], [stg.b], "wstg" + str(k % 2))
                S.op("act", lambda e, stg=stg, c=c: e.mul(out=WOUT[:, c, :], in_=stg[:, 0:1024], mul=0.5),
                     reads=[stg.b], writes=[WOUT.b])

        X = [sb([128, D], F32, "X0"), None]
        XS = sb([128, D], BF16, "XS")
        XT = sb([128, 8, 128], BF16, "XT")
        SMALL = sb([128, 64], F32, "SMALL")
        QB = sb([128, 512], BF16, "QB")
        KVF = sb([128, 768], F32, "KVF")
        GL = sb([128, 24], F32, "GL")
        GTL = [sb([128, 24], F32, "GT0"), None]
        SZL = [sb([128, D], F32, "SZ0"), None]
        PIN = sb([128, 256], F32, "PIN")
        UV = sb([128, 512], F32, "UV")
        MIXL = [sb([128, D], BF16, "MIX0"), None]
        MT = sb([128, 8, 128], BF16, "MT")
        Y = sb([128, D], F32, "Y")
        YAL = [sb([128, 512], F32, "YA0"), None]
        VN = sb([128, 256], F32, "VN")
        VNB = sb([128, 256], BF16, "VNB")
        YC = sb([128, 256], F32, "YC")
        TMPB = sb([128, 256], F32, "TMPB")
        SQ = TMPB
        JUNK = sb([128, D], BF16, "JUNK")

        def run(gen):
            n = 0
            for _ in gen:
                n += 1
            return n

        def run2(a, na, b, nb):
            live = [a, b]
            cnt = {id(a): 0, id(b): 0}
            while live:
                for g_ in list(live):
                    try:
                        next(g_)
                        cnt[id(g_)] += 1
                    except StopIteration:
                        live.remove(g_)
            return cnt[id(a)], cnt[id(b)]

        def rstd_from_ss(ss_ap, out_ap, n, M, bufs_r, bufs_w, scratch):
            S.op("dve", lambda e: e.tensor_scalar(out=scratch, in0=ss_ap, scalar1=1.0 / n, scalar2=EPS,
                                                  op0=ALU.mult, op1=ALU.add), reads=bufs_r, writes=bufs_w)
            w = scratch.shape[1]
            S.op("pool", lambda e: e.tensor_tensor(out=out_ap, in0=scratch, in1=cmisc[0:M, 2:3].to_broadcast([M, w]), op=ALU.pow),
                 reads=list(bufs_w) + [cmisc.b], writes=bufs_w)

        def project(M, xt, par=0):
            GT = GTL[par]
            SZ = SZL[par]
            S.op("act", lambda e: e.activation(out=JUNK[0:M, :], in_=xt[0:M, :], func=AF.Square),
                 reads=[xt.b], writes=[JUNK.b])
            S.op("dve", lambda e: e.tensor_reduce(out=SMALL[0:M, 0:1], in_=JUNK[0:M, :], axis=AX.X, op=ALU.add),
                 reads=[JUNK.b], writes=[SMALL.b])
            rstd_from_ss(SMALL[0:M, 0:1], SMALL[0:M, 2:3], float(D), M, [SMALL.b], [SMALL.b], SMALL[0:M, 1:2])
            S.op("dve", lambda e: e.tensor_scalar(out=XS[0:M, :], in0=xt[0:M, :], scalar1=SMALL[0:M, 2:3],
                                                  scalar2=None, op0=ALU.mult), reads=[xt.b, SMALL.b], writes=[XS.b])
            yield
            for c in range(8):
                S.op("pe", lambda e, c=c: e.transpose(out=TP[:, c * 128:c * 128 + M], in_=XS[0:M, c * 128:(c + 1) * 128],
                                                      identity=ident[0:M, 0:M]),
                     reads=[XS.b, ident.b], writes=[TP.b])
            S.op("act", lambda e: e.copy(out=XT[:, :, 0:M], in_=TP[:].rearrange("p (c m) -> p c m", c=8)[:, :, 0:M]),
                 reads=[TP.b], writes=[XT.b])
            yield
            for gi, (c0, c1) in enumerate(PG):
                if gi > 0:
                    yield
                pj = PJ[gi % 2]
                w = c1 - c0
                for c in range(8):
                    S.op("pe", lambda e, c=c, pj=pj, c0=c0, c1=c1, w=w: e.matmul(
                        pj[0:M, 0:w], lhsT=XT[:, c, 0:M], rhs=WIN[:, c, c0:c1], start=(c == 0), stop=(c == 7)),
                         reads=[XT.b, WIN.b], writes=[pj.b])
                if gi == 0:
                    S.op("act", lambda e, pj=pj: e.copy(out=QB[0:M, :], in_=pj[0:M, 0:512]), reads=[pj.b], writes=[QB.b])
                elif gi == 1:
                    S.op("dve", lambda e, pj=pj: e.tensor_copy(out=KVF[0:M, 0:512], in_=pj[0:M, 0:512]),
                         reads=[pj.b], writes=[KVF.b])
                elif gi == 2:
                    S.op("dve", lambda e, pj=pj: e.tensor_copy(out=KVF[0:M, 512:768], in_=pj[0:M, 0:256]),
                         reads=[pj.b], writes=[KVF.b])
                    S.op("dve", lambda e, pj=pj: e.tensor_copy(out=GL[0:M, :], in_=pj[0:M, 256:280]),
                         reads=[pj.b], writes=[GL.b])
                elif gi == 3:
                    S.op("act", lambda e, pj=pj: e.activation(out=SZ[0:M, 0:512], in_=pj[0:M, 0:512], func=AF.Tanh, scale=0.5),
                         reads=[pj.b], writes=[SZ.b])
                    S.op("dve", lambda e, pj=pj: e.scalar_tensor_tensor(out=SZ[0:M, 0:512], in0=SZ[0:M, 0:512], scalar=1.0,
                                                                        in1=pj[0:M, 0:512], op0=ALU.add, op1=ALU.mult),
                         reads=[pj.b, SZ.b], writes=[SZ.b])
                elif gi == 4:
                    S.op("dve", lambda e, pj=pj: e.tensor_copy(out=PIN[0:M, :], in_=pj[0:M, 0:256]),
                         reads=[pj.b], writes=[PIN.b])
                    S.op("act", lambda e, pj=pj: e.activation(out=SZ[0:M, 512:768], in_=pj[0:M, 256:512], func=AF.Tanh, scale=0.5),
                         reads=[pj.b], writes=[SZ.b])
                    S.op("dve", lambda e, pj=pj: e.scalar_tensor_tensor(out=SZ[0:M, 512:768], in0=SZ[0:M, 512:768], scalar=1.0,
                                                                        in1=pj[0:M, 256:512], op0=ALU.add, op1=ALU.mult),
                         reads=[pj.b, SZ.b], writes=[SZ.b])
                elif gi == 5:
                    S.op("dve", lambda e, pj=pj: e.tensor_copy(out=UV[0:M, :], in_=pj[0:M, 0:512]),
                         reads=[pj.b], writes=[UV.b])
                else:
                    S.op("act", lambda e, pj=pj: e.activation(out=SZ[0:M, 768:1024], in_=pj[0:M, 0:256], func=AF.Tanh, scale=0.5),
                         reads=[pj.b], writes=[SZ.b])
                    S.op("dve", lambda e, pj=pj: e.scalar_tensor_tensor(out=SZ[0:M, 768:1024], in0=SZ[0:M, 768:1024], scalar=1.0,
                                                                        in1=pj[0:M, 0:256], op0=ALU.add, op1=ALU.mult),
                         reads=[pj.b, SZ.b], writes=[SZ.b])
            yield
            S.op("act", lambda e: e.activation(out=GT[0:M, :], in_=GL[0:M, :], func=AF.Exp, scale=-1.0),
                 reads=[GL.b], writes=[GT.b])
            S.op("dve", lambda e: e.tensor_scalar(out=GT[0:M, :], in0=GT[0:M, :], scalar1=1.0, scalar2=None, op0=ALU.add),
                 reads=[GT.b], writes=[GT.b])
            S.op("dve", lambda e: e.reciprocal(out=GT[0:M, :], in_=GT[0:M, :]), reads=[GT.b], writes=[GT.b])

        def gmlp_norm(M):
            v = UV[0:M, 256:512]
            S.op("dve", lambda e: e.tensor_tensor(out=SQ[0:M, :], in0=v, in1=v, op=ALU.mult), reads=[UV.b], writes=[SQ.b])
            S.op("dve", lambda e: e.tensor_reduce(out=SMALL[0:M, 8:12], in_=SQ[0:M, :].rearrange("p (g c) -> p g c", g=4),
                                                  axis=AX.X, op=ALU.add), reads=[SQ.b], writes=[SMALL.b])
            rstd_from_ss(SMALL[0:M, 8:12], SMALL[0:M, 16:20], 64.0, M, [SMALL.b], [SMALL.b], SMALL[0:M, 12:16])
            S.op("dve", lambda e: e.tensor_tensor(out=VN[0:M, :].rearrange("p (g c) -> p g c", g=4),
                                                  in0=v.rearrange("p (g c) -> p g c", g=4),
                                                  in1=SMALL[0:M, 16:20].unsqueeze(2).to_broadcast([M, 4, 64]), op=ALU.mult),
                 reads=[UV.b, SMALL.b], writes=[VN.b])
            S.op("dve", lambda e: e.tensor_tensor(out=VN[0:M, :], in0=VN[0:M, :], in1=GNR[0:M, :], op=ALU.mult),
                 reads=[VN.b, GNR.b], writes=[VN.b])

        def merge_out(M, xt, dst_ap, dst_bufs, stream, par=0):
            MIX = MIXL[par]
            for c in range(8):
                S.op("pe", lambda e, c=c: e.transpose(out=TP[:, c * 128:c * 128 + M], in_=MIX[0:M, c * 128:(c + 1) * 128],
                                                      identity=ident[0:M, 0:M]),
                     reads=[MIX.b, ident.b], writes=[TP.b])
            S.op("act", lambda e: e.copy(out=MT[:, :, 0:M], in_=TP[:].rearrange("p (c m) -> p c m", c=8)[:, :, 0:M]),
                 reads=[TP.b], writes=[MT.b])
            yield
            for hf in range(2):
                for c in range(8):
                    S.op("pe", lambda e, c=c, hf=hf: e.matmul(PJ[hf][0:M, :], lhsT=MT[:, c, 0:M],
                                                               rhs=WOUT[:, c, hf * 512:(hf + 1) * 512],
                                                               start=(c == 0), stop=(c == 7)),
                         reads=[MT.b, WOUT.b], writes=[PJ[hf].b])
            for hf in range(2):
                S.op("act", lambda e, hf=hf: e.activation(out=JUNK[0:M, hf * 512:(hf + 1) * 512], in_=PJ[hf][0:M, :],
                                                          func=AF.Square), reads=[PJ[hf].b], writes=[JUNK.b])
            S.op("dve", lambda e: e.tensor_reduce(out=SMALL[0:M, 24:25], in_=JUNK[0:M, :], axis=AX.X, op=ALU.add),
                 reads=[JUNK.b], writes=[SMALL.b])
            rstd_from_ss(SMALL[0:M, 24:25], SMALL[0:M, 26:27], float(D), M, [SMALL.b], [SMALL.b], SMALL[0:M, 25:26])
            for hf in range(2):
                sl = slice(hf * 512, (hf + 1) * 512)
                S.op("dve", lambda e, hf=hf, sl=sl: e.scalar_tensor_tensor(
                    out=Y[0:M, sl], in0=PJ[hf][0:M, :], scalar=SMALL[0:M, 26:27], in1=GP[0:M, sl],
                    op0=ALU.mult, op1=ALU.mult), reads=[PJ[hf].b, SMALL.b, GP.b], writes=[Y.b])
            S.op("pool", lambda e: e.tensor_tensor(out=Y[0:M, :], in0=Y[0:M, :], in1=xt[0:M, :], op=ALU.add),
                 reads=[Y.b, xt.b], writes=[Y.b])
            store(dst_ap, Y[0:M, :], [Y.b], stream, writes=dst_bufs)
            yield

        XP1b = [Buf(f"xp1_{i}") for i in range(NT)]

        def prompt_layer(l, pstack):
            X[1] = sb([128, D], F32, "X1", pstack)
            GTL[1] = sb([128, 24], F32, "GT1", pstack)
            SZL[1] = sb([128, D], F32, "SZ1", pstack)
            MIXL[1] = sb([128, D], BF16, "MIX1", pstack)
            YAL[1] = sb([128, 512], F32, "YA1", pstack)
            KT = sb([100, 4, T], BF16, "KT", pstack)
            KTb = [Buf(f"KT{i}") for i in range(NT)]
            KTaug = Buf("KTaug")
            VS = sb([128, NT, 2, 65], BF16, "VS", pstack)
            VW = sb([128, 8, 2, 65], BF16, "VW", pstack)
            VSb = [Buf(f"VS{i}") for i in range(NT)]
            VWb = [Buf(f"VW{i}") for i in range(8)]
            NMP = sb([128, 2, 2, 96], BF16, "NMP", pstack)
            POOLB = sb([128, 3, 4, 128], BF16, "POOLB", pstack)
            HBT = sb([128, 2, 64], BF16, "HBT", pstack)
            KCc = sb([64, 2, 64], BF16, "KCc", pstack)
            VCc = sb([64, 2, 64], BF16, "VCc", pstack)
            QA = [sb([100, 2, 2, 512], BF16, "QA0", pstack), sb([100, 2, 2, 512], BF16, "QA1", pstack)]
            QAaug = [Buf("QAaug0"), Buf("QAaug1")]
            KBF = sb([128, 2, 128], BF16, "KBF", pstack)
            KCP = sb([128, 256], BF16, "KCP", pstack)
            KCT = sb([128, 2, 128], BF16, "KCT", pstack)
            HID = sb([128, 2, 128], F32, "HID", pstack)
            SC = sb([128, 8, 64], F32, "SC", pstack)
            PC = SC
            PCB = sb([128, 8, 64], BF16, "PCB", pstack)
            PCT = sb([64, 8, 128], BF16, "PCT", pstack)
            IMP = sb([128, 2, 64], F32, "IMP", pstack)
            SCR = sb([128, 2, 64], F32, "SCR", pstack)
            WK1 = sb([128, 64], F32, "WK1", pstack)
            WK2 = sb([128, 64], F32, "WK2", pstack)
            M8 = sb([128, 8], F32, "M8", pstack)
            SEL = sb([128, 2, 64], BF16, "SEL", pstack)
            SELTL = [sb([64, 2, 128], BF16, "SELT0", pstack), sb([64, 2, 128], BF16, "SELT1", pstack)]
            VIS = sb([128, 64], F32, "VIS", pstack)
            NVIS = sb([128, 64], F32, "NVIS", pstack)
            STT = sb([128, 64], F32, "STT", pstack)
            ADDT = sb([128, 64], F32, "ADDT", pstack)
            PT = [sb([128, 512], BF16, f"PT{j}", pstack) for j in range(3)]
            MKD = sb([128, 128], BF16, "MKD", pstack)
            OSL = [sb([128, 4, 65], F32, "OS0", pstack), sb([128, 4, 65], F32, "OS1", pstack)]
            CSL = [sb([128, 8], F32, "CS0", pstack), sb([128, 8], F32, "CS1", pstack)]
            oac = [0]
            PINB = [sb([128, 256], BF16, "PINB0", pstack), sb([128, 256], BF16, "PINB1", pstack)]
            DT = sb([64, 4, 128], BF16, "DT", pstack)

            load(POOLB[:], c_poolb, [POOLB.b], "const")
            for j in range(4):
                load(KT[96:100, j, :], c_kaug, [KTaug], "const")
            load(KT[64:96, 0, :], c_kmask, [KTaug], "const")
            load(KT[64:96, 1, :], c_kmask, [KTaug], "const")
            S.op("dve", lambda e: e.memset(KT[64:96, 2:4, :], 0.0), writes=[KTaug])
            S.op("dve", lambda e: e.memset(NMP[:], 0.0), writes=[NMP.b])
            for qq in QA:
                S.op("dve", lambda e, qq=qq: e.memset(qq[64:96, :, :, :], 0.0), writes=[qq.b])
            S.op("dve", lambda e: e.memset(VS[:], 1.0), writes=VSb)
            S.op("dve", lambda e: e.memset(VW[:], 1.0), writes=VWb)
            S.op("dve", lambda e: e.memset(HBT[:], 0.0), writes=[HBT.b])
            ptc = [0]
            mkc = [0]
            stc = [0]

            def attend(i, kv, branch, qa, qab):
                par = i % 2
                GT, YA, SELT = GTL[par], YAL[par], SELTL[par]
                og = oac[0] % 2
                oac[0] += 1
                OS, CS = OSL[og], CSL[og]
                oa_ap = OA[:, :] if og == 0 else MK[:].rearrange("p a b -> p (a b)")
                oa_b = OA.b if og == 0 else MK.b
                if branch == 0:
                    kts = list(range(0, i + 1))
                    slot, Vt, Vb = kv, VS, VSb
                    vsl = lambda kt: kt
                else:
                    kts = list(range(max(0, i - 4), i + 1))
                    slot, Vt, Vb = 2 + kv, VW, VWb
                    vsl = lambda kt: kt % 8
                n = len(kts)
                pts = [None] * n

                def s1a(idx):
                    kt = kts[idx]
                    st = ST[stc[0] % 2]
                    stc[0] += 1
                    hf = (kt // 16) if branch == 0 else 0
                    S.op("pe", lambda e, st=st, kt=kt, hf=hf: e.matmul(st[:, :], lhsT=KT[0:100, slot, kt * 128:(kt + 1) * 128],
                                                                        rhs=qa[0:100, kv, hf, :], start=True, stop=True),
                         reads=[KTb[kt], KTaug, qa.b, qab], writes=[st.b])
                    pt = PT[ptc[0] % len(PT)]
                    ptc[0] += 1
                    pts[idx] = pt
                    S.op("act", lambda e, st=st, pt=pt: e.activation(out=pt[:], in_=st[:, :], func=AF.Exp),
                         reads=[st.b], writes=[pt.b])

                def s1b(idx):
                    kt = kts[idx]
                    pt = pts[idx]
                    mask_ap = None
                    mreads = []
                    if branch == 0:
                        if kt == i:
                            mask_ap = trile[:]
                            mreads = [trile.b]
                    else:
                        if kt == i:
                            mask_ap = trile[:]
                            mreads = [trile.b]
                        elif kt == i - 4:
                            mask_ap = trigt[:]
                            mreads = [trigt.b]
                    if mask_ap is not None:
                        S.op("dve", lambda e, pt=pt, mask_ap=mask_ap: e.tensor_tensor(
                            out=pt[:].rearrange("p (g q) -> p g q", g=4), in0=pt[:].rearrange("p (g q) -> p g q", g=4),
                            in1=mask_ap.unsqueeze(1).to_broadcast([128, 4, 128]), op=ALU.mult),
                             reads=[pt.b] + mreads, writes=[pt.b])

                def s2(idx):
                    kt = kts[idx]
                    pt = pts[idx]
                    for g in range(4):
                        S.op("pe", lambda e, pt=pt, g=g, kt=kt, idx=idx: e.matmul(
                            oa_ap[:, g * 65:(g + 1) * 65], lhsT=pt[:, g * 128:(g + 1) * 128], rhs=Vt[:, vsl(kt), kv, :],
                            start=(idx == 0 and g == 0), stop=(idx == n - 1 and g == 3)),
                             reads=[pt.b, Vb[vsl(kt)]], writes=[oa_b])

                s1a(0)
                if n > 1:
                    s1a(1)
                s1b(0)
                for idx in range(n):
                    if idx + 2 < n:
                        s1a(idx + 2)
                    if idx + 1 < n:
                        s1b(idx + 1)
                    s2(idx)
                    yield
                S.op("act", lambda e: e.copy(out=OS[:].rearrange("p g d -> p (g d)"), in_=oa_ap[:, 0:260]),
                     reads=[oa_b], writes=[OS.b])
                gcol = 1 + branch
                S.op("dve", lambda e: e.reciprocal(out=CS[:, 0:4], in_=OS[:, :, 64]), reads=[OS.b], writes=[CS.b])
                S.op("dve", lambda e: e.tensor_tensor(
                    out=CS[:, 4:8], in0=CS[:, 0:4],
                    in1=GT[:, :].rearrange("p (h t) -> p h t", t=3)[:, kv * 4:(kv + 1) * 4, gcol], op=ALU.mult),
                     reads=[CS.b, GT.b], writes=[CS.b])
                for g in range(4):
                    h = kv * 4 + g
                    S.op("dve", lambda e, g=g, h=h: e.scalar_tensor_tensor(
                        out=YA[:, h * 64:(h + 1) * 64], in0=OS[:, g, 0:64], scalar=CS[:, 4 + g:5 + g],
                        in1=YA[:, h * 64:(h + 1) * 64], op0=ALU.mult, op1=ALU.add),
                         reads=[OS.b, CS.b, YA.b], writes=[YA.b])

            def front(i):
                par = i % 2
                GT, SZ, MIX, YA, SELT = GTL[par], SZL[par], MIXL[par], YAL[par], SELTL[par]
                xt = X[i % 2]
                src = xp if l == 0 else xp1
                load(xt[:], src[i * 128:(i + 1) * 128, :], [xt.b], f"x{i % 2}", reads=([XP1b[i]] if l == 1 else []))
                yield from project(128, xt, par)
                yield
                store(kvc_p[l, i * 128:(i + 1) * 128, :], KVF[:, 0:256], [KVF.b], "o_kvf")
                store(kvs_p[l, i * 128:(i + 1) * 128, :], KVF[:, 256:512], [KVF.b], "o_kvf")
                if i >= NT - 4:
                    j = i - (NT - 4)
                    store(kvw_p[l, j * 128:(j + 1) * 128, :], KVF[:, 512:768], [KVF.b], "o_kvf")
                if i == NT - 1:
                    store(pool_p[l], PIN[113:128, :], [PIN.b], "o_pin")
                yield
                S.op("pool", lambda e: e.tensor_copy(out=KBF[:, 0, :], in_=KVF[:, 256:384]), reads=[KVF.b], writes=[KBF.b])
                S.op("pool", lambda e: e.tensor_copy(out=KBF[:, 1, :], in_=KVF[:, 512:640]), reads=[KVF.b], writes=[KBF.b])
                S.op("pool", lambda e, i=i: e.tensor_copy(out=VS[:, i, :, 0:64],
                                                           in_=KVF[:, 384:512].rearrange("p (k d) -> p k d", k=2)),
                     reads=[KVF.b], writes=[VSb[i]])
                S.op("pool", lambda e, i=i: e.tensor_copy(out=VW[:, i % 8, :, 0:64],
                                                           in_=KVF[:, 640:768].rearrange("p (k d) -> p k d", k=2)),
                     reads=[KVF.b], writes=[VWb[i % 8]])
                for j in range(4):
                    S.op("pe", lambda e, j=j: e.transpose(out=TP[0:64, j * 128:(j + 1) * 128],
                                                          in_=KBF[:, j // 2, (j % 2) * 64:(j % 2) * 64 + 64], identity=ident[:]),
                         reads=[KBF.b, ident.b], writes=[TP.b])
                S.op("act", lambda e, i=i: e.copy(out=KT[0:64, :, i * 128:(i + 1) * 128],
                                                  in_=TP[0:64, 0:512].rearrange("p (j t) -> p j t", j=4)),
                     reads=[TP.b], writes=[KTb[i]])
                yield
                qa = QA[i % 2]
                qab = QAaug[i % 2]
                load(qa[96:100, :, :, :].rearrange("p k h n -> p (k h n)"), c_qaug[i], [qab], f"qaug{i % 2}", reads=[qa.b])
                for h in range(8):
                    S.op("pe", lambda e, h=h: e.transpose(out=TP[0:64, h * 128:(h + 1) * 128], in_=QB[:, h * 64:(h + 1) * 64],
                                                          identity=ident[:]), reads=[QB.b, ident.b], writes=[TP.b])
                S.op("act", lambda e, qa=qa: e.copy(
                    out=qa[0:64, :, :, :], in_=TP[0:64, :].rearrange("p (k n) -> p k n", k=2).unsqueeze(2).to_broadcast([64, 2, 2, 512])),
                     reads=[TP.b, qab], writes=[qa.b])
                yield
                S.op("dve", lambda e: e.tensor_tensor(out=KCP[:], in0=KVF[:, 0:256], in1=PETOK[:], op=ALU.add),
                     reads=[KVF.b, PETOK.b], writes=[KCP.b])
                for r in range(2):
                    S.op("pe", lambda e, r=r: e.transpose(out=TP[:, r * 128:(r + 1) * 128], in_=KCP[:, r * 128:(r + 1) * 128],
                                                          identity=ident[:]), reads=[KCP.b, ident.b], writes=[TP.b])
                S.op("act", lambda e: e.copy(out=KCT[:].rearrange("p r t -> p (r t)"), in_=TP[:, 0:256]),
                     reads=[TP.b], writes=[KCT.b])
                for r in range(2):
                    S.op("pe", lambda e, r=r: e.matmul(MS[:, r * 128:(r + 1) * 128], lhsT=W1BD[:, r, :], rhs=KCT[:, r, :],
                                                       start=True, stop=True), reads=[W1BD.b, KCT.b], writes=[MS.b])
                S.op("act", lambda e: e.activation(out=HID[:].rearrange("p r t -> p (r t)"), in_=MS[:, 0:256], func=AF.Tanh, scale=0.5),
                     reads=[MS.b], writes=[HID.b])
                S.op("dve", lambda e: e.scalar_tensor_tensor(out=HID[:].rearrange("p r t -> p (r t)"),
                                                             in0=HID[:].rearrange("p r t -> p (r t)"), scalar=1.0, in1=MS[:, 0:256],
                                                             op0=ALU.add, op1=ALU.mult), reads=[MS.b, HID.b], writes=[HID.b])
                S.op("dve", lambda e: e.tensor_reduce(out=SMALL[:, 34:38].rearrange("p (r b) -> p r b", r=2),
                                                      in_=HID[:].rearrange("p r (b c) -> p r b c", b=2),
                                                      axis=AX.X, op=ALU.add), reads=[HID.b], writes=[SMALL.b])
                S.op("dve", lambda e, i=i: e.tensor_copy(out=HBT[:, :, 2 * i:2 * i + 2],
                                                         in_=SMALL[:, 34:38].rearrange("p (r b) -> p r b", r=2)),
                     reads=[SMALL.b], writes=[HBT.b])
                for kv in range(2):
                    S.op("pe", lambda e, kv=kv: e.matmul(MS[0:64, kv * 64:(kv + 1) * 64], lhsT=W2KV[:, kv, :], rhs=HBT[:, 0, :],
                                                         start=True, stop=True), reads=[W2KV.b, HBT.b], writes=[MS.b])
                    S.op("pe", lambda e, kv=kv: e.matmul(MS[0:64, 128 + kv * 64:128 + (kv + 1) * 64], lhsT=HBT[:, 1, :],
                                                         rhs=W2KV[:, 2 + kv, :], start=True, stop=True),
                         reads=[W2KV.b, HBT.b], writes=[MS.b])
                S.op("act", lambda e: e.copy(out=KCc[:].rearrange("p k n -> p (k n)"), in_=MS[0:64, 0:128]),
                     reads=[MS.b], writes=[KCc.b])
                S.op("act", lambda e: e.copy(out=VCc[:].rearrange("p k n -> p (k n)"), in_=MS[0:64, 128:256]),
                     reads=[MS.b], writes=[VCc.b])
                yield
                for h in range(8):
                    kv, g = h // 4, h % 4
                    S.op("pe", lambda e, h=h, kv=kv, g=g, qa=qa: e.matmul(MS[:, h * 64:(h + 1) * 64],
                                                                   lhsT=qa[0:64, kv, 0, g * 128:(g + 1) * 128],
                                                                   rhs=KCc[:, kv, :], start=True, stop=True),
                         reads=[qa.b, KCc.b], writes=[MS.b])
                S.op("dve", lambda e, i=i: e.tensor_scalar(out=SMALL[:, 32:33], in0=cmisc[:, 0:1], scalar1=float(2 * i),
                                                           scalar2=None, op0=ALU.add), reads=[cmisc.b], writes=[SMALL.b])
                S.op("dve", lambda e: e.tensor_scalar(out=VIS[:], in0=iota64[:], scalar1=SMALL[:, 32:33], scalar2=None,
                                                      op0=ALU.is_le), reads=[iota64.b, SMALL.b], writes=[VIS.b])
                S.op("dve", lambda e: e.tensor_scalar(out=NVIS[:], in0=VIS[:], scalar1=-1.0, scalar2=-NEG,
                                                      op0=ALU.add, op1=ALU.mult), reads=[VIS.b], writes=[NVIS.b])
                S.op("dve", lambda e: e.tensor_tensor(out=SC[:].rearrange("p h n -> p (h n)"), in0=MS[:, :], in1=alb[:], op=ALU.add),
                     reads=[MS.b, alb.b], writes=[SC.b])
                yield
                S.op("dve", lambda e: e.tensor_tensor(out=SC[:], in0=SC[:], in1=NVIS[:].unsqueeze(1).to_broadcast([128, 8, 64]),
                                                      op=ALU.add), reads=[SC.b, NVIS.b], writes=[SC.b])
                yield
                S.op("dve", lambda e: e.tensor_reduce(out=SMALL[:, 40:48], in_=SC[:], axis=AX.X, op=ALU.max),
                     reads=[SC.b], writes=[SMALL.b])
                yield
                S.op("dve", lambda e: e.tensor_tensor(out=SC[:], in0=SC[:],
                                                      in1=SMALL[:, 40:48].unsqueeze(2).to_broadcast([128, 8, 64]),
                                                      op=ALU.subtract), reads=[SC.b, SMALL.b], writes=[SC.b])
                yield
                S.op("act", lambda e: e.activation(out=PC[:].rearrange("p h n -> p (h n)"),
                                                   in_=SC[:].rearrange("p h n -> p (h n)"), func=AF.Exp),
                     reads=[SC.b], writes=[PC.b])
                yield
                S.op("dve", lambda e: e.tensor_tensor(out=PC[:], in0=PC[:], in1=VIS[:].unsqueeze(1).to_broadcast([128, 8, 64]),
                                                      op=ALU.mult), reads=[PC.b, VIS.b], writes=[PC.b])
                yield
                S.op("dve", lambda e: e.tensor_reduce(out=SMALL[:, 48:56], in_=PC[:], axis=AX.X, op=ALU.add),
                     reads=[PC.b], writes=[SMALL.b])
                yield
                S.op("dve", lambda e: e.tensor_scalar(out=SMALL[:, 48:56], in0=SMALL[:, 48:56], scalar1=1e-30, scalar2=None,
                                                      op0=ALU.max), reads=[SMALL.b], writes=[SMALL.b])
                yield
                S.op("dve", lambda e: e.reciprocal(out=SMALL[:, 56:64], in_=SMALL[:, 48:56]), reads=[SMALL.b], writes=[SMALL.b])
                yield
                S.op("dve", lambda e: e.tensor_tensor(out=PC[:], in0=PC[:],
                                                      in1=SMALL[:, 56:64].unsqueeze(2).to_broadcast([128, 8, 64]), op=ALU.mult),
                     reads=[PC.b, SMALL.b], writes=[PC.b])
                yield
                S.op("act", lambda e: e.copy(out=PCB[:].rearrange("p h n -> p (h n)"), in_=PC[:].rearrange("p h n -> p (h n)")),
                     reads=[PC.b], writes=[PCB.b])
                yield
                S.op("dve", lambda e: e.tensor_reduce(out=IMP[:], in_=PC[:].rearrange("p (k g) n -> p k n g", k=2),
                                                      axis=AX.X, op=ALU.add), reads=[PC.b], writes=[IMP.b])
                yield
                S.op("dve", lambda e, i=i: e.tensor_scalar(out=SMALL[:, 33:34], in0=cmisc[:, 1:2], scalar1=float(2 * i),
                                                           scalar2=None, op0=ALU.add), reads=[cmisc.b], writes=[SMALL.b])
                yield
                S.op("dve", lambda e: e.tensor_scalar(out=STT[:], in0=iota64[:], scalar1=SMALL[:, 33:34], scalar2=None,
                                                      op0=ALU.is_le), reads=[iota64.b, SMALL.b], writes=[STT.b])
                yield
                S.op("dve", lambda e: e.tensor_scalar(out=ADDT[:], in0=iota64[:], scalar1=SMALL[:, 33:34], scalar2=None,
                                                      op0=ALU.is_equal), reads=[iota64.b, SMALL.b], writes=[ADDT.b])
                yield
                S.op("dve", lambda e: e.tensor_tensor(out=ADDT[:], in0=ADDT[:], in1=STT[:], op=ALU.add),
                     reads=[ADDT.b, STT.b], writes=[ADDT.b])
                yield
                S.op("dve", lambda e: e.tensor_scalar(out=ADDT[:], in0=ADDT[:], scalar1=1e4, scalar2=-1e4,
                                                      op0=ALU.mult, op1=ALU.add), reads=[ADDT.b], writes=[ADDT.b])
                yield
                S.op("dve", lambda e: e.tensor_tensor(out=ADDT[:], in0=ADDT[:], in1=force0[:], op=ALU.add),
                     reads=[ADDT.b, force0.b], writes=[ADDT.b])
                yield
                S.op("dve", lambda e: e.tensor_tensor(out=SCR[:], in0=IMP[:], in1=STT[:].unsqueeze(1).to_broadcast([128, 2, 64]),
                                                      op=ALU.mult), reads=[IMP.b, STT.b], writes=[SCR.b])
                yield
                S.op("dve", lambda e: e.tensor_tensor(out=SCR[:], in0=SCR[:], in1=ADDT[:].unsqueeze(1).to_broadcast([128, 2, 64]),
                                                      op=ALU.add), reads=[SCR.b, ADDT.b], writes=[SCR.b])
                yield
                for kv in range(2):
                    S.op("dve", lambda e, kv=kv: e.max(out=M8[:], in_=SCR[:, kv, :]), reads=[SCR.b], writes=[M8.b])
                    S.op("dve", lambda e, kv=kv: e.match_replace(out=WK1[:], in_to_replace=M8[:], in_values=SCR[:, kv, :],
                                                                 imm_value=NEG), reads=[SCR.b, M8.b], writes=[WK1.b])
                    S.op("dve", lambda e: e.max(out=M8[:], in_=WK1[:]), reads=[WK1.b], writes=[M8.b])
                    S.op("dve", lambda e: e.match_replace(out=WK2[:], in_to_replace=M8[:], in_values=WK1[:], imm_value=NEG),
                         reads=[WK1.b, M8.b], writes=[WK2.b])
                    S.op("dve", lambda e, kv=kv: e.tensor_tensor(out=SEL[:, kv, :], in0=WK2[:], in1=SCR[:, kv, :],
                                                                 op=ALU.not_equal), reads=[WK2.b, SCR.b], writes=[SEL.b])
                yield
                S.op("dve", lambda e: e.tensor_scalar(out=NMP[:, :, :, 64:96], in0=SEL[:].rearrange("p k (h n) -> p k h n", h=2),
                                                      scalar1=-1.0, scalar2=-NEG, op0=ALU.add, op1=ALU.mult),
                     reads=[SEL.b], writes=[NMP.b])
                nhalf = 2 if i >= 16 else 1
                for kv in range(2):
                    for hf in range(nhalf):
                        S.op("pe", lambda e, kv=kv, hf=hf: e.matmul(MS[0:96, (kv * 2 + hf) * 128:(kv * 2 + hf + 1) * 128],
                                                                     lhsT=NMP[:, kv, hf, :], rhs=ident[:], start=True, stop=True),
                             reads=[NMP.b, ident.b], writes=[MS.b])
                for kv in range(2):
                    for hf in range(nhalf):
                        S.op("act", lambda e, kv=kv, hf=hf, qa=qa: e.copy(
                            out=qa[64:96, kv, hf, :].rearrange("p (g q) -> p g q", g=4),
                            in_=MS[64:96, (kv * 2 + hf) * 128:(kv * 2 + hf + 1) * 128].unsqueeze(1).to_broadcast([32, 4, 128])),
                             reads=[MS.b], writes=[qa.b])
                for h in range(8):
                    S.op("pe", lambda e, h=h: e.transpose(out=TP[0:64, h * 128:(h + 1) * 128], in_=PCB[:, h, :],
                                                          identity=ident[:]), reads=[PCB.b, ident.b], writes=[TP.b])
                S.op("act", lambda e: e.copy(out=PCT[:].rearrange("p h q -> p (h q)"), in_=TP[0:64, :]),
                     reads=[TP.b], writes=[PCT.b])
                for h in range(8):
                    S.op("pe", lambda e, h=h: e.matmul(MS[:, h * 64:(h + 1) * 64], lhsT=PCT[:, h, :], rhs=VCc[:, h // 4, :],
                                                       start=True, stop=True), reads=[PCT.b, VCc.b], writes=[MS.b])
                S.op("dve", lambda e: e.tensor_tensor(
                    out=YA[:].rearrange("p (h d) -> p h d", h=8), in0=MS[:, :].rearrange("p (h d) -> p h d", h=8),
                    in1=GT[:, :].rearrange("p (h t) -> p h t", t=3)[:, :, 0:1].to_broadcast([128, 8, 64]), op=ALU.mult),
                     reads=[MS.b, GT.b], writes=[YA.b])
                pb = PINB[i % 2]
                pbp = PINB[(i + 1) % 2]
                S.op("pool", lambda e, pb=pb: e.tensor_copy(out=pb[:], in_=PIN[:]), reads=[PIN.b], writes=[pb.b])
                for g in range(4):
                    var = 2 if i == 0 else 0
                    S.op("pe", lambda e, g=g, pb=pb, var=var, i=i: e.matmul(MS[0:64, g * 128:(g + 1) * 128],
                                                                             lhsT=pb[:, g * 64:(g + 1) * 64],
                                                                             rhs=POOLB[:, var, g, :], start=True, stop=(i == 0)),
                         reads=[pb.b, POOLB.b], writes=[MS.b])
                    if i > 0:
                        S.op("pe", lambda e, g=g, pbp=pbp: e.matmul(MS[0:64, g * 128:(g + 1) * 128],
                                                                     lhsT=pbp[:, g * 64:(g + 1) * 64], rhs=POOLB[:, 1, g, :],
                                                                     start=False, stop=True),
                             reads=[pbp.b, POOLB.b], writes=[MS.b])
                S.op("act", lambda e: e.copy(out=DT[:].rearrange("p g t -> p (g t)"), in_=MS[0:64, :]), reads=[MS.b], writes=[DT.b])
                for g in range(4):
                    S.op("pe", lambda e, g=g: e.matmul(MS[:, g * 64:(g + 1) * 64], lhsT=DT[:, g, :], rhs=WPOOL[:, g, :],
                                                       start=True, stop=True), reads=[DT.b, WPOOL.b], writes=[MS.b])
                S.op("dve", lambda e: e.tensor_tensor(out=TMPB[:], in0=MS[:, 0:256], in1=PSC[:], op=ALU.mult),
                     reads=[MS.b, PSC.b], writes=[TMPB.b])
                S.op("pool", lambda e: e.tensor_tensor(out=MIX[:, 512:768], in0=TMPB[:], in1=SZ[:, 512:768], op=ALU.mult),
                     reads=[TMPB.b, SZ.b], writes=[MIX.b])
                yield
                gmlp_norm(128)
                yield
                if i == NT - 1:
                    store(gv_p[l], VN[:], [VN.b], "o_vn")
                S.op("act", lambda e: e.copy(out=VNB[:], in_=VN[:]), reads=[VN.b], writes=[VNB.b])
                for g in range(4):
                    S.op("pe", lambda e, g=g: e.matmul(MS[:, g * 64:(g + 1) * 64], lhsT=WT[:, g, :], rhs=VNB[:, g * 64:(g + 1) * 64],
                                                       start=True, stop=True), reads=[WT.b, VNB.b], writes=[MS.b])
                for g in range(4):
                    S.op("dve", lambda e, g=g: e.scalar_tensor_tensor(
                        out=YC[:, g * 64:(g + 1) * 64], in0=MS[:, g * 64:(g + 1) * 64], scalar=BST[:, g:g + 1],
                        in1=UV[:, g * 64:(g + 1) * 64], op0=ALU.add, op1=ALU.mult),
                         reads=[MS.b, BST.b, UV.b], writes=[YC.b])
                S.op("pool", lambda e: e.tensor_tensor(out=MIX[:, 768:1024], in0=YC[:], in1=SZ[:, 768:1024], op=ALU.mult),
                     reads=[YC.b, SZ.b], writes=[MIX.b])

            def back(i):
                par = i % 2
                GT, SZ, MIX, YA, SELT = GTL[par], SZL[par], MIXL[par], YAL[par], SELTL[par]
                xt = X[i % 2]
                qa = QA[i % 2]
                qab = QAaug[i % 2]
                for kv in range(2):
                    yield from attend(i, kv, 0, qa, qab)
                    yield from attend(i, kv, 1, qa, qab)
                S.op("dve", lambda e: e.tensor_tensor(out=MIX[:, 0:512], in0=YA[:], in1=SZ[:, 0:512], op=ALU.mult),
                     reads=[YA.b, SZ.b], writes=[MIX.b])
                if l == 0 and not dbg_y0:
                    yield from merge_out(128, xt, xp1[i * 128:(i + 1) * 128, :], [XP1b[i]], "o_y", par)
                else:
                    yield from merge_out(128, xt, y_p[i * 128:(i + 1) * 128, :], [], "o_y", par)


            ntiles = min(NT, nt_limit)
            prev = None
            nprev = 1
            nfront = 200
            for i in range(ntiles):
                f = front(i)
                if prev is None:
                    nfront = run(f)
                else:
                    _, nfront = run2(prev, nprev, f, nfront)
                prev = back(i)
                nprev = 2 * ((i + 1) + min(i + 1, 5)) + 6
            if prev is not None:
                run(prev)

        XS1b = Buf("xs1")
        BNC = Buf("bounce")

        def sample_layer(l, ss):
            PTI = sb([128, NS * NPG], I32, "PTI", ss)
            IDXF = sb([128, NS * NPG], F32, "IDXF", ss)
            IDX = sb([128, NS * NPG], I32, "IDX", ss)
            PGL = [sb([128, NPG, 256], F32, "PGa", ss), sb([128, NPG, 256], F32, "PGb", ss)]
            PGbL = [[Buf(f"PGa{j}") for j in range(NPG)], [Buf(f"PGb{j}") for j in range(NPG)]]
            PG = PGL[0]
            PGb = PGbL[0]
            KCS = sb([64, NS, 2, 32], BF16, "KCS", ss)
            VCS = sb([32, NS, 2, 64], BF16, "VCS", ss)
            QTS = sb([64, 8, NS], BF16, "QTS", ss)
            KNB = sb([NS, 2, 128], BF16, "KNB", ss)
            VNB2 = sb([NS, 2, 2, 64], BF16, "VNB2", ss)
            KNT = sb([64, 4, NS], BF16, "KNT", ss)
            SCs = sb([4, 32, 32], F32, "SCs", ss)
            SM4 = sb([4, 96], F32, "SM4", ss)
            PCT2 = sb([32, 4, 32], F32, "PCT2", ss)
            PCT2B = sb([32, 4, 32], BF16, "PCT2B", ss)
            PCTg = sb([32, 4, 32], BF16, "PCTg", ss)
            IMPs = sb([32, 32], F32, "IMPs", ss)
            SCRs = sb([32, 40], F32, "SCRs", ss)
            WK1s = sb([32, 40], F32, "WK1s", ss)
            WK2s = sb([32, 40], F32, "WK2s", ss)
            M8s = sb([32, 8], F32, "M8s", ss)
            SELs = sb([32, 33], BF16, "SELs", ss)
            SELTs = sb([33, 32], BF16, "SELTs", ss)
            SFORCE = sb([32, 33], F32, "SFORCE", ss)
            SEXPE = sb([33, 17 * 128], BF16, "SEXPE", ss)
            MTS = sb([128, 17, 32], BF16, "MTS", ss)
            SALB = sb([128, 17, 8], F32, "SALB", ss)
            SWALB = sb([128, 5, 8], F32, "SWALB", ss)
            SWMASK = sb([128, 5], F32, "SWMASK", ss)
            SCALB = sb([4, 2, 32], F32, "SCALB", ss)
            OSS = sb([4, 32, 65], F32, "OSS", ss)
            OT = sb([NS, 3, 8, 65], F32, "OT", ss)
            xt = X[0]

            for tb, src in [(SFORCE, c_sforce), (SEXPE, c_sexpe), (SALB, c_salb), (SWALB, c_swalb),
                            (SWMASK, c_swmask), (SCALB, c_scalb)]:
                load(tb[:], src, [tb.b])
            S.op("sp", lambda e: e.dma_start(out=kvw_s[l, :, 0:511, :], in_=ckw[l, :, 1:512, :]), dma="cp_kvw")
            S.op("sp", lambda e: e.dma_start(out=pool_s[l, :, 0:14, :], in_=spool[l, :, 1:15, :]), dma="cp_pool")
            load(PTI[:], ptab, [PTI.b])
            S.op("dve", lambda e: e.tensor_copy(out=IDXF[:], in_=PTI[:]), reads=[PTI.b], writes=[IDXF.b])
            S.op("dve", lambda e: e.tensor_scalar(out=IDXF[:], in0=IDXF[:], scalar1=128.0, scalar2=cmisc[:, 3:4],
                                                  op0=ALU.mult, op1=ALU.add), reads=[IDXF.b, cmisc.b], writes=[IDXF.b])
            S.op("dve", lambda e: e.tensor_scalar(out=IDXF[:], in0=IDXF[:], scalar1=float(l * NPHYS * 128), scalar2=None,
                                                  op0=ALU.add), reads=[IDXF.b], writes=[IDXF.b])
            S.op("dve", lambda e: e.tensor_copy(out=IDX[:], in_=IDXF[:]), reads=[IDXF.b], writes=[IDX.b])
            S.op("dve", lambda e: e.memset(SCRs[:], 0.0), writes=[SCRs.b])
            S.op("dve", lambda e: e.memset(KCS[:], 0.0), writes=[KCS.b])
            S.op("dve", lambda e: e.memset(VCS[:], 0.0), writes=[VCS.b])
            S.op("dve", lambda e: e.memset(OSS[:], 0.0), writes=[OSS.b])

            src = xs if l == 0 else xs1
            load(xt[0:NS, :], src, [xt.b], reads=([XS1b] if l == 1 else []))
            run(project(NS, xt, 0))
            GT, SZ, MIX, YA = GTL[0], SZL[0], MIXL[0], YAL[0]
            store(kvc_s[l], KVF[0:NS, 0:256], [KVF.b], "o_kvf")
            store(kvs_s[l], KVF[0:NS, 256:512], [KVF.b], "o_kvf")
            store(kvw_s[l, :, 511, :], KVF[0:NS, 512:768], [KVF.b], "o_kvf")
            store(pool_s[l, :, 14, :], PIN[0:NS, :], [PIN.b], "o_pin")

            def gather(cache, s_, sl):
                for j in range(NPG):
                    col = s_ * NPG + j
                    S.op("pool", lambda e, j=j, col=col: e.indirect_dma_start(
                        out=PGL[sl][:, j, :], out_offset=None, in_=cache[:, :],
                        in_offset=bass.IndirectOffsetOnAxis(ap=IDX[:, col:col + 1], axis=0)),
                         reads=[IDX.b], writes=[PGbL[sl][j]], dma=f"L_PG{sl}")

            for h in range(8):
                S.op("pe", lambda e, h=h: e.transpose(out=TP[0:64, h * NS:(h + 1) * NS], in_=QB[0:NS, h * 64:(h + 1) * 64],
                                                      identity=ident[0:NS, 0:NS]), reads=[QB.b, ident.b], writes=[TP.b])
            S.op("act", lambda e: e.copy(out=QTS[:].rearrange("p h s -> p (h s)"), in_=TP[0:64, 0:8 * NS]),
                 reads=[TP.b], writes=[QTS.b])
            S.op("dve", lambda e: e.tensor_copy(out=KNB[:, 0, :], in_=KVF[0:NS, 256:384]), reads=[KVF.b], writes=[KNB.b])
            S.op("dve", lambda e: e.tensor_copy(out=KNB[:, 1, :], in_=KVF[0:NS, 512:640]), reads=[KVF.b], writes=[KNB.b])
            S.op("dve", lambda e: e.tensor_copy(out=VNB2[:, 0, :, :], in_=KVF[0:NS, 384:512].rearrange("p (k d) -> p k d", k=2)),
                 reads=[KVF.b], writes=[VNB2.b])
            S.op("dve", lambda e: e.tensor_copy(out=VNB2[:, 1, :, :], in_=KVF[0:NS, 640:768].rearrange("p (k d) -> p k d", k=2)),
                 reads=[KVF.b], writes=[VNB2.b])
            for j in range(4):
                S.op("pe", lambda e, j=j: e.transpose(out=TP[0:64, j * NS:(j + 1) * NS],
                                                      in_=KNB[:, j // 2, (j % 2) * 64:(j % 2) * 64 + 64],
                                                      identity=ident[0:NS, 0:NS]), reads=[KNB.b, ident.b], writes=[TP.b])
            S.op("act", lambda e: e.copy(out=KNT[:].rearrange("p j s -> p (j s)"), in_=TP[0:64, 0:4 * NS]),
                 reads=[TP.b], writes=[KNT.b])

            sa = ss.enter_context(ExitStack())
            KCPs = sb([128, NPG, 256], BF16, "KCPs", sa)
            KCTs = sb([128, 2, NPG, 128], BF16, "KCTs", sa)
            HIDs = sb([128, 512], F32, "HIDs", sa)
            HBS = sb([128, 2, 32], F32, "HBS", sa)
            HBTs = sb([128, 2, 32], BF16, "HBTs", sa)
            nsq = min(NS, ns_limit)
            gather(ckc, 0, 0)
            for s_ in range(nsq):
                if s_ + 1 < nsq:
                    gather(ckc, s_ + 1, (s_ + 1) % 2)
                else:
                    gather(cks, 0, (s_ + 1) % 2)
                PG = PGL[s_ % 2]
                PGb = PGbL[s_ % 2]
                S.op("dve", lambda e, PG=PG: e.tensor_tensor(out=KCPs[:], in0=PG[:],
                                                      in1=PETOK[:].unsqueeze(1).to_broadcast([128, NPG, 256]), op=ALU.add),
                     reads=PGb + [PETOK.b], writes=[KCPs.b])
                for q4 in range(4):
                    for jl in range(4):
                        for r in range(2):
                            S.op("pe", lambda e, q4=q4, jl=jl, r=r: e.transpose(
                                out=TP[:, (jl * 2 + r) * 128:(jl * 2 + r + 1) * 128],
                                in_=KCPs[:, q4 * 4 + jl, r * 128:(r + 1) * 128], identity=ident[:]),
                                 reads=[KCPs.b, ident.b], writes=[TP.b])
                    S.op("act", lambda e, q4=q4: e.copy(
                        out=KCTs[:, :, q4 * 4:(q4 + 1) * 4, :].rearrange("p r j t -> p j r t"),
                        in_=TP[:].rearrange("p (j r t) -> p j r t", j=4, r=2)), reads=[TP.b], writes=[KCTs.b])
                for r in range(2):
                    for ch in range(4):
                        pj = PJ[(r * 4 + ch) % 2]
                        S.op("pe", lambda e, r=r, ch=ch, pj=pj: e.matmul(
                            pj[:, :], lhsT=W1BD[:, r, :],
                            rhs=KCTs[:, r, ch * 4:(ch + 1) * 4, :].rearrange("p j t -> p (j t)"), start=True, stop=True),
                             reads=[W1BD.b, KCTs.b], writes=[pj.b])
                        S.op("act", lambda e, pj=pj: e.activation(out=HIDs[:], in_=pj[:, :], func=AF.Tanh, scale=0.5),
                             reads=[pj.b], writes=[HIDs.b])
                        S.op("dve", lambda e, pj=pj: e.scalar_tensor_tensor(out=HIDs[:], in0=HIDs[:], scalar=1.0, in1=pj[:, :],
                                                                            op0=ALU.add, op1=ALU.mult), reads=[pj.b, HIDs.b], writes=[HIDs.b])
                        S.op("dve", lambda e, r=r, ch=ch: e.tensor_reduce(
                            out=HBS[:, r, ch * 8:(ch + 1) * 8], in_=HIDs[:].rearrange("p (b c) -> p b c", c=64),
                            axis=AX.X, op=ALU.add), reads=[HIDs.b], writes=[HBS.b])
                S.op("dve", lambda e: e.tensor_copy(out=HBTs[:], in_=HBS[:]), reads=[HBS.b], writes=[HBTs.b])
                for kv in range(2):
                    S.op("pe", lambda e, kv=kv: e.matmul(MS[0:64, kv * 32:(kv + 1) * 32], lhsT=W2KV[:, kv, :], rhs=HBTs[:, 0, :],
                                                         start=True, stop=True), reads=[W2KV.b, HBTs.b], writes=[MS.b])
                    S.op("pe", lambda e, kv=kv: e.matmul(MS[0:32, 64 + kv * 64:64 + (kv + 1) * 64], lhsT=HBTs[:, 1, :],
                                                         rhs=W2KV[:, 2 + kv, :], start=True, stop=True),
                         reads=[W2KV.b, HBTs.b], writes=[MS.b])
                S.op("act", lambda e, s_=s_: e.copy(out=KCS[:, s_, :, :].rearrange("p k n -> p (k n)"), in_=MS[0:64, 0:64]),
                     reads=[MS.b], writes=[KCS.b])
                S.op("act", lambda e, s_=s_: e.copy(out=VCS[:, s_, :, :].rearrange("p k n -> p (k n)"), in_=MS[0:32, 64:192]),
                     reads=[MS.b], writes=[VCS.b])

            sa.close()
            S.fence()
            for s_ in range(NS):
                for kv in range(2):
                    j = s_ * 2 + kv
                    pj = PJ[j // 16]
                    S.op("pe", lambda e, s_=s_, kv=kv, j=j, pj=pj: e.matmul(
                        pj[0:4, (j % 16) * 32:(j % 16 + 1) * 32], lhsT=QTS[:, kv * 4:(kv + 1) * 4, s_],
                        rhs=KCS[:, s_, kv, :], start=True, stop=True), reads=[QTS.b, KCS.b], writes=[pj.b])
            for hf in range(2):
                S.op("dve", lambda e, hf=hf: e.tensor_tensor(
                    out=SCs[:, hf * 16:(hf + 1) * 16, :].rearrange("p (s k) n -> p s (k n)", k=2),
                    in0=PJ[hf][0:4, :].rearrange("p (s kn) -> p s kn", s=8),
                    in1=SCALB[:].rearrange("p k n -> p (k n)").unsqueeze(1).to_broadcast([4, 8, 64]), op=ALU.add),
                     reads=[PJ[hf].b, SCALB.b], writes=[SCs.b])
            S.op("dve", lambda e: e.tensor_reduce(out=SM4[:, 0:32], in_=SCs[:], axis=AX.X, op=ALU.max), reads=[SCs.b], writes=[SM4.b])
            S.op("dve", lambda e: e.tensor_tensor(out=SCs[:], in0=SCs[:], in1=SM4[:, 0:32].unsqueeze(2).to_broadcast([4, 32, 32]),
                                                  op=ALU.subtract), reads=[SCs.b, SM4.b], writes=[SCs.b])
            S.op("act", lambda e: e.activation(out=SCs[:].rearrange("p j n -> p (j n)"), in_=SCs[:].rearrange("p j n -> p (j n)"),
                                               func=AF.Exp), reads=[SCs.b], writes=[SCs.b])
            S.op("dve", lambda e: e.tensor_reduce(out=SM4[:, 32:64], in_=SCs[:], axis=AX.X, op=ALU.add), reads=[SCs.b], writes=[SM4.b])
            S.op("dve", lambda e: e.reciprocal(out=SM4[:, 64:96], in_=SM4[:, 32:64]), reads=[SM4.b], writes=[SM4.b])
            S.op("dve", lambda e: e.tensor_tensor(out=SCs[:], in0=SCs[:], in1=SM4[:, 64:96].unsqueeze(2).to_broadcast([4, 32, 32]),
                                                  op=ALU.mult), reads=[SCs.b, SM4.b], writes=[SCs.b])
            store(bounce[0:4, 0:1024], SCs[:].rearrange("p j n -> p (j n)"), [SCs.b], "o_bnc", writes=[BNC])
            load(PCT2[:], bounce[0:4, 0:1024].rearrange("g (j n) -> j g n", j=32), [PCT2.b], reads=[BNC])
            S.op("dve", lambda e: e.tensor_reduce(out=IMPs[:], in_=PCT2[:].rearrange("p g n -> p n g"), axis=AX.X, op=ALU.add),
                 reads=[PCT2.b], writes=[IMPs.b])
            S.op("dve", lambda e: e.tensor_tensor(out=SCRs[:, 0:32], in0=IMPs[:], in1=SFORCE[:, 0:32], op=ALU.add),
                 reads=[IMPs.b, SFORCE.b], writes=[SCRs.b])
            S.op("dve", lambda e: e.tensor_copy(out=SCRs[:, 32:33], in_=SFORCE[:, 32:33]), reads=[SFORCE.b], writes=[SCRs.b])
            S.op("dve", lambda e: e.max(out=M8s[:], in_=SCRs[:, 0:33]), reads=[SCRs.b], writes=[M8s.b])
            S.op("dve", lambda e: e.match_replace(out=WK1s[:, 0:33], in_to_replace=M8s[:], in_values=SCRs[:, 0:33], imm_value=NEG),
                 reads=[SCRs.b, M8s.b], writes=[WK1s.b])
            S.op("dve", lambda e: e.max(out=M8s[:], in_=WK1s[:, 0:33]), reads=[WK1s.b], writes=[M8s.b])
            S.op("dve", lambda e: e.match_replace(out=WK2s[:, 0:33], in_to_replace=M8s[:], in_values=WK1s[:, 0:33], imm_value=NEG),
                 reads=[WK1s.b, M8s.b], writes=[WK2s.b])
            S.op("dve", lambda e: e.tensor_tensor(out=SELs[:], in0=WK2s[:, 0:33], in1=SCRs[:, 0:33], op=ALU.not_equal),
                 reads=[WK2s.b, SCRs.b], writes=[SELs.b])
            S.op("pe", lambda e: e.transpose(out=TP[0:33, 0:32], in_=SELs[:], identity=ident[0:32, 0:32]),
                 reads=[SELs.b, ident.b], writes=[TP.b])
            S.op("act", lambda e: e.copy(out=SELTs[:], in_=TP[0:33, 0:32]), reads=[TP.b], writes=[SELTs.b])
            for kt in range(17):
                pj = PJ[0] if kt < 9 else PJ[1]
                k0 = kt if kt < 9 else kt - 9
                S.op("pe", lambda e, kt=kt, pj=pj, k0=k0: e.matmul(pj[:, k0 * 32:(k0 + 1) * 32], lhsT=SEXPE[:, kt * 128:(kt + 1) * 128],
                                                                   rhs=SELTs[:], start=True, stop=True),
                     reads=[SEXPE.b, SELTs.b], writes=[pj.b])
            S.op("act", lambda e: e.copy(out=MTS[:, 0:9, :].rearrange("p k j -> p (k j)"), in_=PJ[0][:, 0:288]),
                 reads=[PJ[0].b], writes=[MTS.b])
            S.op("act", lambda e: e.copy(out=MTS[:, 9:17, :].rearrange("p k j -> p (k j)"), in_=PJ[1][:, 0:256]),
                 reads=[PJ[1].b], writes=[MTS.b])
            S.op("dve", lambda e: e.tensor_copy(out=PCT2B[:], in_=PCT2[:]), reads=[PCT2.b], writes=[PCT2B.b])
            for g in range(4):
                S.op("pe", lambda e, g=g: e.transpose(out=TP[0:32, g * 32:(g + 1) * 32], in_=PCT2B[:, g, :], identity=ident[0:32, 0:32]),
                     reads=[PCT2B.b, ident.b], writes=[TP.b])
            S.op("act", lambda e: e.copy(out=PCTg[:].rearrange("p g j -> p (g j)"), in_=TP[0:32, 0:128]), reads=[TP.b], writes=[PCTg.b])
            for rnd in range(4):
                pj = PJ[rnd % 2]
                for jj in range(8):
                    j = rnd * 8 + jj
                    s_, kv = j // 2, j % 2
                    S.op("pe", lambda e, j=j, jj=jj, s_=s_, kv=kv, pj=pj: e.matmul(
                        pj[0:4, jj * 64:(jj + 1) * 64], lhsT=PCTg[:, :, j], rhs=VCS[:, s_, kv, :], start=True, stop=True),
                         reads=[PCTg.b, VCS.b], writes=[pj.b])
                S.op("act", lambda e, rnd=rnd, pj=pj: e.copy(out=OSS[:, rnd * 8:(rnd + 1) * 8, 0:64],
                                                            in_=pj[0:4, :].rearrange("p (j d) -> p j d", j=8)),
                     reads=[pj.b], writes=[OSS.b])

            def bounce_out(br):
                store(bounce[0:4, 0:2080], OSS[:].rearrange("p j d -> p (j d)"), [OSS.b], "o_bnc", writes=[BNC])
                load(OT[:, br, :, :].rearrange("s (k g) d -> s k g d", k=2),
                     bounce[0:4, 0:2080].rearrange("g (s k d) -> s k g d", s=NS, k=2), [OT.b], reads=[BNC])

            bounce_out(0)

            S.fence()
            sb2 = ss.enter_context(ExitStack())
            KSB = sb([128, NPG, 128], BF16, "KSB", sb2)
            KTSs = sb([64, 2, 17, 128], BF16, "KTSs", sb2)
            VSs = sb([128, 17, 2, 65], BF16, "VSs", sb2)
            KTWs = sb([64, 2, 5, 128], BF16, "KTWs", sb2)
            VWs = sb([128, 5, 2, 65], BF16, "VWs", sb2)
            SSs = sb([128, 17, 4], F32, "SSs", sb2)
            PTs = sb([128, 17, 4], BF16, "PTs", sb2)
            S.op("dve", lambda e: e.memset(KTSs[:], 0.0), writes=[KTSs.b])
            S.op("dve", lambda e: e.memset(KTWs[:], 0.0), writes=[KTWs.b])
            S.op("dve", lambda e: e.memset(VSs[:], 0.0), writes=[VSs.b])
            S.op("dve", lambda e: e.memset(VWs[:], 0.0), writes=[VWs.b])
            S.op("dve", lambda e: e.memset(VSs[:, :, :, 64:65], 1.0), writes=[VSs.b])
            S.op("dve", lambda e: e.memset(VWs[:, :, :, 64:65], 1.0), writes=[VWs.b])
            def attend_s(s_, br):
                nkt = 17 if br == 1 else 5
                KTt, Vt = (KTSs, VSs) if br == 1 else (KTWs, VWs)
                ALBt = SALB if br == 1 else SWALB
                for kv in range(2):
                    j = s_ * 2 + kv
                    st = ST[j % 2]
                    for kt in range(nkt):
                        S.op("pe", lambda e, kt=kt, st=st, kv=kv: e.matmul(
                            st[:, kt * 4:(kt + 1) * 4], lhsT=KTt[:, kv, kt, :], rhs=QTS[:, kv * 4:(kv + 1) * 4, s_],
                            start=True, stop=True), reads=[KTt.b, QTS.b], writes=[st.b])
                    S.op("dve", lambda e, st=st, kv=kv: e.tensor_tensor(
                        out=SSs[:, 0:nkt, :], in0=st[:, 0:nkt * 4].rearrange("p (k g) -> p k g", g=4),
                        in1=ALBt[:, :, kv * 4:(kv + 1) * 4], op=ALU.add), reads=[st.b, ALBt.b], writes=[SSs.b])
                    S.op("act", lambda e: e.activation(out=PTs[:, 0:nkt, :], in_=SSs[:, 0:nkt, :], func=AF.Exp),
                         reads=[SSs.b], writes=[PTs.b])
                    if br == 1:
                        m_ap = MTS[:, :, j:j + 1].to_broadcast([128, 17, 4])
                        mrd = [MTS.b]
                    else:
                        m_ap = SWMASK[:].unsqueeze(2).to_broadcast([128, 5, 4])
                        mrd = [SWMASK.b]
                    S.op("dve", lambda e, m_ap=m_ap: e.tensor_tensor(out=PTs[:, 0:nkt, :], in0=PTs[:, 0:nkt, :], in1=m_ap, op=ALU.mult),
                         reads=[PTs.b] + mrd, writes=[PTs.b])
                    for kt in range(nkt):
                        S.op("pe", lambda e, kt=kt, kv=kv: e.matmul(OA[0:4, 0:65], lhsT=PTs[:, kt, :], rhs=Vt[:, kt, kv, :],
                                                                     start=(kt == 0), stop=(kt == nkt - 1)),
                             reads=[PTs.b, Vt.b], writes=[OA.b])
                    S.op("act", lambda e, j=j: e.copy(out=OSS[:, j, :], in_=OA[0:4, 0:65]), reads=[OA.b], writes=[OSS.b])

            for s_ in range(nsq):
                slb = (nsq + s_) % 2
                if s_ + 1 < nsq:
                    gather(cks, s_ + 1, (nsq + s_ + 1) % 2)
                PG = PGL[slb]
                PGb = PGbL[slb]
                S.op("dve", lambda e, PG=PG: e.tensor_copy(out=KSB[:], in_=PG[:, :, 0:128]), reads=PGb, writes=[KSB.b])
                S.op("dve", lambda e, PG=PG: e.tensor_copy(out=VSs[:, 0:16, :, 0:64],
                                                    in_=PG[:, :, 128:256].rearrange("p j (k d) -> p j k d", k=2)),
                     reads=PGb, writes=[VSs.b])
                S.op("sp", lambda e, s_=s_: e.dma_start(out=VSs[0:1, 16, :, 0:64], in_=VNB2[s_:s_ + 1, 0, :, :]),
                     reads=[VNB2.b], writes=[VSs.b], dma="L_vnew")
                for q4 in range(4):
                    for jl in range(4):
                        for kv in range(2):
                            S.op("pe", lambda e, q4=q4, jl=jl, kv=kv: e.transpose(
                                out=TP[0:64, (kv * 4 + jl) * 128:(kv * 4 + jl + 1) * 128],
                                in_=KSB[:, q4 * 4 + jl, kv * 64:(kv + 1) * 64], identity=ident[:]),
                                 reads=[KSB.b, ident.b], writes=[TP.b])
                    S.op("act", lambda e, q4=q4: e.copy(out=KTSs[:, :, q4 * 4:(q4 + 1) * 4, :],
                                                        in_=TP[0:64, :].rearrange("p (k j t) -> p k j t", k=2, j=4)),
                         reads=[TP.b], writes=[KTSs.b])
                S.op("dve", lambda e, s_=s_: e.tensor_copy(out=KTSs[:, :, 16, 0:1], in_=KNT[:, 0:2, s_:s_ + 1]),
                     reads=[KNT.b], writes=[KTSs.b])
                attend_s(s_, 1)
            bounce_out(1)
            load(PGL[0][:, 0:4, :], ckw[l, 0].rearrange("(k p) f -> p k f", p=128), PGbL[0][0:4])
            for s_ in range(nsq):
                if s_ + 1 < nsq:
                    load(PGL[(s_ + 1) % 2][:, 0:4, :], ckw[l, s_ + 1].rearrange("(k p) f -> p k f", p=128),
                         PGbL[(s_ + 1) % 2][0:4])
                PG = PGL[s_ % 2]
                PGb = PGbL[s_ % 2]
                S.op("dve", lambda e, PG=PG: e.tensor_copy(out=KSB[:, 0:4, :], in_=PG[:, 0:4, 0:128]), reads=PGb[0:4], writes=[KSB.b])
                S.op("dve", lambda e, PG=PG: e.tensor_copy(out=VWs[:, 0:4, :, 0:64],
                                                    in_=PG[:, 0:4, 128:256].rearrange("p j (k d) -> p j k d", k=2)),
                     reads=PGb[0:4], writes=[VWs.b])
                S.op("sp", lambda e, s_=s_: e.dma_start(out=VWs[0:1, 4, :, 0:64], in_=VNB2[s_:s_ + 1, 1, :, :]),
                     reads=[VNB2.b], writes=[VWs.b], dma="L_vnew")
                for jl in range(4):
                    for kv in range(2):
                        S.op("pe", lambda e, jl=jl, kv=kv: e.transpose(
                            out=TP[0:64, (kv * 4 + jl) * 128:(kv * 4 + jl + 1) * 128],
                            in_=KSB[:, jl, kv * 64:(kv + 1) * 64], identity=ident[:]),
                             reads=[KSB.b, ident.b], writes=[TP.b])
                S.op("act", lambda e: e.copy(out=KTWs[:, :, 0:4, :], in_=TP[0:64, :].rearrange("p (k j t) -> p k j t", k=2, j=4)),
                     reads=[TP.b], writes=[KTWs.b])
                S.op("dve", lambda e, s_=s_: e.tensor_copy(out=KTWs[:, :, 4, 0:1], in_=KNT[:, 2:4, s_:s_ + 1]),
                     reads=[KNT.b], writes=[KTWs.b])
                attend_s(s_, 2)
            bounce_out(2)

            sb2.close()
            S.fence()
            sf = ss.enter_context(ExitStack())
            SPT = sb([NS, 15, 256], F32, "SPT", sf)
            PSUMS = sb([NS, 256], F32, "PSUMS", sf)
            DIFB = sb([NS, 256], BF16, "DIFB", sf)
            DTs = sb([64, 4, NS], BF16, "DTs", sf)
            CSs = sb([NS, 16], F32, "CSs", sf)
            M = NS
            S.op("dve", lambda e: e.tensor_tensor(
                out=YA[0:M, :].rearrange("p (h d) -> p h d", h=8), in0=OT[:, 0, :, 0:64],
                in1=GT[0:M, :].rearrange("p (h t) -> p h t", t=3)[:, :, 0:1].to_broadcast([M, 8, 64]), op=ALU.mult),
                 reads=[OT.b, GT.b], writes=[YA.b])
            for br in (1, 2):
                S.op("dve", lambda e, br=br: e.tensor_scalar(out=CSs[:, 0:8], in0=OT[:, br, :, 64], scalar1=1e-30, scalar2=None,
                                                             op0=ALU.max), reads=[OT.b], writes=[CSs.b])
                S.op("dve", lambda e, br=br: e.reciprocal(out=CSs[:, 0:8], in_=CSs[:, 0:8]), reads=[CSs.b], writes=[CSs.b])
                S.op("dve", lambda e, br=br: e.tensor_tensor(out=CSs[:, 8:16], in0=CSs[:, 0:8],
                                                             in1=GT[0:M, :].rearrange("p (h t) -> p h t", t=3)[:, :, br], op=ALU.mult),
                     reads=[CSs.b, GT.b], writes=[CSs.b])
                for h in range(8):
                    S.op("dve", lambda e, br=br, h=h: e.scalar_tensor_tensor(
                        out=YA[0:M, h * 64:(h + 1) * 64], in0=OT[:, br, h, 0:64], scalar=CSs[:, 8 + h:9 + h],
                        in1=YA[0:M, h * 64:(h + 1) * 64], op0=ALU.mult, op1=ALU.add),
                         reads=[OT.b, CSs.b, YA.b], writes=[YA.b])
            S.op("dve", lambda e: e.tensor_tensor(out=MIX[0:M, 0:512], in0=YA[0:M, :], in1=SZ[0:M, 0:512], op=ALU.mult),
                 reads=[YA.b, SZ.b], writes=[MIX.b])
            load(SPT[:], spool[l], [SPT.b])
            for g, w in enumerate((2, 4, 8, 16)):
                S.op("dve", lambda e, g=g, w=w: e.tensor_reduce(
                    out=PSUMS[:, g * 64:(g + 1) * 64],
                    in_=SPT[:, 15 - (w - 1):15, g * 64:(g + 1) * 64].rearrange("p r c -> p c r"), axis=AX.X, op=ALU.add),
                     reads=[SPT.b], writes=[PSUMS.b])
                S.op("dve", lambda e, g=g, w=w: e.tensor_scalar(out=PSUMS[:, g * 64:(g + 1) * 64], in0=PSUMS[:, g * 64:(g + 1) * 64],
                                                                scalar1=1.0 / w, scalar2=None, op0=ALU.mult),
                     reads=[PSUMS.b], writes=[PSUMS.b])
                S.op("dve", lambda e, g=g, w=w: e.scalar_tensor_tensor(
                    out=DIFB[:, g * 64:(g + 1) * 64], in0=PIN[0:M, g * 64:(g + 1) * 64], scalar=(1.0 / w - 1.0),
                    in1=PSUMS[:, g * 64:(g + 1) * 64], op0=ALU.mult, op1=ALU.add), reads=[PIN.b, PSUMS.b], writes=[DIFB.b])
            for g in range(4):
                S.op("pe", lambda e, g=g: e.transpose(out=TP[0:64, g * M:(g + 1) * M], in_=DIFB[:, g * 64:(g + 1) * 64],
                                                      identity=ident[0:M, 0:M]), reads=[DIFB.b, ident.b], writes=[TP.b])
            S.op("act", lambda e: e.copy(out=DTs[:].rearrange("p g s -> p (g s)"), in_=TP[0:64, 0:4 * M]), reads=[TP.b], writes=[DTs.b])
            for g in range(4):
                S.op("pe", lambda e, g=g: e.matmul(MS[0:M, g * 64:(g + 1) * 64], lhsT=DTs[:, g, :], rhs=WPOOL[:, g, :],
                                                   start=True, stop=True), reads=[DTs.b, WPOOL.b], writes=[MS.b])
            S.op("dve", lambda e: e.tensor_tensor(out=TMPB[0:M, :], in0=MS[0:M, 0:256], in1=PSC[0:M, :], op=ALU.mult),
                 reads=[MS.b, PSC.b], writes=[TMPB.b])
            S.op("dve", lambda e: e.tensor_tensor(out=MIX[0:M, 512:768], in0=TMPB[0:M, :], in1=SZ[0:M, 512:768], op=ALU.mult),
                 reads=[TMPB.b, SZ.b], writes=[MIX.b])
            gmlp_norm(M)
            store(gv_s[l], VN[0:M, :], [VN.b], "o_vn")
            for g in range(4):
                S.op("dve", lambda e, g=g: e.tensor_scalar(out=YC[0:M, g * 64:(g + 1) * 64], in0=VN[0:M, g * 64:(g + 1) * 64],
                                                           scalar1=WS00[0:M, g:g + 1], scalar2=BS0[0:M, g:g + 1],
                                                           op0=ALU.mult, op1=ALU.add), reads=[VN.b, WS00.b, BS0.b], writes=[YC.b])
            S.op("dve", lambda e: e.tensor_tensor(out=YC[0:M, :], in0=YC[0:M, :], in1=UV[0:M, 0:256], op=ALU.mult),
                 reads=[YC.b, UV.b], writes=[YC.b])
            S.op("dve", lambda e: e.tensor_tensor(out=MIX[0:M, 768:1024], in0=YC[0:M, :], in1=SZ[0:M, 768:1024], op=ALU.mult),
                 reads=[YC.b, SZ.b], writes=[MIX.b])
            if l == 0 and not dbg_y0:
                run(merge_out(M, xt, xs1, [XS1b], "o_y", 0))
            else:
                run(merge_out(M, xt, y_s, [], "o_y", 0))

        for l in layers:
            load_weights(l)
            if do_prompt:
                S.fence()
                with ExitStack() as pstack:
                    prompt_layer(l, pstack)
                S.fence()
            if do_sample:
                S.fence()
                with ExitStack() as sstack:
                    sample_layer(l, sstack)
                S.fence()
        S.emit()
    return nc


def _consts():
    bf = ml_dtypes.bfloat16
    c = {}
    c["c_ident"] = np.eye(128, dtype=np.float32).astype(bf)
    c["c_identf"] = np.eye(128, dtype=np.float32)
    kk = np.arange(128)[:, None]
    qq = np.arange(128)[None, :]
    c["c_trile"] = (kk <= qq).astype(np.float32).astype(bf)
    c["c_trigt"] = (kk > qq).astype(np.float32).astype(bf)
    tk = np.arange(T)
    c["c_kaug"] = np.stack([np.ones(T), np.ones(T), tk // 128, tk % 128]).astype(np.float32).astype(bf)
    qa = np.zeros((NT, 4, 2, 4, 128), np.float32)
    a = np.arange(128)
    for i in range(NT):
        for kv in range(2):
            for g in range(4):
                s = SLOPES[kv * 4 + g]
                qa[i, 0, kv, g, :] = -s * 128.0 * i
                qa[i, 1, kv, g, :] = -s * a
                qa[i, 2, kv, g, :] = s * 128.0
                qa[i, 3, kv, g, :] = s
    qa2 = np.broadcast_to(qa.reshape(NT, 4, 2, 1, 512), (NT, 4, 2, 2, 512))
    c["c_qaug"] = np.ascontiguousarray(qa2).reshape(NT, 4, 2048).astype(bf)
    jj = np.arange(32)[:, None]
    c["c_kmask"] = (((tk[None, :] // 64) % 32) == jj).astype(np.float32).astype(bf)
    n = np.arange(64)[:, None]
    c["c_expe"] = (n == (tk[None, :] // 64)).astype(np.float32).astype(bf)
    albm = np.zeros((128, 8, 64), np.float32)
    for h in range(8):
        albm[:, h, :] = SLOPES[h] * 64.0 * np.arange(64)[None, :]
    c["c_alb"] = albm.reshape(128, 512)
    c["c_iota"] = np.tile(np.arange(64, dtype=np.float32)[None, :], (128, 1))
    misc = np.zeros((128, 8), np.float32)
    misc[:, 0] = -1.0 + (a >= 63) + (a >= 127)
    misc[:, 1] = (a >= 64)
    misc[:, 2] = -0.5
    misc[:, 3] = a
    c["c_misc"] = misc
    f0 = np.zeros((128, 64), np.float32)
    f0[:, 0] = 1e4
    c["c_force0"] = f0
    pb = np.zeros((128, 3, 4, 128), np.float32)
    tt = np.arange(128)
    for g, w in enumerate((2, 4, 8, 16)):
        for tp in range(128):
            for s_ in range(w):
                t = tp - s_
                if t >= 0:
                    pb[t, 0, g, tp] += 1.0 / w
                    pb[t, 2, g, tp] += 1.0 / min(w, tp + 1)
                else:
                    pb[128 + t, 1, g, tp] += 1.0 / w
            pb[tp, 0, g, tp] -= 1.0
            pb[tp, 2, g, tp] -= 1.0
    c["c_poolb"] = pb.astype(bf)
    c["c_trilT"] = (kk <= qq).astype(np.float32)
    cc = np.arange(128)[:, None]
    salb = np.zeros((128, 17, 8), np.float32)
    swalb = np.zeros((128, 5, 8), np.float32)
    for h in range(8):
        for kt in range(17):
            salb[:, kt, h] = -SLOPES[h] * (2048.0 - (kt * 128 + np.arange(128)))
        for kt in range(5):
            swalb[:, kt, h] = -SLOPES[h] * (2048.0 - (1536 + kt * 128 + np.arange(128)))
    salb[1:, 16, :] = 0.0
    swalb[1:, 4, :] = 0.0
    c["c_salb"] = salb
    c["c_swalb"] = swalb
    swm = np.ones((128, 5), np.float32)
    swm[0, 0] = 0.0
    swm[1:, 4] = 0.0
    c["c_swmask"] = swm
    scalb = np.zeros((4, 2, 32), np.float32)
    for kv in range(2):
        for g in range(4):
            scalb[g, kv, :] = SLOPES[kv * 4 + g] * 64.0 * np.arange(32)
    c["c_scalb"] = scalb
    sexpe = np.zeros((33, 17 * 128), np.float32)
    for pos in range(2049):
        sexpe[pos // 64, pos] = 1.0
    c["c_sexpe"] = sexpe.astype(bf)
    sf = np.zeros((32, 33), np.float32)
    sf[:, 0] = 1e4
    sf[:, 32] = 1e4
    c["c_sforce"] = sf
    c["c_spool"] = np.zeros((16, 4, 16), np.float32)
    c["c_smask0"] = np.zeros((128, 17), np.float32)
    return c


_NC_CACHE = {}


def kernel(x_prompt, x_sample, cache_kv_cmp, cache_kv_sel, cache_kv_win, state_pool, page_table,
           norm_pre, w_in, cmp_pe, cmp_w1, cmp_w2, pool_w, pool_scale, gmlp_norm, gmlp_ws, gmlp_bs,
           w_out, norm_post, _build_kwargs=None):
    kw = _build_kwargs or {}
    key = tuple(sorted((k, str(v)) for k, v in kw.items()))
    if key not in _NC_CACHE:
        _NC_CACHE[key] = build_program(**kw)
    nc = _NC_CACHE[key]
    in_maps = _prepare(kw, x_prompt, x_sample, cache_kv_cmp, cache_kv_sel, cache_kv_win, state_pool, page_table,
                       norm_pre, w_in, cmp_pe, cmp_w1, cmp_w2, pool_w, pool_scale, gmlp_norm, gmlp_ws, gmlp_bs,
                       w_out, norm_post)
    res = run_bass_kernel_spmd(nc, in_maps, core_ids=list(range(NCORES)))
    return _assemble(res.results)


def _prepare(kw, x_prompt, x_sample, cache_kv_cmp, cache_kv_sel, cache_kv_win, state_pool, page_table,
             norm_pre, w_in, cmp_pe, cmp_w1, cmp_w2, pool_w, pool_scale, gmlp_norm, gmlp_ws, gmlp_bs,
             w_out, norm_post):
    f32 = np.float32
    consts = _consts()
    npre_t = np.ascontiguousarray(np.asarray(norm_pre, f32).reshape(2, 8, 128).transpose(0, 2, 1))
    gpost_rep = np.ascontiguousarray(np.broadcast_to(np.asarray(norm_post, f32)[:, None, :], (2, 128, D)))
    pe = np.asarray(cmp_pe, f32)
    pe_tok = np.zeros((2, 128, 2, 2, 64), f32)
    for kv in range(2):
        pe_tok[:, 0:64, :, kv, :] = pe.transpose(0, 2, 1, 3)
        pe_tok[:, 64:128, :, kv, :] = pe.transpose(0, 2, 1, 3)
    pe_tok = pe_tok.reshape(2, 128, 256)
    w1 = np.asarray(cmp_w1, f32)
    w1bd = np.zeros((2, 128, 2, 128), f32)
    for r in range(2):
        w1bd[:, 0:64, r, 0:64] = w1[:, r]
        w1bd[:, 64:128, r, 64:128] = w1[:, r]
    w2 = np.asarray(cmp_w2, f32)
    w2kv = np.zeros((2, 128, 4, 64), f32)
    for kv in range(2):
        w2kv[:, kv * 64:(kv + 1) * 64, kv, :] = w2[:, 0]
        w2kv[:, kv * 64:(kv + 1) * 64, 2 + kv, :] = w2[:, 1]
    wpool = np.ascontiguousarray(np.asarray(pool_w, f32).transpose(0, 2, 1, 3))
    pscale_rep = np.ascontiguousarray(np.broadcast_to(np.asarray(pool_scale, f32)[:, None, :], (2, 128, 256)))
    gnorm_rep = np.ascontiguousarray(np.broadcast_to(np.asarray(gmlp_norm, f32)[:, None, :], (2, 128, 256)))
    ws = np.asarray(gmlp_ws, f32)
    wsT = np.ascontiguousarray(ws.transpose(0, 3, 1, 2))
    bsT = np.ascontiguousarray(np.asarray(gmlp_bs, f32).transpose(0, 2, 1))
    ws00_rep = np.ascontiguousarray(np.broadcast_to(ws[:, None, :, 0, 0], (2, 128, 4)))
    bs0_rep = np.ascontiguousarray(np.broadcast_to(np.asarray(gmlp_bs, f32)[:, None, :, 0], (2, 128, 4)))
    if kw.get("do_sample", True):
        ckc = np.asarray(cache_kv_cmp, f32).reshape(2 * NPHYS * 128, 256)
        cks = np.asarray(cache_kv_sel, f32).reshape(2 * NPHYS * 128, 256)
    else:
        ckc = cks = np.zeros((128, 256), f32)
    ckw_all = np.asarray(cache_kv_win, f32).reshape(2, 128, 512, 256)
    sp_all = np.asarray(state_pool, f32)
    xs_all = np.asarray(x_sample, f32).reshape(128, D)
    pt_all = np.asarray(page_table, np.int32)
    shared = dict(w_in=np.asarray(w_in, f32), w_out=np.asarray(w_out, f32), npre_t=npre_t, gpost_rep=gpost_rep,
                  pe_tok=pe_tok, w1bd=w1bd, w2kv=w2kv, wpool=wpool, pscale_rep=pscale_rep, gnorm_rep=gnorm_rep,
                  wsT=wsT, bsT=bsT, ws00_rep=ws00_rep, bs0_rep=bs0_rep, ckc=ckc, cks=cks, **consts)
    in_maps = []
    for c in range(NCORES):
        m = dict(shared)
        m["xp"] = np.ascontiguousarray(np.asarray(x_prompt, f32)[c % 4])
        sl = slice(c * NS, (c + 1) * NS)
        m["xs"] = np.ascontiguousarray(xs_all[sl])
        m["ckw"] = np.ascontiguousarray(ckw_all[:, sl])
        m["spool"] = np.ascontiguousarray(sp_all[:, sl])
        m["ptab"] = np.ascontiguousarray(np.broadcast_to(pt_all[sl].reshape(1, NS * NPG), (128, NS * NPG)))
        in_maps.append(m)
    return in_maps


def _assemble(R):
    y_prompt = np.stack([R[b]["y_p"] for b in range(4)])
    y_sample = np.concatenate([R[c]["y_s"] for c in range(NCORES)], 0).reshape(128, 1, D)

    def pstk(name, shp):
        return np.stack([R[b][name] for b in range(4)], axis=1).reshape(shp)

    def sstk(name, shp):
        return np.concatenate([R[c][name] for c in range(NCORES)], axis=1).reshape(shp)

    outs = (
        y_prompt, y_sample,
        pstk("kvc_p", (2, 4, T, 2, 2, 64)), sstk("kvc_s", (2, 128, 1, 2, 2, 64)),
        pstk("kvs_p", (2, 4, T, 2, 2, 64)), sstk("kvs_s", (2, 128, 1, 2, 2, 64)),
        pstk("kvw_p", (2, 4, 512, 2, 2, 64)), sstk("kvw_s", (2, 128, 512, 2, 2, 64)),
        pstk("pool_p", (2, 4, 15, 256)), sstk("pool_s", (2, 128, 15, 256)),
        pstk("gv_p", (2, 4, 128, 256)), sstk("gv_s", (2, 128, 1, 256)),
    )
    return tuple(np.ascontiguousarray(o, dtype=np.float32) for o in outs)
```

```python
import numpy as np
import ml_dtypes
from contextlib import ExitStack
import concourse.bass as bass
import concourse.mybir as mybir
from concourse.bass_utils import run_bass_kernel_spmd

F32 = mybir.dt.float32
BF16 = mybir.dt.bfloat16
I32 = mybir.dt.int32
AF = mybir.ActivationFunctionType
ALU = mybir.AluOpType
AX = mybir.AxisListType

NCORES = 8
T = 4096
NT = T // 128
D = 1024
DIN = 3096
NS = 16
NPG = 16
NPHYS = 2560
EPS = 1e-6
SLOPES = [2.0 ** (-(h + 1)) for h in range(8)]
NEG = -30000.0

PG = [(0, 512), (512, 1024), (1024, 1304), (1304, 1816), (1816, 2328), (2328, 2840), (2840, 3096)]


class Buf:
    __slots__ = ("name", "w", "r", "excl")

    def __init__(self, name, excl=False):
        self.name = name
        self.w = None
        self.r = {}
        self.excl = excl


class Sched:
    ENG = ["pe", "act", "dve", "pool", "sp"]
    ATTR = {"pe": "tensor", "act": "scalar", "dve": "vector", "pool": "gpsimd", "sp": "sync"}

    def __init__(self, nc, es):
        self.nc = nc
        self.es = es
        self.sem = {e: es.enter_context(nc.semaphore("s_" + e)) for e in self.ENG}
        self.cnt = {e: 0 for e in self.ENG}
        self.prog = {e: [] for e in self.ENG}
        self.waited = {e: {} for e in self.ENG}
        self.streams = {}
        self.pending = {e: [] for e in self.ENG}

    def stream(self, name):
        if name not in self.streams:
            self.streams[name] = [self.es.enter_context(self.nc.semaphore("d_" + name)), 0]
        return self.streams[name]

    def fence(self):
        snap = [(self.sem[e], self.cnt[e]) for e in self.ENG if self.cnt[e] > 0]
        snap += [(st[0], st[1]) for st in self.streams.values() if st[1] > 0]
        for e in self.ENG:
            self.pending[e] = list(snap)

    def op(self, eng, fn, reads=(), writes=(), dma=None):
        writes = list(writes) + [b for b in reads if b.excl]
        reads = [b for b in reads if not b.excl]
        deps = {}

        def add(ev):
            if ev is None:
                return
            k = id(ev[0])
            if k not in deps or deps[k][1] < ev[1]:
                deps[k] = ev

        for b in reads:
            add(b.w)
        for b in writes:
            if not b.r:
                add(b.w)
            for ev in b.r.values():
                add(ev)
        if dma is None:
            self.cnt[eng] += 1
            ev = (self.sem[eng], self.cnt[eng], eng)
            inc = (self.sem[eng], 1)
        else:
            st = self.stream(dma)
            st[1] += 16
            ev = (st[0], st[1], "dma")
            inc = (st[0], 16)
        waits = []
        for (sem, val) in self.pending[eng]:
            k = id(sem)
            if sem is self.sem.get(eng):
                continue
            if self.waited[eng].get(k, 0) >= val:
                continue
            self.waited[eng][k] = val
            waits.append((sem, val))
        self.pending[eng] = []
        for k, (sem, val, src) in deps.items():
            if src == eng and eng == "pe":
                continue
            if self.waited[eng].get(k, 0) >= val:
                continue
            self.waited[eng][k] = val
            waits.append((sem, val))
        self.prog[eng].append((waits, fn, inc))
        k = id(ev[0])
        for b in reads:
            if k not in b.r or b.r[k][1] < ev[1]:
                b.r[k] = ev
        for b in writes:
            b.w = ev
            b.r = {}
        return ev

    def emit(self):
        nc = self.nc
        finals = [(self.sem[e], self.cnt[e]) for e in self.ENG if e != "sp" and self.cnt[e] > 0]
        finals += [(st[0], st[1]) for st in self.streams.values() if st[1] > 0]
        with nc.Block() as block:
            for e in self.ENG:
                prog = self.prog[e]

                def body(engine, prog=prog, e=e):
                    for waits, fn, inc in prog:
                        for sem, val in waits:
                            engine.wait_ge(sem, val)
                        ins = fn(engine)
                        ins.then_inc(inc[0], inc[1])
                    if e == "sp":
                        for sem, val in finals:
                            engine.wait_ge(sem, val)

                getattr(block, self.ATTR[e])(body)


class TB:
    def __init__(self, t, name):
        self.t = t
        self.b = Buf(name)

    def __getitem__(self, idx):
        return self.t[idx]


def build_program(layers=(0, 1), do_prompt=True, do_sample=True, nt_limit=NT, ns_limit=NS, dbg_y0=False, stage=99):
    nc = bass.Bass("TRN2", target_bir_lowering=False)

    def din(name, shape, dt=F32):
        return nc.dram_tensor(name, list(shape), dt, kind="ExternalInput").ap()

    def dout(name, shape, dt=F32):
        return nc.dram_tensor(name, list(shape), dt, kind="ExternalOutput").ap()

    xp = din("xp", [T, D])
    xs = din("xs", [NS, D])
    ncache = 2 * NPHYS * 128 if do_sample else 128
    ckc = din("ckc", [ncache, 256])
    cks = din("cks", [ncache, 256])
    ckw = din("ckw", [2, NS, 512, 256])
    spool = din("spool", [2, NS, 15, 256])
    ptab = din("ptab", [128, NS * NPG], I32)
    w_in = din("w_in", [2, D, DIN])
    w_out = din("w_out", [2, D, D])
    npre_t = din("npre_t", [2, 128, 8])
    gpost_rep = din("gpost_rep", [2, 128, D])
    pe_tok = din("pe_tok", [2, 128, 256])
    w1bd = din("w1bd", [2, 128, 2, 128])
    w2kv = din("w2kv", [2, 128, 4, 64])
    wpool = din("wpool", [2, 64, 4, 64])
    pscale_rep = din("pscale_rep", [2, 128, 256])
    gnorm_rep = din("gnorm_rep", [2, 128, 256])
    wsT = din("wsT", [2, 128, 4, 128])
    bsT = din("bsT", [2, 128, 4])
    ws00_rep = din("ws00_rep", [2, 128, 4])
    bs0_rep = din("bs0_rep", [2, 128, 4])
    c_ident = din("c_ident", [128, 128], BF16)
    c_identf = din("c_identf", [128, 128])
    c_trile = din("c_trile", [128, 128], BF16)
    c_trigt = din("c_trigt", [128, 128], BF16)
    c_kaug = din("c_kaug", [4, T], BF16)
    c_qaug = din("c_qaug", [NT, 4, 2048], BF16)
    c_kmask = din("c_kmask", [32, T], BF16)
    c_expe = din("c_expe", [64, T], BF16)
    c_alb = din("c_alb", [128, 512])
    c_iota = din("c_iota", [128, 64])
    c_misc = din("c_misc", [128, 8])
    c_force0 = din("c_force0", [128, 64])
    c_poolb = din("c_poolb", [128, 3, 4, 128], BF16)
    c_trilT = din("c_trilT", [128, 128])
    c_salb = din("c_salb", [128, 17, 8])
    c_swalb = din("c_swalb", [128, 5, 8])
    c_swmask = din("c_swmask", [128, 5])
    c_scalb = din("c_scalb", [4, 2, 32])
    c_sexpe = din("c_sexpe", [33, 17 * 128], BF16)
    c_sforce = din("c_sforce", [32, 33])
    c_spool = din("c_spool", [16, 4, 16])
    c_smask0 = din("c_smask0", [128, 17])

    y_p = dout("y_p", [T, D])
    kvc_p = dout("kvc_p", [2, T, 256])
    kvs_p = dout("kvs_p", [2, T, 256])
    kvw_p = dout("kvw_p", [2, 512, 256])
    pool_p = dout("pool_p", [2, 15, 256])
    gv_p = dout("gv_p", [2, 128, 256])
    y_s = dout("y_s", [NS, D])
    kvc_s = dout("kvc_s", [2, NS, 256])
    kvs_s = dout("kvs_s", [2, NS, 256])
    kvw_s = dout("kvw_s", [2, NS, 512, 256])
    pool_s = dout("pool_s", [2, NS, 15, 256])
    gv_s = dout("gv_s", [2, NS, 256])
    xp1 = nc.dram_tensor("xp1_scratch", [T, D], F32, kind="Internal").ap()
    bounce = nc.dram_tensor("bounce_scratch", [8, 4096], F32, kind="Internal").ap()
    xs1 = nc.dram_tensor("xs1_scratch", [NS, D], F32, kind="Internal").ap()

    with ExitStack() as es:
        S = Sched(nc, es)
        uid = [0]

        def sb(shape, dt, name, stack=None):
            uid[0] += 1
            nm = f"{name}_{uid[0]}"
            t = (stack or es).enter_context(nc.sbuf_tensor(nm, list(shape), dt))
            return TB(t, nm)

        def ps(shape, dt, name):
            uid[0] += 1
            nm = f"{name}_{uid[0]}"
            t = es.enter_context(nc.psum_tensor(nm, list(shape), dt))
            tb = TB(t, nm)
            tb.b.excl = True
            return tb

        def load(dst_ap, src_ap, bufs, stream=None, reads=(), eng="sp"):
            stream = "L_" + bufs[0].name
            S.op(eng, lambda e: e.dma_start(out=dst_ap, in_=src_ap), reads=reads, writes=bufs, dma=stream)

        def store(dst_ap, src_ap, bufs, stream, writes=(), eng="pool"):
            S.op(eng, lambda e: e.dma_start(out=dst_ap, in_=src_ap), reads=bufs, writes=writes, dma=stream)

        TP = ps([128, 1024], BF16, "TP")
        PJ = [ps([128, 512], F32, "PJ0"), ps([128, 512], F32, "PJ1")]
        ST = [ps([128, 512], F32, "ST0"), ps([128, 512], F32, "ST1")]
        OA = ps([128, 512], F32, "OA")
        MK = ps([128, 4, 128], F32, "MK")
        MKb = [MK.b for j in range(4)]
        MS = ps([128, 512], F32, "MS")
        TPF = TP

        ident = sb([128, 128], BF16, "ident")
        identf = sb([128, 128], F32, "identf")
        trile = sb([128, 128], BF16, "trile")
        trigt = sb([128, 128], BF16, "trigt")
        alb = sb([128, 512], F32, "alb")
        iota64 = sb([128, 64], F32, "iota64")
        cmisc = sb([128, 8], F32, "cmisc")
        force0 = sb([128, 64], F32, "force0")
        trilT = sb([128, 128], F32, "trilT")
        for tb, src in [(ident, c_ident), (identf, c_identf), (trile, c_trile), (trigt, c_trigt), (alb, c_alb),
                        (iota64, c_iota), (cmisc, c_misc), (force0, c_force0), (trilT, c_trilT)]:
            load(tb[:], src, [tb.b], "const")

        WIN = sb([128, 8, DIN], BF16, "WIN")
        WOUT = sb([128, 8, D], BF16, "WOUT")
        GP = sb([128, D], F32, "GP")
        NPRE = sb([128, 8], F32, "NPRE")
        PETOK = sb([128, 256], F32, "PETOK")
        W1BD = sb([128, 2, 128], BF16, "W1BD")
        W2KV = sb([128, 4, 64], BF16, "W2KV")
        WPOOL = sb([64, 4, 64], BF16, "WPOOL")
        PSC = sb([128, 256], F32, "PSC")
        GNR = sb([128, 256], F32, "GNR")
        WT = sb([128, 4, 128], BF16, "WT")
        BST = sb([128, 4], F32, "BST")
        WS00 = sb([128, 4], F32, "WS00")
        BS0 = sb([128, 4], F32, "BS0")
        WSTG0 = sb([128, 1032], F32, "WSTG0")
        WSTG = [WSTG0, WSTG0]
        SMALLF = WSTG0

        def load_weights(l):
            load(NPRE[:], npre_t[l], [NPRE.b], "wsmall")
            load(GP[:], gpost_rep[l], [GP.b], "wsmall")
            load(PETOK[:], pe_tok[l], [PETOK.b], "wsmall")
            load(PSC[:], pscale_rep[l], [PSC.b], "wsmall")
            load(GNR[:], gnorm_rep[l], [GNR.b], "wsmall")
            load(BST[:], bsT[l], [BST.b], "wsmall")
            load(WS00[:], ws00_rep[l], [WS00.b], "wsmall")
            load(BS0[:], bs0_rep[l], [BS0.b], "wsmall")
            load(SMALLF[:, 0:256], w1bd[l].rearrange("p r e -> p (r e)"), [SMALLF.b], "wsmall2")
            S.op("dve", lambda e: e.tensor_copy(out=W1BD[:].rearrange("p r e -> p (r e)"), in_=SMALLF[:, 0:256]),
                 reads=[SMALLF.b], writes=[W1BD.b])
            load(SMALLF[:, 0:256], w2kv[l].rearrange("p r e -> p (r e)"), [SMALLF.b], "wsmall2")
            S.op("dve", lambda e: e.tensor_scalar(out=W2KV[:].rearrange("p r e -> p (r e)"), in0=SMALLF[:, 0:256],
                                                  scalar1=1.0 / 128.0, scalar2=None, op0=ALU.mult),
                 reads=[SMALLF.b], writes=[W2KV.b])
            load(SMALLF[0:64, 0:256], wpool[l].rearrange("p r e -> p (r e)"), [SMALLF.b], "wsmall2")
            S.op("dve", lambda e: e.tensor_copy(out=WPOOL[:].rearrange("p r e -> p (r e)"), in_=SMALLF[0:64, 0:256]),
                 reads=[SMALLF.b], writes=[WPOOL.b])
            load(SMALLF[:, 0:512], wsT[l].rearrange("p g i -> p (g i)"), [SMALLF.b], "wsmall2")
            S.op("dve", lambda e: e.tensor_tensor(out=WT[:], in0=SMALLF[:, 0:512].rearrange("p (g i) -> p g i", g=4),
                                                  in1=trilT[:].unsqueeze(1).to_broadcast([128, 4, 128]), op=ALU.mult),
                 reads=[SMALLF.b, trilT.b], writes=[WT.b])
            k = 0
            for c in range(8):
                for (c0, c1) in [(0, 1032), (1032, 2064), (2064, 3096)]:
                    stg = WSTG[k % 2]
                    k += 1
                    load(stg[:, 0:c1 - c0], w_in[l, c * 128:(c + 1) * 128, c0:c1], [stg.b], "wstg" + str(k % 2))
                    if c0 == 0:
                        S.op("dve", lambda e, stg=stg, c=c: e.tensor_scalar(
                            out=WIN[:, c, 0:512], in0=stg[:, 0:512], scalar1=NPRE[:, c:c + 1], scalar2=0.125,
                            op0=ALU.mult, op1=ALU.mult), reads=[stg.b, NPRE.b], writes=[WIN.b])
                        S.op("dve", lambda e, stg=stg, c=c: e.tensor_scalar(
                            out=WIN[:, c, 512:1032], in0=stg[:, 512:1032], scalar1=NPRE[:, c:c + 1], scalar2=None,
                            op0=ALU.mult), reads=[stg.b, NPRE.b], writes=[WIN.b])
                    else:
                        S.op("dve", lambda e, stg=stg, c=c, c0=c0, c1=c1: e.tensor_scalar(
                            out=WIN[:, c, c0:c1], in0=stg[:, 0:c1 - c0], scalar1=NPRE[:, c:c + 1], scalar2=None,
                            op0=ALU.mult), reads=[stg.b, NPRE.b], writes=[WIN.b])
                stg = WSTG[k % 2]
                k += 1
                load(stg[:, 0:1024], w_out[l, c * 128:(c + 1) * 128, :], [stg.b], "wstg" + str(k % 2))
                S.op("act", lambda e, stg=stg, c=c: e.mul(out=WOUT[:, c, :], in_=stg[:, 0:1024], mul=0.5),
                     reads=[stg.b], writes=[WOUT.b])

        X = [sb([128, D], F32, "X0"), None, None]
        XS = sb([128, D], BF16, "XS")
        XT = sb([128, 8, 128], BF16, "XT")
        SMALL = sb([128, 64], F32, "SMALL")
        QB = sb([128, 512], BF16, "QB")
        KVF = sb([128, 768], F32, "KVF")
        GL = sb([128, 24], F32, "GL")
        GTL = [sb([128, 24], F32, "GT0"), None]
        SZL = [sb([128, D], F32, "SZ0"), None]
        PIN = sb([128, 256], F32, "PIN")
        UV = sb([128, 512], F32, "UV")
        MIXL = [sb([128, D], BF16, "MIX0"), None]
        MT = sb([128, 8, 128], BF16, "MT")
        Y = sb([128, D], F32, "Y")
        YAL = [sb([128, 512], F32, "YA0"), None]
        VN = sb([128, 256], F32, "VN")
        VNB = sb([128, 256], BF16, "VNB")
        YC = sb([128, 256], F32, "YC")
        TMPB = sb([128, 256], F32, "TMPB")
        SQ = TMPB
        JUNK = sb([128, D], BF16, "JUNK")

        def run(gen):
            n = 0
            for _ in gen:
                n += 1
            return n

        def run2(a, na, b, nb):
            live = [a, b]
            cnt = {id(a): 0, id(b): 0}
            while live:
                for g_ in list(live):
                    try:
                        next(g_)
                        cnt[id(g_)] += 1
                    except StopIteration:
                        live.remove(g_)
            return cnt[id(a)], cnt[id(b)]

        def rstd_from_ss(ss_ap, out_ap, n, M, bufs_r, bufs_w, scratch):
            S.op("dve", lambda e: e.tensor_scalar(out=scratch, in0=ss_ap, scalar1=1.0 / n, scalar2=EPS,
                                                  op0=ALU.mult, op1=ALU.add), reads=bufs_r, writes=bufs_w)
            w = scratch.shape[1]
            S.op("pool", lambda e: e.tensor_tensor(out=out_ap, in0=scratch, in1=cmisc[0:M, 2:3].to_broadcast([M, w]), op=ALU.pow),
                 reads=list(bufs_w) + [cmisc.b], writes=bufs_w)

        def project(M, xt, par=0):
            GT = GTL[par]
            SZ = SZL[par]
            S.op("act", lambda e: e.activation(out=JUNK[0:M, :], in_=xt[0:M, :], func=AF.Square),
                 reads=[xt.b], writes=[JUNK.b])
            S.op("dve", lambda e: e.tensor_reduce(out=SMALL[0:M, 0:1], in_=JUNK[0:M, :], axis=AX.X, op=ALU.add),
                 reads=[JUNK.b], writes=[SMALL.b])
            rstd_from_ss(SMALL[0:M, 0:1], SMALL[0:M, 2:3], float(D), M, [SMALL.b], [SMALL.b], SMALL[0:M, 1:2])
            S.op("dve", lambda e: e.tensor_scalar(out=XS[0:M, :], in0=xt[0:M, :], scalar1=SMALL[0:M, 2:3],
                                                  scalar2=None, op0=ALU.mult), reads=[xt.b, SMALL.b], writes=[XS.b])
            yield
            for c in range(8):
                S.op("pe", lambda e, c=c: e.transpose(out=TP[:, c * 128:c * 128 + M], in_=XS[0:M, c * 128:(c + 1) * 128],
                                                      identity=ident[0:M, 0:M]),
                     reads=[XS.b, ident.b], writes=[TP.b])
            S.op("act", lambda e: e.copy(out=XT[:, :, 0:M], in_=TP[:].rearrange("p (c m) -> p c m", c=8)[:, :, 0:M]),
                 reads=[TP.b], writes=[XT.b])
            yield
            for gi, (c0, c1) in enumerate(PG):
                if gi > 0:
                    yield
                pj = PJ[gi % 2]
                w = c1 - c0
                for c in range(8):
                    S.op("pe", lambda e, c=c, pj=pj, c0=c0, c1=c1, w=w: e.matmul(
                        pj[0:M, 0:w], lhsT=XT[:, c, 0:M], rhs=WIN[:, c, c0:c1], start=(c == 0), stop=(c == 7)),
                         reads=[XT.b, WIN.b], writes=[pj.b])
                if gi == 0:
                    S.op("act", lambda e, pj=pj: e.copy(out=QB[0:M, :], in_=pj[0:M, 0:512]), reads=[pj.b], writes=[QB.b])
                elif gi == 1:
                    S.op("dve", lambda e, pj=pj: e.tensor_copy(out=KVF[0:M, 0:512], in_=pj[0:M, 0:512]),
                         reads=[pj.b], writes=[KVF.b])
                elif gi == 2:
                    S.op("dve", lambda e, pj=pj: e.tensor_copy(out=KVF[0:M, 512:768], in_=pj[0:M, 0:256]),
                         reads=[pj.b], writes=[KVF.b])
                    S.op("dve", lambda e, pj=pj: e.tensor_copy(out=GL[0:M, :], in_=pj[0:M, 256:280]),
                         reads=[pj.b], writes=[GL.b])
                elif gi == 3:
                    S.op("act", lambda e, pj=pj: e.activation(out=SZ[0:M, 0:512], in_=pj[0:M, 0:512], func=AF.Tanh, scale=0.5),
                         reads=[pj.b], writes=[SZ.b])
                    S.op("dve", lambda e, pj=pj: e.scalar_tensor_tensor(out=SZ[0:M, 0:512], in0=SZ[0:M, 0:512], scalar=1.0,
                                                                        in1=pj[0:M, 0:512], op0=ALU.add, op1=ALU.mult),
                         reads=[pj.b, SZ.b], writes=[SZ.b])
                elif gi == 4:
                    S.op("dve", lambda e, pj=pj: e.tensor_copy(out=PIN[0:M, :], in_=pj[0:M, 0:256]),
                         reads=[pj.b], writes=[PIN.b])
                    S.op("act", lambda e, pj=pj: e.activation(out=SZ[0:M, 512:768], in_=pj[0:M, 256:512], func=AF.Tanh, scale=0.5),
                         reads=[pj.b], writes=[SZ.b])
                    S.op("dve", lambda e, pj=pj: e.scalar_tensor_tensor(out=SZ[0:M, 512:768], in0=SZ[0:M, 512:768], scalar=1.0,
                                                                        in1=pj[0:M, 256:512], op0=ALU.add, op1=ALU.mult),
                         reads=[pj.b, SZ.b], writes=[SZ.b])
                elif gi == 5:
                    S.op("dve", lambda e, pj=pj: e.tensor_copy(out=UV[0:M, :], in_=pj[0:M, 0:512]),
                         reads=[pj.b], writes=[UV.b])
                else:
                    S.op("act", lambda e, pj=pj: e.activation(out=SZ[0:M, 768:1024], in_=pj[0:M, 0:256], func=AF.Tanh, scale=0.5),
                         reads=[pj.b], writes=[SZ.b])
                    S.op("dve", lambda e, pj=pj: e.scalar_tensor_tensor(out=SZ[0:M, 768:1024], in0=SZ[0:M, 768:1024], scalar=1.0,
                                                                        in1=pj[0:M, 0:256], op0=ALU.add, op1=ALU.mult),
                         reads=[pj.b, SZ.b], writes=[SZ.b])
            yield
            S.op("act", lambda e: e.activation(out=GT[0:M, :], in_=GL[0:M, :], func=AF.Exp, scale=-1.0),
                 reads=[GL.b], writes=[GT.b])
            S.op("dve", lambda e: e.tensor_scalar(out=GT[0:M, :], in0=GT[0:M, :], scalar1=1.0, scalar2=None, op0=ALU.add),
                 reads=[GT.b], writes=[GT.b])
            S.op("dve", lambda e: e.reciprocal(out=GT[0:M, :], in_=GT[0:M, :]), reads=[GT.b], writes=[GT.b])

        def gmlp_norm(M):
            v = UV[0:M, 256:512]
            S.op("dve", lambda e: e.tensor_tensor(out=SQ[0:M, :], in0=v, in1=v, op=ALU.mult), reads=[UV.b], writes=[SQ.b])
            S.op("dve", lambda e: e.tensor_reduce(out=SMALL[0:M, 8:12], in_=SQ[0:M, :].rearrange("p (g c) -> p g c", g=4),
                                                  axis=AX.X, op=ALU.add), reads=[SQ.b], writes=[SMALL.b])
            rstd_from_ss(SMALL[0:M, 8:12], SMALL[0:M, 16:20], 64.0, M, [SMALL.b], [SMALL.b], SMALL[0:M, 12:16])
            S.op("dve", lambda e: e.tensor_tensor(out=VN[0:M, :].rearrange("p (g c) -> p g c", g=4),
                                                  in0=v.rearrange("p (g c) -> p g c", g=4),
                                                  in1=SMALL[0:M, 16:20].unsqueeze(2).to_broadcast([M, 4, 64]), op=ALU.mult),
                 reads=[UV.b, SMALL.b], writes=[VN.b])
            S.op("dve", lambda e: e.tensor_tensor(out=VN[0:M, :], in0=VN[0:M, :], in1=GNR[0:M, :], op=ALU.mult),
                 reads=[VN.b, GNR.b], writes=[VN.b])

        def merge_out(M, xt, dst_ap, dst_bufs, stream, par=0):
            MIX = MIXL[par]
            for c in range(8):
                S.op("pe", lambda e, c=c: e.transpose(out=TP[:, c * 128:c * 128 + M], in_=MIX[0:M, c * 128:(c + 1) * 128],
                                                      identity=ident[0:M, 0:M]),
                     reads=[MIX.b, ident.b], writes=[TP.b])
            S.op("act", lambda e: e.copy(out=MT[:, :, 0:M], in_=TP[:].rearrange("p (c m) -> p c m", c=8)[:, :, 0:M]),
                 reads=[TP.b], writes=[MT.b])
            yield
            for hf in range(2):
                for c in range(8):
                    S.op("pe", lambda e, c=c, hf=hf: e.matmul(PJ[hf][0:M, :], lhsT=MT[:, c, 0:M],
                                                               rhs=WOUT[:, c, hf * 512:(hf + 1) * 512],
                                                               start=(c == 0), stop=(c == 7)),
                         reads=[MT.b, WOUT.b], writes=[PJ[hf].b])
            for hf in range(2):
                S.op("act", lambda e, hf=hf: e.activation(out=JUNK[0:M, hf * 512:(hf + 1) * 512], in_=PJ[hf][0:M, :],
                                                          func=AF.Square), reads=[PJ[hf].b], writes=[JUNK.b])
            S.op("dve", lambda e: e.tensor_reduce(out=SMALL[0:M, 24:25], in_=JUNK[0:M, :], axis=AX.X, op=ALU.add),
                 reads=[JUNK.b], writes=[SMALL.b])
            rstd_from_ss(SMALL[0:M, 24:25], SMALL[0:M, 26:27], float(D), M, [SMALL.b], [SMALL.b], SMALL[0:M, 25:26])
            for hf in range(2):
                sl = slice(hf * 512, (hf + 1) * 512)
                S.op("dve", lambda e, hf=hf, sl=sl: e.scalar_tensor_tensor(
                    out=Y[0:M, sl], in0=PJ[hf][0:M, :], scalar=SMALL[0:M, 26:27], in1=GP[0:M, sl],
                    op0=ALU.mult, op1=ALU.mult), reads=[PJ[hf].b, SMALL.b, GP.b], writes=[Y.b])
            S.op("pool", lambda e: e.tensor_tensor(out=Y[0:M, :], in0=Y[0:M, :], in1=xt[0:M, :], op=ALU.add),
                 reads=[Y.b, xt.b], writes=[Y.b])
            store(dst_ap, Y[0:M, :], [Y.b], stream, writes=dst_bufs)
            yield

        XP1b = [Buf(f"xp1_{i}") for i in range(NT)]

        def prompt_layer(l, pstack):
            X[1] = sb([128, D], F32, "X1", pstack)
            X[2] = sb([128, D], F32, "X2", pstack)
            JUNK2 = sb([128, D], BF16, "JUNK2", pstack)
            GTL[1] = sb([128, 24], F32, "GT1", pstack)
            SZL[1] = sb([128, D], F32, "SZ1", pstack)
            MIXL[1] = sb([128, D], BF16, "MIX1", pstack)
            YAL[1] = sb([128, 512], F32, "YA1", pstack)
            KT = sb([100, 4, T], BF16, "KT", pstack)
            KTb = [Buf(f"KT{i}") for i in range(NT)]
            KTaug = Buf("KTaug")
            VS = sb([128, NT, 2, 65], BF16, "VS", pstack)
            VW = sb([128, 8, 2, 65], BF16, "VW", pstack)
            VSb = [Buf(f"VS{i}") for i in range(NT)]
            VWb = [Buf(f"VW{i}") for i in range(8)]
            NMP = sb([128, 2, 2, 96], BF16, "NMP", pstack)
            POOLB = sb([128, 3, 4, 128], BF16, "POOLB", pstack)
            HBT = sb([128, 2, 64], BF16, "HBT", pstack)
            KCc = sb([64, 2, 64], BF16, "KCc", pstack)
            VCc = sb([64, 2, 64], BF16, "VCc", pstack)
            QA = [sb([100, 2, 2, 512], BF16, "QA0", pstack), sb([100, 2, 2, 512], BF16, "QA1", pstack)]
            QAaug = [Buf("QAaug0"), Buf("QAaug1")]
            KBF = sb([128, 2, 128], BF16, "KBF", pstack)
            KCP = sb([128, 256], BF16, "KCP", pstack)
            KCT = sb([128, 2, 128], BF16, "KCT", pstack)
            HID = sb([128, 2, 128], F32, "HID", pstack)
            SC = sb([128, 8, 64], F32, "SC", pstack)
            PC = SC
            PCB = sb([128, 8, 64], BF16, "PCB", pstack)
            PCT = sb([64, 8, 128], BF16, "PCT", pstack)
            IMP = sb([128, 2, 64], F32, "IMP", pstack)
            SCR = sb([128, 2, 64], F32, "SCR", pstack)
            WK1 = sb([128, 64], F32, "WK1", pstack)
            WK2 = sb([128, 64], F32, "WK2", pstack)
            M8 = sb([128, 8], F32, "M8", pstack)
            SEL = sb([128, 2, 64], BF16, "SEL", pstack)
            SELTL = [sb([64, 2, 128], BF16, "SELT0", pstack), sb([64, 2, 128], BF16, "SELT1", pstack)]
            VIS = sb([128, 64], F32, "VIS", pstack)
            NVIS = sb([128, 64], F32, "NVIS", pstack)
            STT = sb([128, 64], F32, "STT", pstack)
            ADDT = sb([128, 64], F32, "ADDT", pstack)
            PT = [sb([128, 512], BF16, f"PT{j}", pstack) for j in range(3)]
            MKD = sb([128, 128], BF16, "MKD", pstack)
            OSL = [sb([128, 4, 65], F32, "OS0", pstack)]
            CSL = [sb([128, 8], F32, "CS0", pstack)]
            oac = [0]
            PINB = [sb([128, 256], BF16, "PINB0", pstack), sb([128, 256], BF16, "PINB1", pstack)]
            DT = sb([64, 4, 128], BF16, "DT", pstack)

            load(POOLB[:], c_poolb, [POOLB.b], "const")
            for j in range(4):
                load(KT[96:100, j, :], c_kaug, [KTaug], "const")
            load(KT[64:96, 0, :], c_kmask, [KTaug], "const")
            load(KT[64:96, 1, :], c_kmask, [KTaug], "const")
            S.op("dve", lambda e: e.memset(KT[64:96, 2:4, :], 0.0), writes=[KTaug])
            S.op("dve", lambda e: e.memset(NMP[:], 0.0), writes=[NMP.b])
            for qq in QA:
                S.op("dve", lambda e, qq=qq: e.memset(qq[64:96, :, :, :], 0.0), writes=[qq.b])
            S.op("dve", lambda e: e.memset(VS[:], 1.0), writes=VSb)
            S.op("dve", lambda e: e.memset(VW[:], 1.0), writes=VWb)
            S.op("dve", lambda e: e.memset(HBT[:], 0.0), writes=[HBT.b])
            ptc = [0]
            mkc = [0]
            stc = [0]

            def attend(i, kv, branch, qa, qab):
                par = i % 2
                GT, YA, SELT = GTL[par], YAL[par], SELTL[par]
                og = 0
                OS, CS = OSL[og], CSL[og]
                oa_ap = OA[:, :] if og == 0 else MK[:].rearrange("p a b -> p (a b)")
                oa_b = OA.b if og == 0 else MK.b
                if branch == 0:
                    kts = list(range(0, i + 1))
                    slot, Vt, Vb = kv, VS, VSb
                    vsl = lambda kt: kt
                else:
                    kts = list(range(max(0, i - 4), i + 1))
                    slot, Vt, Vb = 2 + kv, VW, VWb
                    vsl = lambda kt: kt % 8
                n = len(kts)
                pts = [None] * n

                def s1a(idx):
                    kt = kts[idx]
                    st = ST[stc[0] % 2]
                    stc[0] += 1
                    hf = (kt // 16) if branch == 0 else 0
                    S.op("pe", lambda e, st=st, kt=kt, hf=hf: e.matmul(st[:, :], lhsT=KT[0:100, slot, kt * 128:(kt + 1) * 128],
                                                                        rhs=qa[0:100, kv, hf, :], start=True, stop=True),
                         reads=[KTb[kt], KTaug, qa.b, qab], writes=[st.b])
                    pt = PT[ptc[0] % len(PT)]
                    ptc[0] += 1
                    pts[idx] = pt
                    S.op("act", lambda e, st=st, pt=pt: e.activation(out=pt[:], in_=st[:, :], func=AF.Exp),
                         reads=[st.b], writes=[pt.b])

                def s1b(idx):
                    kt = kts[idx]
                    pt = pts[idx]
                    mask_ap = None
                    mreads = []
                    if branch == 0:
                        if kt == i:
                            mask_ap = trile[:]
                            mreads = [trile.b]
                    else:
                        if kt == i:
                            mask_ap = trile[:]
                            mreads = [trile.b]
                        elif kt == i - 4:
                            mask_ap = trigt[:]
                            mreads = [trigt.b]
                    if mask_ap is not None:
                        S.op("dve", lambda e, pt=pt, mask_ap=mask_ap: e.tensor_tensor(
                            out=pt[:].rearrange("p (g q) -> p g q", g=4), in0=pt[:].rearrange("p (g q) -> p g q", g=4),
                            in1=mask_ap.unsqueeze(1).to_broadcast([128, 4, 128]), op=ALU.mult),
                             reads=[pt.b] + mreads, writes=[pt.b])

                def s2(idx):
                    kt = kts[idx]
                    pt = pts[idx]
                    for g in range(4):
                        S.op("pe", lambda e, pt=pt, g=g, kt=kt, idx=idx: e.matmul(
                            oa_ap[:, g * 65:(g + 1) * 65], lhsT=pt[:, g * 128:(g + 1) * 128], rhs=Vt[:, vsl(kt), kv, :],
                            start=(idx == 0 and g == 0), stop=(idx == n - 1 and g == 3)),
                             reads=[pt.b, Vb[vsl(kt)]], writes=[oa_b])

                s1a(0)
                if n > 1:
                    s1a(1)
                s1b(0)
                for idx in range(n):
                    if idx + 2 < n:
                        s1a(idx + 2)
                    if idx + 1 < n:
                        s1b(idx + 1)
                    s2(idx)
                    yield
                S.op("act", lambda e: e.copy(out=OS[:].rearrange("p g d -> p (g d)"), in_=oa_ap[:, 0:260]),
                     reads=[oa_b], writes=[OS.b])
                gcol = 1 + branch
                S.op("dve", lambda e: e.reciprocal(out=CS[:, 0:4], in_=OS[:, :, 64]), reads=[OS.b], writes=[CS.b])
                S.op("dve", lambda e: e.tensor_tensor(
                    out=CS[:, 4:8], in0=CS[:, 0:4],
                    in1=GT[:, :].rearrange("p (h t) -> p h t", t=3)[:, kv * 4:(kv + 1) * 4, gcol], op=ALU.mult),
                     reads=[CS.b, GT.b], writes=[CS.b])
                for g in range(4):
                    h = kv * 4 + g
                    S.op("dve", lambda e, g=g, h=h: e.scalar_tensor_tensor(
                        out=YA[:, h * 64:(h + 1) * 64], in0=OS[:, g, 0:64], scalar=CS[:, 4 + g:5 + g],
                        in1=YA[:, h * 64:(h + 1) * 64], op0=ALU.mult, op1=ALU.add),
                         reads=[OS.b, CS.b, YA.b], writes=[YA.b])

            def front(i):
                par = i % 2
                GT, SZ, MIX, YA, SELT = GTL[par], SZL[par], MIXL[par], YAL[par], SELTL[par]
                xt = X[i % 3]
                src = xp if l == 0 else xp1
                load(xt[:], src[i * 128:(i + 1) * 128, :], [xt.b], f"x{i % 2}", reads=([XP1b[i]] if l == 1 else []))
                yield from project(128, xt, par)
                yield
                store(kvc_p[l, i * 128:(i + 1) * 128, :], KVF[:, 0:256], [KVF.b], "o_kvf")
                store(kvs_p[l, i * 128:(i + 1) * 128, :], KVF[:, 256:512], [KVF.b], "o_kvf")
                if i >= NT - 4:
                    j = i - (NT - 4)
                    store(kvw_p[l, j * 128:(j + 1) * 128, :], KVF[:, 512:768], [KVF.b], "o_kvf")
                if i == NT - 1:
                    store(pool_p[l], PIN[113:128, :], [PIN.b], "o_pin")
                yield
                S.op("pool", lambda e: e.tensor_copy(out=KBF[:, 0, :], in_=KVF[:, 256:384]), reads=[KVF.b], writes=[KBF.b])
                S.op("pool", lambda e: e.tensor_copy(out=KBF[:, 1, :], in_=KVF[:, 512:640]), reads=[KVF.b], writes=[KBF.b])
                S.op("pool", lambda e, i=i: e.tensor_copy(out=VS[:, i, :, 0:64],
                                                           in_=KVF[:, 384:512].rearrange("p (k d) -> p k d", k=2)),
                     reads=[KVF.b], writes=[VSb[i]])
                S.op("pool", lambda e, i=i: e.tensor_copy(out=VW[:, i % 8, :, 0:64],
                                                           in_=KVF[:, 640:768].rearrange("p (k d) -> p k d", k=2)),
                     reads=[KVF.b], writes=[VWb[i % 8]])
                for j in range(4):
                    S.op("pe", lambda e, j=j: e.transpose(out=TP[0:64, j * 128:(j + 1) * 128],
                                                          in_=KBF[:, j // 2, (j % 2) * 64:(j % 2) * 64 + 64], identity=ident[:]),
                         reads=[KBF.b, ident.b], writes=[TP.b])
                S.op("act", lambda e, i=i: e.copy(out=KT[0:64, :, i * 128:(i + 1) * 128],
                                                  in_=TP[0:64, 0:512].rearrange("p (j t) -> p j t", j=4)),
                     reads=[TP.b], writes=[KTb[i]])
                yield
                qa = QA[i % 2]
                qab = QAaug[i % 2]
                load(qa[96:100, :, :, :].rearrange("p k h n -> p (k h n)"), c_qaug[i], [qab], f"qaug{i % 2}", reads=[qa.b])
                for h in range(8):
                    S.op("pe", lambda e, h=h: e.transpose(out=TP[0:64, h * 128:(h + 1) * 128], in_=QB[:, h * 64:(h + 1) * 64],
                                                          identity=ident[:]), reads=[QB.b, ident.b], writes=[TP.b])
                S.op("act", lambda e, qa=qa: e.copy(
                    out=qa[0:64, :, :, :], in_=TP[0:64, :].rearrange("p (k n) -> p k n", k=2).unsqueeze(2).to_broadcast([64, 2, 2, 512])),
                     reads=[TP.b, qab], writes=[qa.b])
                yield
                S.op("dve", lambda e: e.tensor_tensor(out=KCP[:], in0=KVF[:, 0:256], in1=PETOK[:], op=ALU.add),
                     reads=[KVF.b, PETOK.b], writes=[KCP.b])
                for r in range(2):
                    S.op("pe", lambda e, r=r: e.transpose(out=TP[:, r * 128:(r + 1) * 128], in_=KCP[:, r * 128:(r + 1) * 128],
                                                          identity=ident[:]), reads=[KCP.b, ident.b], writes=[TP.b])
                S.op("act", lambda e: e.copy(out=KCT[:].rearrange("p r t -> p (r t)"), in_=TP[:, 0:256]),
                     reads=[TP.b], writes=[KCT.b])
                for r in range(2):
                    S.op("pe", lambda e, r=r: e.matmul(MS[:, r * 128:(r + 1) * 128], lhsT=W1BD[:, r, :], rhs=KCT[:, r, :],
                                                       start=True, stop=True), reads=[W1BD.b, KCT.b], writes=[MS.b])
                S.op("act", lambda e: e.activation(out=HID[:].rearrange("p r t -> p (r t)"), in_=MS[:, 0:256], func=AF.Tanh, scale=0.5),
                     reads=[MS.b], writes=[HID.b])
                S.op("dve", lambda e: e.scalar_tensor_tensor(out=HID[:].rearrange("p r t -> p (r t)"),
                                                             in0=HID[:].rearrange("p r t -> p (r t)"), scalar=1.0, in1=MS[:, 0:256],
                                                             op0=ALU.add, op1=ALU.mult), reads=[MS.b, HID.b], writes=[HID.b])
                S.op("dve", lambda e: e.tensor_reduce(out=SMALL[:, 34:38].rearrange("p (r b) -> p r b", r=2),
                                                      in_=HID[:].rearrange("p r (b c) -> p r b c", b=2),
                                                      axis=AX.X, op=ALU.add), reads=[HID.b], writes=[SMALL.b])
                S.op("dve", lambda e, i=i: e.tensor_copy(out=HBT[:, :, 2 * i:2 * i + 2],
                                                         in_=SMALL[:, 34:38].rearrange("p (r b) -> p r b", r=2)),
                     reads=[SMALL.b], writes=[HBT.b])
                for kv in range(2):
                    S.op("pe", lambda e, kv=kv: e.matmul(MS[0:64, kv * 64:(kv + 1) * 64], lhsT=W2KV[:, kv, :], rhs=HBT[:, 0, :],
                                                         start=True, stop=True), reads=[W2KV.b, HBT.b], writes=[MS.b])
                    S.op("pe", lambda e, kv=kv: e.matmul(MS[0:64, 128 + kv * 64:128 + (kv + 1) * 64], lhsT=HBT[:, 1, :],
                                                         rhs=W2KV[:, 2 + kv, :], start=True, stop=True),
                         reads=[W2KV.b, HBT.b], writes=[MS.b])
                S.op("act", lambda e: e.copy(out=KCc[:].rearrange("p k n -> p (k n)"), in_=MS[0:64, 0:128]),
                     reads=[MS.b], writes=[KCc.b])
                S.op("act", lambda e: e.copy(out=VCc[:].rearrange("p k n -> p (k n)"), in_=MS[0:64, 128:256]),
                     reads=[MS.b], writes=[VCc.b])
                yield
                for h in range(8):
                    kv, g = h // 4, h % 4
                    S.op("pe", lambda e, h=h, kv=kv, g=g, qa=qa: e.matmul(MS[:, h * 64:(h + 1) * 64],
                                                                   lhsT=qa[0:64, kv, 0, g * 128:(g + 1) * 128],
                                                                   rhs=KCc[:, kv, :], start=True, stop=True),
                         reads=[qa.b, KCc.b], writes=[MS.b])
                S.op("dve", lambda e, i=i: e.tensor_scalar(out=SMALL[:, 32:33], in0=cmisc[:, 0:1], scalar1=float(2 * i),
                                                           scalar2=None, op0=ALU.add), reads=[cmisc.b], writes=[SMALL.b])
                S.op("dve", lambda e: e.tensor_scalar(out=VIS[:], in0=iota64[:], scalar1=SMALL[:, 32:33], scalar2=None,
                                                      op0=ALU.is_le), reads=[iota64.b, SMALL.b], writes=[VIS.b])
                S.op("dve", lambda e: e.tensor_scalar(out=NVIS[:], in0=VIS[:], scalar1=-1.0, scalar2=-NEG,
                                                      op0=ALU.add, op1=ALU.mult), reads=[VIS.b], writes=[NVIS.b])
                S.op("dve", lambda e: e.tensor_tensor(out=SC[:].rearrange("p h n -> p (h n)"), in0=MS[:, :], in1=alb[:], op=ALU.add),
                     reads=[MS.b, alb.b], writes=[SC.b])
                yield
                S.op("dve", lambda e: e.tensor_tensor(out=SC[:], in0=SC[:], in1=NVIS[:].unsqueeze(1).to_broadcast([128, 8, 64]),
                                                      op=ALU.add), reads=[SC.b, NVIS.b], writes=[SC.b])
                yield
                S.op("dve", lambda e: e.tensor_reduce(out=SMALL[:, 40:48], in_=SC[:], axis=AX.X, op=ALU.max),
                     reads=[SC.b], writes=[SMALL.b])
                yield
                S.op("dve", lambda e: e.tensor_tensor(out=SC[:], in0=SC[:],
                                                      in1=SMALL[:, 40:48].unsqueeze(2).to_broadcast([128, 8, 64]),
                                                      op=ALU.subtract), reads=[SC.b, SMALL.b], writes=[SC.b])
                yield
                S.op("act", lambda e: e.activation(out=PC[:].rearrange("p h n -> p (h n)"),
                                                   in_=SC[:].rearrange("p h n -> p (h n)"), func=AF.Exp),
                     reads=[SC.b], writes=[PC.b])
                yield
                S.op("dve", lambda e: e.tensor_tensor(out=PC[:], in0=PC[:], in1=VIS[:].unsqueeze(1).to_broadcast([128, 8, 64]),
                                                      op=ALU.mult), reads=[PC.b, VIS.b], writes=[PC.b])
                yield
                S.op("dve", lambda e: e.tensor_reduce(out=SMALL[:, 48:56], in_=PC[:], axis=AX.X, op=ALU.add),
                     reads=[PC.b], writes=[SMALL.b])
                yield
                S.op("dve", lambda e: e.tensor_scalar(out=SMALL[:, 48:56], in0=SMALL[:, 48:56], scalar1=1e-30, scalar2=None,
                                                      op0=ALU.max), reads=[SMALL.b], writes=[SMALL.b])
                yield
                S.op("dve", lambda e: e.reciprocal(out=SMALL[:, 56:64], in_=SMALL[:, 48:56]), reads=[SMALL.b], writes=[SMALL.b])
                yield
                S.op("dve", lambda e: e.tensor_tensor(out=PC[:], in0=PC[:],
                                                      in1=SMALL[:, 56:64].unsqueeze(2).to_broadcast([128, 8, 64]), op=ALU.mult),
                     reads=[PC.b, SMALL.b], writes=[PC.b])
                yield
                S.op("act", lambda e: e.copy(out=PCB[:].rearrange("p h n -> p (h n)"), in_=PC[:].rearrange("p h n -> p (h n)")),
                     reads=[PC.b], writes=[PCB.b])
                yield
                S.op("dve", lambda e: e.tensor_reduce(out=IMP[:], in_=PC[:].rearrange("p (k g) n -> p k n g", k=2),
                                                      axis=AX.X, op=ALU.add), reads=[PC.b], writes=[IMP.b])
                yield
                S.op("dve", lambda e, i=i: e.tensor_scalar(out=SMALL[:, 33:34], in0=cmisc[:, 1:2], scalar1=float(2 * i),
                                                           scalar2=None, op0=ALU.add), reads=[cmisc.b], writes=[SMALL.b])
                yield
                S.op("dve", lambda e: e.tensor_scalar(out=STT[:], in0=iota64[:], scalar1=SMALL[:, 33:34], scalar2=None,
                                                      op0=ALU.is_le), reads=[iota64.b, SMALL.b], writes=[STT.b])
                yield
                S.op("dve", lambda e: e.tensor_scalar(out=ADDT[:], in0=iota64[:], scalar1=SMALL[:, 33:34], scalar2=None,
                                                      op0=ALU.is_equal), reads=[iota64.b, SMALL.b], writes=[ADDT.b])
                yield
                S.op("dve", lambda e: e.tensor_tensor(out=ADDT[:], in0=ADDT[:], in1=STT[:], op=ALU.add),
                     reads=[ADDT.b, STT.b], writes=[ADDT.b])
                yield
                S.op("dve", lambda e: e.tensor_scalar(out=ADDT[:], in0=ADDT[:], scalar1=1e4, scalar2=-1e4,
                                                      op0=ALU.mult, op1=ALU.add), reads=[ADDT.b], writes=[ADDT.b])
                yield
                S.op("dve", lambda e: e.tensor_tensor(out=ADDT[:], in0=ADDT[:], in1=force0[:], op=ALU.add),
                     reads=[ADDT.b, force0.b], writes=[ADDT.b])
                yield
                S.op("dve", lambda e: e.tensor_tensor(out=SCR[:], in0=IMP[:], in1=STT[:].unsqueeze(1).to_broadcast([128, 2, 64]),
                                                      op=ALU.mult), reads=[IMP.b, STT.b], writes=[SCR.b])
                yield
                S.op("dve", lambda e: e.tensor_tensor(out=SCR[:], in0=SCR[:], in1=ADDT[:].unsqueeze(1).to_broadcast([128, 2, 64]),
                                                      op=ALU.add), reads=[SCR.b, ADDT.b], writes=[SCR.b])
                yield
                for kv in range(2):
                    S.op("dve", lambda e, kv=kv: e.max(out=M8[:], in_=SCR[:, kv, :]), reads=[SCR.b], writes=[M8.b])
                    S.op("dve", lambda e, kv=kv: e.match_replace(out=WK1[:], in_to_replace=M8[:], in_values=SCR[:, kv, :],
                                                                 imm_value=NEG), reads=[SCR.b, M8.b], writes=[WK1.b])
                    S.op("dve", lambda e: e.max(out=M8[:], in_=WK1[:]), reads=[WK1.b], writes=[M8.b])
                    S.op("dve", lambda e: e.match_replace(out=WK2[:], in_to_replace=M8[:], in_values=WK1[:], imm_value=NEG),
                         reads=[WK1.b, M8.b], writes=[WK2.b])
                    S.op("dve", lambda e, kv=kv: e.tensor_tensor(out=SEL[:, kv, :], in0=WK2[:], in1=SCR[:, kv, :],
                                                                 op=ALU.not_equal), reads=[WK2.b, SCR.b], writes=[SEL.b])
                yield
                S.op("dve", lambda e: e.tensor_scalar(out=NMP[:, :, :, 64:96], in0=SEL[:].rearrange("p k (h n) -> p k h n", h=2),
                                                      scalar1=-1.0, scalar2=-NEG, op0=ALU.add, op1=ALU.mult),
                     reads=[SEL.b], writes=[NMP.b])
                nhalf = 2 if i >= 16 else 1
                for kv in range(2):
                    for hf in range(nhalf):
                        S.op("pe", lambda e, kv=kv, hf=hf: e.matmul(MS[0:96, (kv * 2 + hf) * 128:(kv * 2 + hf + 1) * 128],
                                                                     lhsT=NMP[:, kv, hf, :], rhs=ident[:], start=True, stop=True),
                             reads=[NMP.b, ident.b], writes=[MS.b])
                for kv in range(2):
                    for hf in range(nhalf):
                        S.op("act", lambda e, kv=kv, hf=hf, qa=qa: e.copy(
                            out=qa[64:96, kv, hf, :].rearrange("p (g q) -> p g q", g=4),
                            in_=MS[64:96, (kv * 2 + hf) * 128:(kv * 2 + hf + 1) * 128].unsqueeze(1).to_broadcast([32, 4, 128])),
                             reads=[MS.b], writes=[qa.b])
                for h in range(8):
                    S.op("pe", lambda e, h=h: e.transpose(out=TP[0:64, h * 128:(h + 1) * 128], in_=PCB[:, h, :],
                                                          identity=ident[:]), reads=[PCB.b, ident.b], writes=[TP.b])
                S.op("act", lambda e: e.copy(out=PCT[:].rearrange("p h q -> p (h q)"), in_=TP[0:64, :]),
                     reads=[TP.b], writes=[PCT.b])
                for h in range(8):
                    S.op("pe", lambda e, h=h: e.matmul(MS[:, h * 64:(h + 1) * 64], lhsT=PCT[:, h, :], rhs=VCc[:, h // 4, :],
                                                       start=True, stop=True), reads=[PCT.b, VCc.b], writes=[MS.b])
                S.op("dve", lambda e: e.tensor_tensor(
                    out=YA[:].rearrange("p (h d) -> p h d", h=8), in0=MS[:, :].rearrange("p (h d) -> p h d", h=8),
                    in1=GT[:, :].rearrange("p (h t) -> p h t", t=3)[:, :, 0:1].to_broadcast([128, 8, 64]), op=ALU.mult),
                     reads=[MS.b, GT.b], writes=[YA.b])
                pb = PINB[i % 2]
                pbp = PINB[(i + 1) % 2]
                S.op("pool", lambda e, pb=pb: e.tensor_copy(out=pb[:], in_=PIN[:]), reads=[PIN.b], writes=[pb.b])
                for g in range(4):
                    var = 2 if i == 0 else 0
                    S.op("pe", lambda e, g=g, pb=pb, var=var, i=i: e.matmul(MS[0:64, g * 128:(g + 1) * 128],
                                                                             lhsT=pb[:, g * 64:(g + 1) * 64],
                                                                             rhs=POOLB[:, var, g, :], start=True, stop=(i == 0)),
                         reads=[pb.b, POOLB.b], writes=[MS.b])
                    if i > 0:
                        S.op("pe", lambda e, g=g, pbp=pbp: e.matmul(MS[0:64, g * 128:(g + 1) * 128],
                                                                     lhsT=pbp[:, g * 64:(g + 1) * 64], rhs=POOLB[:, 1, g, :],
                                                                     start=False, stop=True),
                             reads=[pbp.b, POOLB.b], writes=[MS.b])
                S.op("act", lambda e: e.copy(out=DT[:].rearrange("p g t -> p (g t)"), in_=MS[0:64, :]), reads=[MS.b], writes=[DT.b])
                for g in range(4):
                    S.op("pe", lambda e, g=g: e.matmul(MS[:, g * 64:(g + 1) * 64], lhsT=DT[:, g, :], rhs=WPOOL[:, g, :],
                                                       start=True, stop=True), reads=[DT.b, WPOOL.b], writes=[MS.b])
                S.op("dve", lambda e: e.tensor_tensor(out=TMPB[:], in0=MS[:, 0:256], in1=PSC[:], op=ALU.mult),
                     reads=[MS.b, PSC.b], writes=[TMPB.b])
                S.op("pool", lambda e: e.tensor_tensor(out=MIX[:, 512:768], in0=TMPB[:], in1=SZ[:, 512:768], op=ALU.mult),
                     reads=[TMPB.b, SZ.b], writes=[MIX.b])
                yield
                gmlp_norm(128)
                yield
                if i == NT - 1:
                    store(gv_p[l], VN[:], [VN.b], "o_vn")
                S.op("act", lambda e: e.copy(out=VNB[:], in_=VN[:]), reads=[VN.b], writes=[VNB.b])
                for g in range(4):
                    S.op("pe", lambda e, g=g: e.matmul(MS[:, g * 64:(g + 1) * 64], lhsT=WT[:, g, :], rhs=VNB[:, g * 64:(g + 1) * 64],
                                                       start=True, stop=True), reads=[WT.b, VNB.b], writes=[MS.b])
                for g in range(4):
                    S.op("dve", lambda e, g=g: e.scalar_tensor_tensor(
                        out=YC[:, g * 64:(g + 1) * 64], in0=MS[:, g * 64:(g + 1) * 64], scalar=BST[:, g:g + 1],
                        in1=UV[:, g * 64:(g + 1) * 64], op0=ALU.add, op1=ALU.mult),
                         reads=[MS.b, BST.b, UV.b], writes=[YC.b])
                S.op("pool", lambda e: e.tensor_tensor(out=MIX[:, 768:1024], in0=YC[:], in1=SZ[:, 768:1024], op=ALU.mult),
                     reads=[YC.b, SZ.b], writes=[MIX.b])

            def back(i):
                par = i % 2
                GT, SZ, MIX, YA, SELT = GTL[par], SZL[par], MIXL[par], YAL[par], SELTL[par]
                qa = QA[i % 2]
                qab = QAaug[i % 2]
                for kv in range(2):
                    yield from attend(i, kv, 0, qa, qab)
                    yield from attend(i, kv, 1, qa, qab)
                S.op("dve", lambda e: e.tensor_tensor(out=MIX[:, 0:512], in0=YA[:], in1=SZ[:, 0:512], op=ALU.mult),
                     reads=[YA.b, SZ.b], writes=[MIX.b])

            OB = MK[:].rearrange("p a b -> p (a b)")

            def tail(i):
                par = i % 2
                MIX = MIXL[par]
                xt = X[i % 3]
                for c in range(8):
                    S.op("pe", lambda e, c=c: e.transpose(out=TP[:, c * 128:(c + 1) * 128], in_=MIX[:, c * 128:(c + 1) * 128],
                                                          identity=ident[:]), reads=[MIX.b, ident.b], writes=[TP.b])
                S.op("act", lambda e: e.copy(out=MT[:], in_=TP[:].rearrange("p (c m) -> p c m", c=8)), reads=[TP.b], writes=[MT.b])
                yield
                for hf in range(2):
                    sl = slice(hf * 512, (hf + 1) * 512)
                    for c in range(8):
                        S.op("pe", lambda e, c=c, sl=sl: e.matmul(OB[:, :], lhsT=MT[:, c, :], rhs=WOUT[:, c, sl],
                                                                   start=(c == 0), stop=(c == 7)),
                             reads=[MT.b, WOUT.b], writes=[MK.b])
                    S.op("act", lambda e, sl=sl: e.activation(out=JUNK2[:, sl], in_=OB[:, :], func=AF.Square),
                         reads=[MK.b], writes=[JUNK2.b])
                    S.op("dve", lambda e, sl=sl: e.tensor_copy(out=Y[:, sl], in_=OB[:, :]), reads=[MK.b], writes=[Y.b])
                    yield
                S.op("dve", lambda e: e.tensor_reduce(out=SMALL[:, 24:25], in_=JUNK2[:], axis=AX.X, op=ALU.add),
                     reads=[JUNK2.b], writes=[SMALL.b])
                rstd_from_ss(SMALL[:, 24:25], SMALL[:, 26:27], float(D), 128, [SMALL.b], [SMALL.b], SMALL[:, 25:26])
                yield
                S.op("dve", lambda e: e.scalar_tensor_tensor(out=Y[:], in0=Y[:], scalar=SMALL[:, 26:27], in1=GP[:],
                                                             op0=ALU.mult, op1=ALU.mult), reads=[Y.b, SMALL.b, GP.b], writes=[Y.b])
                yield
                S.op("dve", lambda e: e.tensor_tensor(out=Y[:], in0=Y[:], in1=xt[:], op=ALU.add), reads=[Y.b, xt.b], writes=[Y.b])
                if l == 0 and not dbg_y0:
                    store(xp1[i * 128:(i + 1) * 128, :], Y[:], [Y.b], "o_y", writes=[XP1b[i]])
                else:
                    store(y_p[i * 128:(i + 1) * 128, :], Y[:], [Y.b], "o_y")
                yield

            def run3(gens):
                live = [g_ for g_ in gens if g_ is not None]
                while live:
                    for g_ in list(live):
                        try:
                            next(g_)
                        except StopIteration:
                            live.remove(g_)

            ntiles = min(NT, nt_limit)
            for slot in range(ntiles + 2):
                gens = []
                if 0 <= slot - 2 < ntiles:
                    gens.append(tail(slot - 2))
                if 0 <= slot - 1 < ntiles:
                    gens.append(back(slot - 1))
                if slot < ntiles:
                    gens.append(front(slot))
                run3(gens)

        XS1b = Buf("xs1")
        BNC = Buf("bounce")

        def sample_layer(l, ss):
            PTI = sb([128, NS * NPG], I32, "PTI", ss)
            IDXF = sb([128, NS * NPG], F32, "IDXF", ss)
            IDX = sb([128, NS * NPG], I32, "IDX", ss)
            PGL = [sb([128, NPG, 256], F32, "PGa", ss), sb([128, NPG, 256], F32, "PGb", ss)]
            PGbL = [[Buf(f"PGa{j}") for j in range(NPG)], [Buf(f"PGb{j}") for j in range(NPG)]]
            PG = PGL[0]
            PGb = PGbL[0]
            KCS = sb([64, NS, 2, 32], BF16, "KCS", ss)
            VCS = sb([32, NS, 2, 64], BF16, "VCS", ss)
            QTS = sb([64, 8, NS], BF16, "QTS", ss)
            KNB = sb([NS, 2, 128], BF16, "KNB", ss)
            VNB2 = sb([NS, 2, 2, 64], BF16, "VNB2", ss)
            KNT = sb([64, 4, NS], BF16, "KNT", ss)
            SCs = sb([4, 32, 32], F32, "SCs", ss)
            SM4 = sb([4, 96], F32, "SM4", ss)
            PCT2 = sb([32, 4, 32], F32, "PCT2", ss)
            PCT2B = sb([32, 4, 32], BF16, "PCT2B", ss)
            PCTg = sb([32, 4, 32], BF16, "PCTg", ss)
            IMPs = sb([32, 32], F32, "IMPs", ss)
            SCRs = sb([32, 40], F32, "SCRs", ss)
            WK1s = sb([32, 40], F32, "WK1s", ss)
            WK2s = sb([32, 40], F32, "WK2s", ss)
            M8s = sb([32, 8], F32, "M8s", ss)
            SELs = sb([32, 33], BF16, "SELs", ss)
            SELTs = sb([33, 32], BF16, "SELTs", ss)
            SFORCE = sb([32, 33], F32, "SFORCE", ss)
            SEXPE = sb([33, 17 * 128], BF16, "SEXPE", ss)
            MTS = sb([128, 17, 32], BF16, "MTS", ss)
            SALB = sb([128, 17, 8], F32, "SALB", ss)
            SWALB = sb([128, 5, 8], F32, "SWALB", ss)
            SWMASK = sb([128, 5], F32, "SWMASK", ss)
            SCALB = sb([4, 2, 32], F32, "SCALB", ss)
            OSS = sb([4, 32, 65], F32, "OSS", ss)
            OT = sb([NS, 3, 8, 65], F32, "OT", ss)
            xt = X[0]

            for tb, src in [(SFORCE, c_sforce), (SEXPE, c_sexpe), (SALB, c_salb), (SWALB, c_swalb),
                            (SWMASK, c_swmask), (SCALB, c_scalb)]:
                load(tb[:], src, [tb.b])
            S.op("sp", lambda e: e.dma_start(out=kvw_s[l, :, 0:511, :], in_=ckw[l, :, 1:512, :]), dma="cp_kvw")
            S.op("sp", lambda e: e.dma_start(out=pool_s[l, :, 0:14, :], in_=spool[l, :, 1:15, :]), dma="cp_pool")
            load(PTI[:], ptab, [PTI.b])
            S.op("dve", lambda e: e.tensor_copy(out=IDXF[:], in_=PTI[:]), reads=[PTI.b], writes=[IDXF.b])
            S.op("dve", lambda e: e.tensor_scalar(out=IDXF[:], in0=IDXF[:], scalar1=128.0, scalar2=cmisc[:, 3:4],
                                                  op0=ALU.mult, op1=ALU.add), reads=[IDXF.b, cmisc.b], writes=[IDXF.b])
            S.op("dve", lambda e: e.tensor_scalar(out=IDXF[:], in0=IDXF[:], scalar1=float(l * NPHYS * 128), scalar2=None,
                                                  op0=ALU.add), reads=[IDXF.b], writes=[IDXF.b])
            S.op("dve", lambda e: e.tensor_copy(out=IDX[:], in_=IDXF[:]), reads=[IDXF.b], writes=[IDX.b])
            S.op("dve", lambda e: e.memset(SCRs[:], 0.0), writes=[SCRs.b])
            S.op("dve", lambda e: e.memset(KCS[:], 0.0), writes=[KCS.b])
            S.op("dve", lambda e: e.memset(VCS[:], 0.0), writes=[VCS.b])
            S.op("dve", lambda e: e.memset(OSS[:], 0.0), writes=[OSS.b])

            src = xs if l == 0 else xs1
            load(xt[0:NS, :], src, [xt.b], reads=([XS1b] if l == 1 else []))
            run(project(NS, xt, 0))
            GT, SZ, MIX, YA = GTL[0], SZL[0], MIXL[0], YAL[0]
            store(kvc_s[l], KVF[0:NS, 0:256], [KVF.b], "o_kvf")
            store(kvs_s[l], KVF[0:NS, 256:512], [KVF.b], "o_kvf")
            store(kvw_s[l, :, 511, :], KVF[0:NS, 512:768], [KVF.b], "o_kvf")
            store(pool_s[l, :, 14, :], PIN[0:NS, :], [PIN.b], "o_pin")

            def gather(cache, s_, sl):
                for j in range(NPG):
                    col = s_ * NPG + j
                    S.op("pool", lambda e, j=j, col=col: e.indirect_dma_start(
                        out=PGL[sl][:, j, :], out_offset=None, in_=cache[:, :],
                        in_offset=bass.IndirectOffsetOnAxis(ap=IDX[:, col:col + 1], axis=0)),
                         reads=[IDX.b], writes=[PGbL[sl][j]], dma=f"L_PG{sl}")

            for h in range(8):
                S.op("pe", lambda e, h=h: e.transpose(out=TP[0:64, h * NS:(h + 1) * NS], in_=QB[0:NS, h * 64:(h + 1) * 64],
                                                      identity=ident[0:NS, 0:NS]), reads=[QB.b, ident.b], writes=[TP.b])
            S.op("act", lambda e: e.copy(out=QTS[:].rearrange("p h s -> p (h s)"), in_=TP[0:64, 0:8 * NS]),
                 reads=[TP.b], writes=[QTS.b])
            S.op("dve", lambda e: e.tensor_copy(out=KNB[:, 0, :], in_=KVF[0:NS, 256:384]), reads=[KVF.b], writes=[KNB.b])
            S.op("dve", lambda e: e.tensor_copy(out=KNB[:, 1, :], in_=KVF[0:NS, 512:640]), reads=[KVF.b], writes=[KNB.b])
            S.op("dve", lambda e: e.tensor_copy(out=VNB2[:, 0, :, :], in_=KVF[0:NS, 384:512].rearrange("p (k d) -> p k d", k=2)),
                 reads=[KVF.b], writes=[VNB2.b])
            S.op("dve", lambda e: e.tensor_copy(out=VNB2[:, 1, :, :], in_=KVF[0:NS, 640:768].rearrange("p (k d) -> p k d", k=2)),
                 reads=[KVF.b], writes=[VNB2.b])
            for j in range(4):
                S.op("pe", lambda e, j=j: e.transpose(out=TP[0:64, j * NS:(j + 1) * NS],
                                                      in_=KNB[:, j // 2, (j % 2) * 64:(j % 2) * 64 + 64],
                                                      identity=ident[0:NS, 0:NS]), reads=[KNB.b, ident.b], writes=[TP.b])
            S.op("act", lambda e: e.copy(out=KNT[:].rearrange("p j s -> p (j s)"), in_=TP[0:64, 0:4 * NS]),
                 reads=[TP.b], writes=[KNT.b])

            sa = ss.enter_context(ExitStack())
            KCPs = sb([128, NPG, 256], BF16, "KCPs", sa)
            KCTs = sb([128, 2, NPG, 128], BF16, "KCTs", sa)
            HIDs = sb([128, 512], F32, "HIDs", sa)
            HBS = sb([128, 2, 32], F32, "HBS", sa)
            HBTs = sb([128, 2, 32], BF16, "HBTs", sa)
            nsq = min(NS, ns_limit)
            gather(ckc, 0, 0)
            for s_ in range(nsq):
                if s_ + 1 < nsq:
                    gather(ckc, s_ + 1, (s_ + 1) % 2)
                else:
                    gather(cks, 0, (s_ + 1) % 2)
                PG = PGL[s_ % 2]
                PGb = PGbL[s_ % 2]
                S.op("dve", lambda e, PG=PG: e.tensor_tensor(out=KCPs[:], in0=PG[:],
                                                      in1=PETOK[:].unsqueeze(1).to_broadcast([128, NPG, 256]), op=ALU.add),
                     reads=PGb + [PETOK.b], writes=[KCPs.b])
                for q4 in range(4):
                    for jl in range(4):
                        for r in range(2):
                            S.op("pe", lambda e, q4=q4, jl=jl, r=r: e.transpose(
                                out=TP[:, (jl * 2 + r) * 128:(jl * 2 + r + 1) * 128],
                                in_=KCPs[:, q4 * 4 + jl, r * 128:(r + 1) * 128], identity=ident[:]),
                                 reads=[KCPs.b, ident.b], writes=[TP.b])
                    S.op("act", lambda e, q4=q4: e.copy(
                        out=KCTs[:, :, q4 * 4:(q4 + 1) * 4, :].rearrange("p r j t -> p j r t"),
                        in_=TP[:].rearrange("p (j r t) -> p j r t", j=4, r=2)), reads=[TP.b], writes=[KCTs.b])
                for r in range(2):
                    for ch in range(4):
                        pj = PJ[(r * 4 + ch) % 2]
                        S.op("pe", lambda e, r=r, ch=ch, pj=pj: e.matmul(
                            pj[:, :], lhsT=W1BD[:, r, :],
                            rhs=KCTs[:, r, ch * 4:(ch + 1) * 4, :].rearrange("p j t -> p (j t)"), start=True, stop=True),
                             reads=[W1BD.b, KCTs.b], writes=[pj.b])
                        S.op("act", lambda e, pj=pj: e.activation(out=HIDs[:], in_=pj[:, :], func=AF.Tanh, scale=0.5),
                             reads=[pj.b], writes=[HIDs.b])
                        S.op("dve", lambda e, pj=pj: e.scalar_tensor_tensor(out=HIDs[:], in0=HIDs[:], scalar=1.0, in1=pj[:, :],
                                                                            op0=ALU.add, op1=ALU.mult), reads=[pj.b, HIDs.b], writes=[HIDs.b])
                        S.op("dve", lambda e, r=r, ch=ch: e.tensor_reduce(
                            out=HBS[:, r, ch * 8:(ch + 1) * 8], in_=HIDs[:].rearrange("p (b c) -> p b c", c=64),
                            axis=AX.X, op=ALU.add), reads=[HIDs.b], writes=[HBS.b])
                S.op("dve", lambda e: e.tensor_copy(out=HBTs[:], in_=HBS[:]), reads=[HBS.b], writes=[HBTs.b])
                for kv in range(2):
                    S.op("pe", lambda e, kv=kv: e.matmul(MS[0:64, kv * 32:(kv + 1) * 32], lhsT=W2KV[:, kv, :], rhs=HBTs[:, 0, :],
                                                         start=True, stop=True), reads=[W2KV.b, HBTs.b], writes=[MS.b])
                    S.op("pe", lambda e, kv=kv: e.matmul(MS[0:32, 64 + kv * 64:64 + (kv + 1) * 64], lhsT=HBTs[:, 1, :],
                                                         rhs=W2KV[:, 2 + kv, :], start=True, stop=True),
                         reads=[W2KV.b, HBTs.b], writes=[MS.b])
                S.op("act", lambda e, s_=s_: e.copy(out=KCS[:, s_, :, :].rearrange("p k n -> p (k n)"), in_=MS[0:64, 0:64]),
                     reads=[MS.b], writes=[KCS.b])
                S.op("act", lambda e, s_=s_: e.copy(out=VCS[:, s_, :, :].rearrange("p k n -> p (k n)"), in_=MS[0:32, 64:192]),
                     reads=[MS.b], writes=[VCS.b])

            sa.close()
            S.fence()
            for s_ in range(NS):
                for kv in range(2):
                    j = s_ * 2 + kv
                    pj = PJ[j // 16]
                    S.op("pe", lambda e, s_=s_, kv=kv, j=j, pj=pj: e.matmul(
                        pj[0:4, (j % 16) * 32:(j % 16 + 1) * 32], lhsT=QTS[:, kv * 4:(kv + 1) * 4, s_],
                        rhs=KCS[:, s_, kv, :], start=True, stop=True), reads=[QTS.b, KCS.b], writes=[pj.b])
            for hf in range(2):
                S.op("dve", lambda e, hf=hf: e.tensor_tensor(
                    out=SCs[:, hf * 16:(hf + 1) * 16, :].rearrange("p (s k) n -> p s (k n)", k=2),
                    in0=PJ[hf][0:4, :].rearrange("p (s kn) -> p s kn", s=8),
                    in1=SCALB[:].rearrange("p k n -> p (k n)").unsqueeze(1).to_broadcast([4, 8, 64]), op=ALU.add),
                     reads=[PJ[hf].b, SCALB.b], writes=[SCs.b])
            S.op("dve", lambda e: e.tensor_reduce(out=SM4[:, 0:32], in_=SCs[:], axis=AX.X, op=ALU.max), reads=[SCs.b], writes=[SM4.b])
            S.op("dve", lambda e: e.tensor_tensor(out=SCs[:], in0=SCs[:], in1=SM4[:, 0:32].unsqueeze(2).to_broadcast([4, 32, 32]),
                                                  op=ALU.subtract), reads=[SCs.b, SM4.b], writes=[SCs.b])
            S.op("act", lambda e: e.activation(out=SCs[:].rearrange("p j n -> p (j n)"), in_=SCs[:].rearrange("p j n -> p (j n)"),
                                               func=AF.Exp), reads=[SCs.b], writes=[SCs.b])
            S.op("dve", lambda e: e.tensor_reduce(out=SM4[:, 32:64], in_=SCs[:], axis=AX.X, op=ALU.add), reads=[SCs.b], writes=[SM4.b])
            S.op("dve", lambda e: e.reciprocal(out=SM4[:, 64:96], in_=SM4[:, 32:64]), reads=[SM4.b], writes=[SM4.b])
            S.op("dve", lambda e: e.tensor_tensor(out=SCs[:], in0=SCs[:], in1=SM4[:, 64:96].unsqueeze(2).to_broadcast([4, 32, 32]),
                                                  op=ALU.mult), reads=[SCs.b, SM4.b], writes=[SCs.b])
            store(bounce[0:4, 0:1024], SCs[:].rearrange("p j n -> p (j n)"), [SCs.b], "o_bnc", writes=[BNC])
            load(PCT2[:], bounce[0:4, 0:1024].rearrange("g (j n) -> j g n", j=32), [PCT2.b], reads=[BNC])
            S.op("dve", lambda e: e.tensor_reduce(out=IMPs[:], in_=PCT2[:].rearrange("p g n -> p n g"), axis=AX.X, op=ALU.add),
                 reads=[PCT2.b], writes=[IMPs.b])
            S.op("dve", lambda e: e.tensor_tensor(out=SCRs[:, 0:32], in0=IMPs[:], in1=SFORCE[:, 0:32], op=ALU.add),
                 reads=[IMPs.b, SFORCE.b], writes=[SCRs.b])
            S.op("dve", lambda e: e.tensor_copy(out=SCRs[:, 32:33], in_=SFORCE[:, 32:33]), reads=[SFORCE.b], writes=[SCRs.b])
            S.op("dve", lambda e: e.max(out=M8s[:], in_=SCRs[:, 0:33]), reads=[SCRs.b], writes=[M8s.b])
            S.op("dve", lambda e: e.match_replace(out=WK1s[:, 0:33], in_to_replace=M8s[:], in_values=SCRs[:, 0:33], imm_value=NEG),
                 reads=[SCRs.b, M8s.b], writes=[WK1s.b])
            S.op("dve", lambda e: e.max(out=M8s[:], in_=WK1s[:, 0:33]), reads=[WK1s.b], writes=[M8s.b])
            S.op("dve", lambda e: e.match_replace(out=WK2s[:, 0:33], in_to_replace=M8s[:], in_values=WK1s[:, 0:33], imm_value=NEG),
                 reads=[WK1s.b, M8s.b], writes=[WK2s.b])
            S.op("dve", lambda e: e.tensor_tensor(out=SELs[:], in0=WK2s[:, 0:33], in1=SCRs[:, 0:33], op=ALU.not_equal),
                 reads=[WK2s.b, SCRs.b], writes=[SELs.b])
            S.op("pe", lambda e: e.transpose(out=TP[0:33, 0:32], in_=SELs[:], identity=ident[0:32, 0:32]),
                 reads=[SELs.b, ident.b], writes=[TP.b])
            S.op("act", lambda e: e.copy(out=SELTs[:], in_=TP[0:33, 0:32]), reads=[TP.b], writes=[SELTs.b])
            for kt in range(17):
                pj = PJ[0] if kt < 9 else PJ[1]
                k0 = kt if kt < 9 else kt - 9
                S.op("pe", lambda e, kt=kt, pj=pj, k0=k0: e.matmul(pj[:, k0 * 32:(k0 + 1) * 32], lhsT=SEXPE[:, kt * 128:(kt + 1) * 128],
                                                                   rhs=SELTs[:], start=True, stop=True),
                     reads=[SEXPE.b, SELTs.b], writes=[pj.b])
            S.op("act", lambda e: e.copy(out=MTS[:, 0:9, :].rearrange("p k j -> p (k j)"), in_=PJ[0][:, 0:288]),
                 reads=[PJ[0].b], writes=[MTS.b])
            S.op("act", lambda e: e.copy(out=MTS[:, 9:17, :].rearrange("p k j -> p (k j)"), in_=PJ[1][:, 0:256]),
                 reads=[PJ[1].b], writes=[MTS.b])
            S.op("dve", lambda e: e.tensor_copy(out=PCT2B[:], in_=PCT2[:]), reads=[PCT2.b], writes=[PCT2B.b])
            for g in range(4):
                S.op("pe", lambda e, g=g: e.transpose(out=TP[0:32, g * 32:(g + 1) * 32], in_=PCT2B[:, g, :], identity=ident[0:32, 0:32]),
                     reads=[PCT2B.b, ident.b], writes=[TP.b])
            S.op("act", lambda e: e.copy(out=PCTg[:].rearrange("p g j -> p (g j)"), in_=TP[0:32, 0:128]), reads=[TP.b], writes=[PCTg.b])
            for rnd in range(4):
                pj = PJ[rnd % 2]
                for jj in range(8):
                    j = rnd * 8 + jj
                    s_, kv = j // 2, j % 2
                    S.op("pe", lambda e, j=j, jj=jj, s_=s_, kv=kv, pj=pj: e.matmul(
                        pj[0:4, jj * 64:(jj + 1) * 64], lhsT=PCTg[:, :, j], rhs=VCS[:, s_, kv, :], start=True, stop=True),
                         reads=[PCTg.b, VCS.b], writes=[pj.b])
                S.op("act", lambda e, rnd=rnd, pj=pj: e.copy(out=OSS[:, rnd * 8:(rnd + 1) * 8, 0:64],
                                                            in_=pj[0:4, :].rearrange("p (j d) -> p j d", j=8)),
                     reads=[pj.b], writes=[OSS.b])

            def bounce_out(br):
                store(bounce[0:4, 0:2080], OSS[:].rearrange("p j d -> p (j d)"), [OSS.b], "o_bnc", writes=[BNC])
                load(OT[:, br, :, :].rearrange("s (k g) d -> s k g d", k=2),
                     bounce[0:4, 0:2080].rearrange("g (s k d) -> s k g d", s=NS, k=2), [OT.b], reads=[BNC])

            bounce_out(0)

            S.fence()
            sb2 = ss.enter_context(ExitStack())
            KSB = sb([128, NPG, 128], BF16, "KSB", sb2)
            KTSs = sb([64, 2, 17, 128], BF16, "KTSs", sb2)
            VSs = sb([128, 17, 2, 65], BF16, "VSs", sb2)
            KTWs = sb([64, 2, 5, 128], BF16, "KTWs", sb2)
            VWs = sb([128, 5, 2, 65], BF16, "VWs", sb2)
            SSs = sb([128, 17, 4], F32, "SSs", sb2)
            PTs = sb([128, 17, 4], BF16, "PTs", sb2)
            S.op("dve", lambda e: e.memset(KTSs[:], 0.0), writes=[KTSs.b])
            S.op("dve", lambda e: e.memset(KTWs[:], 0.0), writes=[KTWs.b])
            S.op("dve", lambda e: e.memset(VSs[:], 0.0), writes=[VSs.b])
            S.op("dve", lambda e: e.memset(VWs[:], 0.0), writes=[VWs.b])
            S.op("dve", lambda e: e.memset(VSs[:, :, :, 64:65], 1.0), writes=[VSs.b])
            S.op("dve", lambda e: e.memset(VWs[:, :, :, 64:65], 1.0), writes=[VWs.b])
            def attend_s(s_, br):
                nkt = 17 if br == 1 else 5
                KTt, Vt = (KTSs, VSs) if br == 1 else (KTWs, VWs)
                ALBt = SALB if br == 1 else SWALB
                for kv in range(2):
                    j = s_ * 2 + kv
                    st = ST[j % 2]
                    for kt in range(nkt):
                        S.op("pe", lambda e, kt=kt, st=st, kv=kv: e.matmul(
                            st[:, kt * 4:(kt + 1) * 4], lhsT=KTt[:, kv, kt, :], rhs=QTS[:, kv * 4:(kv + 1) * 4, s_],
                            start=True, stop=True), reads=[KTt.b, QTS.b], writes=[st.b])
                    S.op("dve", lambda e, st=st, kv=kv: e.tensor_tensor(
                        out=SSs[:, 0:nkt, :], in0=st[:, 0:nkt * 4].rearrange("p (k g) -> p k g", g=4),
                        in1=ALBt[:, :, kv * 4:(kv + 1) * 4], op=ALU.add), reads=[st.b, ALBt.b], writes=[SSs.b])
                    S.op("act", lambda e: e.activation(out=PTs[:, 0:nkt, :], in_=SSs[:, 0:nkt, :], func=AF.Exp),
                         reads=[SSs.b], writes=[PTs.b])
                    if br == 1:
                        m_ap = MTS[:, :, j:j + 1].to_broadcast([128, 17, 4])
                        mrd = [MTS.b]
                    else:
                        m_ap = SWMASK[:].unsqueeze(2).to_broadcast([128, 5, 4])
                        mrd = [SWMASK.b]
                    S.op("dve", lambda e, m_ap=m_ap: e.tensor_tensor(out=PTs[:, 0:nkt, :], in0=PTs[:, 0:nkt, :], in1=m_ap, op=ALU.mult),
                         reads=[PTs.b] + mrd, writes=[PTs.b])
                    for kt in range(nkt):
                        S.op("pe", lambda e, kt=kt, kv=kv: e.matmul(OA[0:4, 0:65], lhsT=PTs[:, kt, :], rhs=Vt[:, kt, kv, :],
                                                                     start=(kt == 0), stop=(kt == nkt - 1)),
                             reads=[PTs.b, Vt.b], writes=[OA.b])
                    S.op("act", lambda e, j=j: e.copy(out=OSS[:, j, :], in_=OA[0:4, 0:65]), reads=[OA.b], writes=[OSS.b])

            for s_ in range(nsq):
                slb = (nsq + s_) % 2
                if s_ + 1 < nsq:
                    gather(cks, s_ + 1, (nsq + s_ + 1) % 2)
                PG = PGL[slb]
                PGb = PGbL[slb]
                S.op("dve", lambda e, PG=PG: e.tensor_copy(out=KSB[:], in_=PG[:, :, 0:128]), reads=PGb, writes=[KSB.b])
                S.op("dve", lambda e, PG=PG: e.tensor_copy(out=VSs[:, 0:16, :, 0:64],
                                                    in_=PG[:, :, 128:256].rearrange("p j (k d) -> p j k d", k=2)),
                     reads=PGb, writes=[VSs.b])
                S.op("sp", lambda e, s_=s_: e.dma_start(out=VSs[0:1, 16, :, 0:64], in_=VNB2[s_:s_ + 1, 0, :, :]),
                     reads=[VNB2.b], writes=[VSs.b], dma="L_vnew")
                for q4 in range(4):
                    for jl in range(4):
                        for kv in range(2):
                            S.op("pe", lambda e, q4=q4, jl=jl, kv=kv: e.transpose(
                                out=TP[0:64, (kv * 4 + jl) * 128:(kv * 4 + jl + 1) * 128],
                                in_=KSB[:, q4 * 4 + jl, kv * 64:(kv + 1) * 64], identity=ident[:]),
                                 reads=[KSB.b, ident.b], writes=[TP.b])
                    S.op("act", lambda e, q4=q4: e.copy(out=KTSs[:, :, q4 * 4:(q4 + 1) * 4, :],
                                                        in_=TP[0:64, :].rearrange("p (k j t) -> p k j t", k=2, j=4)),
                         reads=[TP.b], writes=[KTSs.b])
                S.op("dve", lambda e, s_=s_: e.tensor_copy(out=KTSs[:, :, 16, 0:1], in_=KNT[:, 0:2, s_:s_ + 1]),
                     reads=[KNT.b], writes=[KTSs.b])
                attend_s(s_, 1)
            bounce_out(1)
            load(PGL[0][:, 0:4, :], ckw[l, 0].rearrange("(k p) f -> p k f", p=128), PGbL[0][0:4])
            for s_ in range(nsq):
                if s_ + 1 < nsq:
                    load(PGL[(s_ + 1) % 2][:, 0:4, :], ckw[l, s_ + 1].rearrange("(k p) f -> p k f", p=128),
                         PGbL[(s_ + 1) % 2][0:4])
                PG = PGL[s_ % 2]
                PGb = PGbL[s_ % 2]
                S.op("dve", lambda e, PG=PG: e.tensor_copy(out=KSB[:, 0:4, :], in_=PG[:, 0:4, 0:128]), reads=PGb[0:4], writes=[KSB.b])
                S.op("dve", lambda e, PG=PG: e.tensor_copy(out=VWs[:, 0:4, :, 0:64],
                                                    in_=PG[:, 0:4, 128:256].rearrange("p j (k d) -> p j k d", k=2)),
                     reads=PGb[0:4], writes=[VWs.b])
                S.op("sp", lambda e, s_=s_: e.dma_start(out=VWs[0:1, 4, :, 0:64], in_=VNB2[s_:s_ + 1, 1, :, :]),
                     reads=[VNB2.b], writes=[VWs.b], dma="L_vnew")
                for jl in range(4):
                    for kv in range(2):
                        S.op("pe", lambda e, jl=jl, kv=kv: e.transpose(
                            out=TP[0:64, (kv * 4 + jl) * 128:(kv * 4 + jl + 1) * 128],
                            in_=KSB[:, jl, kv * 64:(kv + 1) * 64], identity=ident[:]),
                             reads=[KSB.b, ident.b], writes=[TP.b])
                S.op("act", lambda e: e.copy(out=KTWs[:, :, 0:4, :], in_=TP[0:64, :].rearrange("p (k j t) -> p k j t", k=2, j=4)),
                     reads=[TP.b], writes=[KTWs.b])
                S.op("dve", lambda e, s_=s_: e.tensor_copy(out=KTWs[:, :, 4, 0:1], in_=KNT[:, 2:4, s_:s_ + 1]),
                     reads=[KNT.b], writes=[KTWs.b])
                attend_s(s_, 2)
            bounce_out(2)

            sb2.close()
            S.fence()
            sf = ss.enter_context(ExitStack())
            SPT = sb([NS, 15, 256], F32, "SPT", sf)
            PSUMS = sb([NS, 256], F32, "PSUMS", sf)
            DIFB = sb([NS, 256], BF16, "DIFB", sf)
            DTs = sb([64, 4, NS], BF16, "DTs", sf)
            CSs = sb([NS, 16], F32, "CSs", sf)
            M = NS
            S.op("dve", lambda e: e.tensor_tensor(
                out=YA[0:M, :].rearrange("p (h d) -> p h d", h=8), in0=OT[:, 0, :, 0:64],
                in1=GT[0:M, :].rearrange("p (h t) -> p h t", t=3)[:, :, 0:1].to_broadcast([M, 8, 64]), op=ALU.mult),
                 reads=[OT.b, GT.b], writes=[YA.b])
            for br in (1, 2):
                S.op("dve", lambda e, br=br: e.tensor_scalar(out=CSs[:, 0:8], in0=OT[:, br, :, 64], scalar1=1e-30, scalar2=None,
                                                             op0=ALU.max), reads=[OT.b], writes=[CSs.b])
                S.op("dve", lambda e, br=br: e.reciprocal(out=CSs[:, 0:8], in_=CSs[:, 0:8]), reads=[CSs.b], writes=[CSs.b])
                S.op("dve", lambda e, br=br: e.tensor_tensor(out=CSs[:, 8:16], in0=CSs[:, 0:8],
                                                             in1=GT[0:M, :].rearrange("p (h t) -> p h t", t=3)[:, :, br], op=ALU.mult),
                     reads=[CSs.b, GT.b], writes=[CSs.b])
                for h in range(8):
                    S.op("dve", lambda e, br=br, h=h: e.scalar_tensor_tensor(
                        out=YA[0:M, h * 64:(h + 1) * 64], in0=OT[:, br, h, 0:64], scalar=CSs[:, 8 + h:9 + h],
                        in1=YA[0:M, h * 64:(h + 1) * 64], op0=ALU.mult, op1=ALU.add),
                         reads=[OT.b, CSs.b, YA.b], writes=[YA.b])
            S.op("dve", lambda e: e.tensor_tensor(out=MIX[0:M, 0:512], in0=YA[0:M, :], in1=SZ[0:M, 0:512], op=ALU.mult),
                 reads=[YA.b, SZ.b], writes=[MIX.b])
            load(SPT[:], spool[l], [SPT.b])
            for g, w in enumerate((2, 4, 8, 16)):
                S.op("dve", lambda e, g=g, w=w: e.tensor_reduce(
                    out=PSUMS[:, g * 64:(g + 1) * 64],
                    in_=SPT[:, 15 - (w - 1):15, g * 64:(g + 1) * 64].rearrange("p r c -> p c r"), axis=AX.X, op=ALU.add),
                     reads=[SPT.b], writes=[PSUMS.b])
                S.op("dve", lambda e, g=g, w=w: e.tensor_scalar(out=PSUMS[:, g * 64:(g + 1) * 64], in0=PSUMS[:, g * 64:(g + 1) * 64],
                                                                scalar1=1.0 / w, scalar2=None, op0=ALU.mult),
                     reads=[PSUMS.b], writes=[PSUMS.b])
                S.op("dve", lambda e, g=g, w=w: e.scalar_tensor_tensor(
                    out=DIFB[:, g * 64:(g + 1) * 64], in0=PIN[0:M, g * 64:(g + 1) * 64], scalar=(1.0 / w - 1.0),
                    in1=PSUMS[:, g * 64:(g + 1) * 64], op0=ALU.mult, op1=ALU.add), reads=[PIN.b, PSUMS.b], writes=[DIFB.b])
            for g in range(4):
                S.op("pe", lambda e, g=g: e.transpose(out=TP[0:64, g * M:(g + 1) * M], in_=DIFB[:, g * 64:(g + 1) * 64],
                                                      identity=ident[0:M, 0:M]), reads=[DIFB.b, ident.b], writes=[TP.b])
            S.op("act", lambda e: e.copy(out=DTs[:].rearrange("p g s -> p (g s)"), in_=TP[0:64, 0:4 * M]), reads=[TP.b], writes=[DTs.b])
            for g in range(4):
                S.op("pe", lambda e, g=g: e.matmul(MS[0:M, g * 64:(g + 1) * 64], lhsT=DTs[:, g, :], rhs=WPOOL[:, g, :],
                                                   start=True, stop=True), reads=[DTs.b, WPOOL.b], writes=[MS.b])
            S.op("dve", lambda e: e.tensor_tensor(out=TMPB[0:M, :], in0=MS[0:M, 0:256], in1=PSC[0:M, :], op=ALU.mult),
                 reads=[MS.b, PSC.b], writes=[TMPB.b])
            S.op("dve", lambda e: e.tensor_tensor(out=MIX[0:M, 512:768], in0=TMPB[0:M, :], in1=SZ[0:M, 512:768], op=ALU.mult),
                 reads=[TMPB.b, SZ.b], writes=[MIX.b])
            gmlp_norm(M)
            store(gv_s[l], VN[0:M, :], [VN.b], "o_vn")
            for g in range(4):
                S.op("dve", lambda e, g=g: e.tensor_scalar(out=YC[0:M, g * 64:(g + 1) * 64], in0=VN[0:M, g * 64:(g + 1) * 64],
                                                           scalar1=WS00[0:M, g:g + 1], scalar2=BS0[0:M, g:g + 1],
                                                           op0=ALU.mult, op1=ALU.add), reads=[VN.b, WS00.b, BS0.b], writes=[YC.b])
            S.op("dve", lambda e: e.tensor_tensor(out=YC[0:M, :], in0=YC[0:M, :], in1=UV[0:M, 0:256], op=ALU.mult),
                 reads=[YC.b, UV.b], writes=[YC.b])
            S.op("dve", lambda e: e.tensor_tensor(out=MIX[0:M, 768:1024], in0=YC[0:M, :], in1=SZ[0:M, 768:1024], op=ALU.mult),
                 reads=[YC.b, SZ.b], writes=[MIX.b])
            if l == 0 and not dbg_y0:
                run(merge_out(M, xt, xs1, [XS1b], "o_y", 0))
            else:
                run(merge_out(M, xt, y_s, [], "o_y", 0))

        for l in layers:
            load_weights(l)
            if do_prompt:
                S.fence()
                with ExitStack() as pstack:
                    prompt_layer(l, pstack)
                S.fence()
            if do_sample:
                S.fence()
                with ExitStack() as sstack:
                    sample_layer(l, sstack)
                S.fence()
        S.emit()
    return nc


def _consts():
    bf = ml_dtypes.bfloat16
    c = {}
    c["c_ident"] = np.eye(128, dtype=np.float32).astype(bf)
    c["c_identf"] = np.eye(128, dtype=np.float32)
    kk = np.arange(128)[:, None]
    qq = np.arange(128)[None, :]
    c["c_trile"] = (kk <= qq).astype(np.float32).astype(bf)
    c["c_trigt"] = (kk > qq).astype(np.float32).astype(bf)
    tk = np.arange(T)
    c["c_kaug"] = np.stack([np.ones(T), np.ones(T), tk // 128, tk % 128]).astype(np.float32).astype(bf)
    qa = np.zeros((NT, 4, 2, 4, 128), np.float32)
    a = np.arange(128)
    for i in range(NT):
        for kv in range(2):
            for g in range(4):
                s = SLOPES[kv * 4 + g]
                qa[i, 0, kv, g, :] = -s * 128.0 * i
                qa[i, 1, kv, g, :] = -s * a
                qa[i, 2, kv, g, :] = s * 128.0
                qa[i, 3, kv, g, :] = s
    qa2 = np.broadcast_to(qa.reshape(NT, 4, 2, 1, 512), (NT, 4, 2, 2, 512))
    c["c_qaug"] = np.ascontiguousarray(qa2).reshape(NT, 4, 2048).astype(bf)
    jj = np.arange(32)[:, None]
    c["c_kmask"] = (((tk[None, :] // 64) % 32) == jj).astype(np.float32).astype(bf)
    n = np.arange(64)[:, None]
    c["c_expe"] = (n == (tk[None, :] // 64)).astype(np.float32).astype(bf)
    albm = np.zeros((128, 8, 64), np.float32)
    for h in range(8):
        albm[:, h, :] = SLOPES[h] * 64.0 * np.arange(64)[None, :]
    c["c_alb"] = albm.reshape(128, 512)
    c["c_iota"] = np.tile(np.arange(64, dtype=np.float32)[None, :], (128, 1))
    misc = np.zeros((128, 8), np.float32)
    misc[:, 0] = -1.0 + (a >= 63) + (a >= 127)
    misc[:, 1] = (a >= 64)
    misc[:, 2] = -0.5
    misc[:, 3] = a
    c["c_misc"] = misc
    f0 = np.zeros((128, 64), np.float32)
    f0[:, 0] = 1e4
    c["c_force0"] = f0
    pb = np.zeros((128, 3, 4, 128), np.float32)
    tt = np.arange(128)
    for g, w in enumerate((2, 4, 8, 16)):
        for tp in range(128):
            for s_ in range(w):
                t = tp - s_
                if t >= 0:
                    pb[t, 0, g, tp] += 1.0 / w
                    pb[t, 2, g, tp] += 1.0 / min(w, tp + 1)
                else:
                    pb[128 + t, 1, g, tp] += 1.0 / w
            pb[tp, 0, g, tp] -= 1.0
            pb[tp, 2, g, tp] -= 1.0
    c["c_poolb"] = pb.astype(bf)
    c["c_trilT"] = (kk <= qq).astype(np.float32)
    cc = np.arange(128)[:, None]
    salb = np.zeros((128, 17, 8), np.float32)
    swalb = np.zeros((128, 5, 8), np.float32)
    for h in range(8):
        for kt in range(17):
            salb[:, kt, h] = -SLOPES[h] * (2048.0 - (kt * 128 + np.arange(128)))
        for kt in range(5):
            swalb[:, kt, h] = -SLOPES[h] * (2048.0 - (1536 + kt * 128 + np.arange(128)))
    salb[1:, 16, :] = 0.0
    swalb[1:, 4, :] = 0.0
    c["c_salb"] = salb
    c["c_swalb"] = swalb
    swm = np.ones((128, 5), np.float32)
    swm[0, 0] = 0.0
    swm[1:, 4] = 0.0
    c["c_swmask"] = swm
    scalb = np.zeros((4, 2, 32), np.float32)
    for kv in range(2):
        for g in range(4):
            scalb[g, kv, :] = SLOPES[kv * 4 + g] * 64.0 * np.arange(32)
    c["c_scalb"] = scalb
    sexpe = np.zeros((33, 17 * 128), np.float32)
    for pos in range(2049):
        sexpe[pos // 64, pos] = 1.0
    c["c_sexpe"] = sexpe.astype(bf)
    sf = np.zeros((32, 33), np.float32)
    sf[:, 0] = 1e4
    sf[:, 32] = 1e4
    c["c_sforce"] = sf
    c["c_spool"] = np.zeros((16, 4, 16), np.float32)
    c["c_smask0"] = np.zeros((128, 17), np.float32)
    return c


_NC_CACHE = {}


def kernel(x_prompt, x_sample, cache_kv_cmp, cache_kv_sel, cache_kv_win, state_pool, page_table,
           norm_pre, w_in, cmp_pe, cmp_w1, cmp_w2, pool_w, pool_scale, gmlp_norm, gmlp_ws, gmlp_bs,
           w_out, norm_post, _build_kwargs=None):
    kw = _build_kwargs or {}
    key = tuple(sorted((k, str(v)) for k, v in kw.items()))
    if key not in _NC_CACHE:
        _NC_CACHE[key] = build_program(**kw)
    nc = _NC_CACHE[key]
    in_maps = _prepare(kw, x_prompt, x_sample, cache_kv_cmp, cache_kv_sel, cache_kv_win, state_pool, page_table,
                       norm_pre, w_in, cmp_pe, cmp_w1, cmp_w2, pool_w, pool_scale, gmlp_norm, gmlp_ws, gmlp_bs,
                       w_out, norm_post)
    res = run_bass_kernel_spmd(nc, in_maps, core_ids=list(range(NCORES)))
    return _assemble(res.results)


def _prepare(kw, x_prompt, x_sample, cache_kv_cmp, cache_kv_sel, cache_kv_win, state_pool, page_table,
             norm_pre, w_in, cmp_pe, cmp_w1, cmp_w2, pool_w, pool_scale, gmlp_norm, gmlp_ws, gmlp_bs,
             w_out, norm_post):
    f32 = np.float32
    consts = _consts()
    npre_t = np.ascontiguousarray(np.asarray(norm_pre, f32).reshape(2, 8, 128).transpose(0, 2, 1))
    gpost_rep = np.ascontiguousarray(np.broadcast_to(np.asarray(norm_post, f32)[:, None, :], (2, 128, D)))
    pe = np.asarray(cmp_pe, f32)
    pe_tok = np.zeros((2, 128, 2, 2, 64), f32)
    for kv in range(2):
        pe_tok[:, 0:64, :, kv, :] = pe.transpose(0, 2, 1, 3)
        pe_tok[:, 64:128, :, kv, :] = pe.transpose(0, 2, 1, 3)
    pe_tok = pe_tok.reshape(2, 128, 256)
    w1 = np.asarray(cmp_w1, f32)
    w1bd = np.zeros((2, 128, 2, 128), f32)
    for r in range(2):
        w1bd[:, 0:64, r, 0:64] = w1[:, r]
        w1bd[:, 64:128, r, 64:128] = w1[:, r]
    w2 = np.asarray(cmp_w2, f32)
    w2kv = np.zeros((2, 128, 4, 64), f32)
    for kv in range(2):
        w2kv[:, kv * 64:(kv + 1) * 64, kv, :] = w2[:, 0]
        w2kv[:, kv * 64:(kv + 1) * 64, 2 + kv, :] = w2[:, 1]
    wpool = np.ascontiguousarray(np.asarray(pool_w, f32).transpose(0, 2, 1, 3))
    pscale_rep = np.ascontiguousarray(np.broadcast_to(np.asarray(pool_scale, f32)[:, None, :], (2, 128, 256)))
    gnorm_rep = np.ascontiguousarray(np.broadcast_to(np.asarray(gmlp_norm, f32)[:, None, :], (2, 128, 256)))
    ws = np.asarray(gmlp_ws, f32)
    wsT = np.ascontiguousarray(ws.transpose(0, 3, 1, 2))
    bsT = np.ascontiguousarray(np.asarray(gmlp_bs, f32).transpose(0, 2, 1))
    ws00_rep = np.ascontiguousarray(np.broadcast_to(ws[:, None, :, 0, 0], (2, 128, 4)))
    bs0_rep = np.ascontiguousarray(np.broadcast_to(np.asarray(gmlp_bs, f32)[:, None, :, 0], (2, 128, 4)))
    if kw.get("do_sample", True):
        ckc = np.asarray(cache_kv_cmp, f32).reshape(2 * NPHYS * 128, 256)
        cks = np.asarray(cache_kv_sel, f32).reshape(2 * NPHYS * 128, 256)
    else:
        ckc = cks = np.zeros((128, 256), f32)
    ckw_all = np.asarray(cache_kv_win, f32).reshape(2, 128, 512, 256)
    sp_all = np.asarray(state_pool, f32)
    xs_all = np.asarray(x_sample, f32).reshape(128, D)
    pt_all = np.asarray(page_table, np.int32)
    shared = dict(w_in=np.asarray(w_in, f32), w_out=np.asarray(w_out, f32), npre_t=npre_t, gpost_rep=gpost_rep,
                  pe_tok=pe_tok, w1bd=w1bd, w2kv=w2kv, wpool=wpool, pscale_rep=pscale_rep, gnorm_rep=gnorm_rep,
                  wsT=wsT, bsT=bsT, ws00_rep=ws00_rep, bs0_rep=bs0_rep, ckc=ckc, cks=cks, **consts)
    in_maps = []
    for c in range(NCORES):
        m = dict(shared)
        m["xp"] = np.ascontiguousarray(np.asarray(x_prompt, f32)[c % 4])
        sl = slice(c * NS, (c + 1) * NS)
        m["xs"] = np.ascontiguousarray(xs_all[sl])
        m["ckw"] = np.ascontiguousarray(ckw_all[:, sl])
        m["spool"] = np.ascontiguousarray(sp_all[:, sl])
        m["ptab"] = np.ascontiguousarray(np.broadcast_to(pt_all[sl].reshape(1, NS * NPG), (128, NS * NPG)))
        in_maps.append(m)
    return in_maps


def _assemble(R):
    y_prompt = np.stack([R[b]["y_p"] for b in range(4)])
    y_sample = np.concatenate([R[c]["y_s"] for c in range(NCORES)], 0).reshape(128, 1, D)

    def pstk(name, shp):
        return np.stack([R[b][name] for b in range(4)], axis=1).reshape(shp)

    def sstk(name, shp):
        return np.concatenate([R[c][name] for c in range(NCORES)], axis=1).reshape(shp)

    outs = (
        y_prompt, y_sample,
        pstk("kvc_p", (2, 4, T, 2, 2, 64)), sstk("kvc_s", (2, 128, 1, 2, 2, 64)),
        pstk("kvs_p", (2, 4, T, 2, 2, 64)), sstk("kvs_s", (2, 128, 1, 2, 2, 64)),
        pstk("kvw_p", (2, 4, 512, 2, 2, 64)), sstk("kvw_s", (2, 128, 512, 2, 2, 64)),
        pstk("pool_p", (2, 4, 15, 256)), sstk("pool_s", (2, 128, 15, 256)),
        pstk("gv_p", (2, 4, 128, 256)), sstk("gv_s", (2, 128, 1, 256)),
    )
    return tuple(np.ascontiguousarray(o, dtype=np.float32) for o in outs)
```

```python
import numpy as np
import ml_dtypes
from contextlib import ExitStack
import concourse.bass as bass
import concourse.mybir as mybir
from concourse.bass_utils import run_bass_kernel_spmd

F32 = mybir.dt.float32
BF16 = mybir.dt.bfloat16
I32 = mybir.dt.int32
AF = mybir.ActivationFunctionType
ALU = mybir.AluOpType
AX = mybir.AxisListType

NCORES = 8
T = 4096
NT = T // 128
D = 1024
DIN = 3096
NS = 16
NPG = 16
NPHYS = 2560
EPS = 1e-6
SLOPES = [2.0 ** (-(h + 1)) for h in range(8)]
NEG = -30000.0

PG = [(0, 512), (512, 1024), (1024, 1304), (1304, 1816), (1816, 2328), (2328, 2840), (2840, 3096)]


class Buf:
    __slots__ = ("name", "w", "r", "excl")

    def __init__(self, name, excl=False):
        self.name = name
        self.w = None
        self.r = {}
        self.excl = excl


class Sched:
    ENG = ["pe", "act", "dve", "pool", "sp"]
    ATTR = {"pe": "tensor", "act": "scalar", "dve": "vector", "pool": "gpsimd", "sp": "sync"}

    def __init__(self, nc, es):
        self.nc = nc
        self.es = es
        self.sem = {e: es.enter_context(nc.semaphore("s_" + e)) for e in self.ENG}
        self.cnt = {e: 0 for e in self.ENG}
        self.prog = {e: [] for e in self.ENG}
        self.waited = {e: {} for e in self.ENG}
        self.streams = {}
        self.pending = {e: [] for e in self.ENG}

    def stream(self, name):
        if name not in self.streams:
            self.streams[name] = [self.es.enter_context(self.nc.semaphore("d_" + name)), 0]
        return self.streams[name]

    def fence(self):
        snap = [(self.sem[e], self.cnt[e]) for e in self.ENG if self.cnt[e] > 0]
        snap += [(st[0], st[1]) for st in self.streams.values() if st[1] > 0]
        for e in self.ENG:
            self.pending[e] = list(snap)

    def op(self, eng, fn, reads=(), writes=(), dma=None):
        writes = list(writes) + [b for b in reads if b.excl]
        reads = [b for b in reads if not b.excl]
        deps = {}

        def add(ev):
            if ev is None:
                return
            k = id(ev[0])
            if k not in deps or deps[k][1] < ev[1]:
                deps[k] = ev

        for b in reads:
            add(b.w)
        for b in writes:
            if not b.r:
                add(b.w)
            for ev in b.r.values():
                add(ev)
        if dma is None:
            self.cnt[eng] += 1
            ev = (self.sem[eng], self.cnt[eng], eng)
            inc = (self.sem[eng], 1)
        else:
            st = self.stream(dma)
            st[1] += 16
            ev = (st[0], st[1], "dma")
            inc = (st[0], 16)
        waits = []
        for (sem, val) in self.pending[eng]:
            k = id(sem)
            if sem is self.sem.get(eng):
                continue
            if self.waited[eng].get(k, 0) >= val:
                continue
            self.waited[eng][k] = val
            waits.append((sem, val))
        self.pending[eng] = []
        for k, (sem, val, src) in deps.items():
            if src == eng and eng == "pe":
                continue
            if self.waited[eng].get(k, 0) >= val:
                continue
            self.waited[eng][k] = val
            waits.append((sem, val))
        self.prog[eng].append((waits, fn, inc))
        k = id(ev[0])
        for b in reads:
            if k not in b.r or b.r[k][1] < ev[1]:
                b.r[k] = ev
        for b in writes:
            b.w = ev
            b.r = {}
        return ev

    def emit(self):
        nc = self.nc
        finals = [(self.sem[e], self.cnt[e]) for e in self.ENG if e != "sp" and self.cnt[e] > 0]
        finals += [(st[0], st[1]) for st in self.streams.values() if st[1] > 0]
        with nc.Block() as block:
            for e in self.ENG:
                prog = self.prog[e]

                def body(engine, prog=prog, e=e):
                    for waits, fn, inc in prog:
                        for sem, val in waits:
                            engine.wait_ge(sem, val)
                        ins = fn(engine)
                        ins.then_inc(inc[0], inc[1])
                    if e == "sp":
                        for sem, val in finals:
                            engine.wait_ge(sem, val)

                getattr(block, self.ATTR[e])(body)


class TB:
    def __init__(self, t, name):
        self.t = t
        self.b = Buf(name)

    def __getitem__(self, idx):
        return self.t[idx]


def build_program(layers=(0, 1), do_prompt=True, do_sample=True, nt_limit=NT, ns_limit=NS, dbg_y0=False, stage=99):
    nc = bass.Bass("TRN2", target_bir_lowering=False)

    def din(name, shape, dt=F32):
        return nc.dram_tensor(name, list(shape), dt, kind="ExternalInput").ap()

    def dout(name, shape, dt=F32):
        return nc.dram_tensor(name, list(shape), dt, kind="ExternalOutput").ap()

    xp = din("xp", [T, D])
    xs = din("xs", [NS, D])
    ncache = 2 * NPHYS * 128 if do_sample else 128
    ckc = din("ckc", [ncache, 256])
    cks = din("cks", [ncache, 256])
    ckw = din("ckw", [2, NS, 512, 256])
    spool = din("spool", [2, NS, 15, 256])
    ptab = din("ptab", [128, NS * NPG], I32)
    w_in = din("w_in", [2, D, DIN])
    w_out = din("w_out", [2, D, D])
    npre_t = din("npre_t", [2, 128, 8])
    gpost_rep = din("gpost_rep", [2, 128, D])
    pe_tok = din("pe_tok", [2, 128, 256])
    w1bd = din("w1bd", [2, 128, 2, 128])
    w2kv = din("w2kv", [2, 128, 4, 64])
    wpool = din("wpool", [2, 64, 4, 64])
    pscale_rep = din("pscale_rep", [2, 128, 256])
    gnorm_rep = din("gnorm_rep", [2, 128, 256])
    wsT = din("wsT", [2, 128, 4, 128])
    bsT = din("bsT", [2, 128, 4])
    ws00_rep = din("ws00_rep", [2, 128, 4])
    bs0_rep = din("bs0_rep", [2, 128, 4])
    c_ident = din("c_ident", [128, 128], BF16)
    c_identf = din("c_identf", [128, 128])
    c_trile = din("c_trile", [128, 128], BF16)
    c_trigt = din("c_trigt", [128, 128], BF16)
    c_kaug = din("c_kaug", [4, T], BF16)
    c_qaug = din("c_qaug", [NT, 4, 2048], BF16)
    c_kmask = din("c_kmask", [32, T], BF16)
    c_expe = din("c_expe", [64, T], BF16)
    c_alb = din("c_alb", [128, 512])
    c_iota = din("c_iota", [128, 64])
    c_misc = din("c_misc", [128, 8])
    c_force0 = din("c_force0", [128, 64])
    c_poolb = din("c_poolb", [128, 3, 4, 128], BF16)
    c_trilT = din("c_trilT", [128, 128])
    c_salb = din("c_salb", [128, 17, 8])
    c_swalb = din("c_swalb", [128, 5, 8])
    c_swmask = din("c_swmask", [128, 5])
    c_scalb = din("c_scalb", [4, 2, 32])
    c_sexpe = din("c_sexpe", [33, 17 * 128], BF16)
    c_sforce = din("c_sforce", [32, 33])
    c_spool = din("c_spool", [16, 4, 16])
    c_smask0 = din("c_smask0", [128, 17])

    y_p = dout("y_p", [T, D])
    kvc_p = dout("kvc_p", [2, T, 256])
    kvs_p = dout("kvs_p", [2, T, 256])
    kvw_p = dout("kvw_p", [2, 512, 256])
    pool_p = dout("pool_p", [2, 15, 256])
    gv_p = dout("gv_p", [2, 128, 256])
    y_s = dout("y_s", [NS, D])
    kvc_s = dout("kvc_s", [2, NS, 256])
    kvs_s = dout("kvs_s", [2, NS, 256])
    kvw_s = dout("kvw_s", [2, NS, 512, 256])
    pool_s = dout("pool_s", [2, NS, 15, 256])
    gv_s = dout("gv_s", [2, NS, 256])
    xp1 = nc.dram_tensor("xp1_scratch", [T, D], F32, kind="Internal").ap()
    bounce = nc.dram_tensor("bounce_scratch", [8, 4096], F32, kind="Internal").ap()
    xs1 = nc.dram_tensor("xs1_scratch", [NS, D], F32, kind="Internal").ap()

    with ExitStack() as es:
        S = Sched(nc, es)
        uid = [0]

        def sb(shape, dt, name, stack=None):
            uid[0] += 1
            nm = f"{name}_{uid[0]}"
            t = (stack or es).enter_context(nc.sbuf_tensor(nm, list(shape), dt))
            return TB(t, nm)

        def ps(shape, dt, name):
            uid[0] += 1
            nm = f"{name}_{uid[0]}"
            t = es.enter_context(nc.psum_tensor(nm, list(shape), dt))
            tb = TB(t, nm)
            tb.b.excl = True
            return tb

        def load(dst_ap, src_ap, bufs, stream=None, reads=(), eng="sp"):
            stream = "L_" + bufs[0].name
            S.op(eng, lambda e: e.dma_start(out=dst_ap, in_=src_ap), reads=reads, writes=bufs, dma=stream)

        def store(dst_ap, src_ap, bufs, stream, writes=(), eng="pool"):
            S.op(eng, lambda e: e.dma_start(out=dst_ap, in_=src_ap), reads=bufs, writes=writes, dma=stream)

        TP = ps([128, 1024], BF16, "TP")
        PJ = [ps([128, 512], F32, "PJ0"), ps([128, 512], F32, "PJ1")]
        ST = [ps([128, 512], F32, "ST0"), ps([128, 512], F32, "ST1")]
        OA = ps([128, 512], F32, "OA")
        MK = ps([128, 4, 128], F32, "MK")
        MKb = [MK.b for j in range(4)]
        MS = ps([128, 512], F32, "MS")
        TPF = TP

        ident = sb([128, 128], BF16, "ident")
        identf = sb([128, 128], F32, "identf")
        trile = sb([128, 128], BF16, "trile")
        trigt = sb([128, 128], BF16, "trigt")
        alb = sb([128, 512], F32, "alb")
        iota64 = sb([128, 64], F32, "iota64")
        cmisc = sb([128, 8], F32, "cmisc")
        force0 = sb([128, 64], F32, "force0")
        trilT = sb([128, 128], F32, "trilT")
        for tb, src in [(ident, c_ident), (identf, c_identf), (trile, c_trile), (trigt, c_trigt), (alb, c_alb),
                        (iota64, c_iota), (cmisc, c_misc), (force0, c_force0), (trilT, c_trilT)]:
            load(tb[:], src, [tb.b], "const")

        WIN = sb([128, 8, DIN], BF16, "WIN")
        WOUT = sb([128, 8, D], BF16, "WOUT")
        GP = sb([128, D], F32, "GP")
        NPRE = sb([128, 8], F32, "NPRE")
        PETOK = sb([128, 256], F32, "PETOK")
        W1BD = sb([128, 2, 128], BF16, "W1BD")
        W2KV = sb([128, 4, 64], BF16, "W2KV")
        WPOOL = sb([64, 4, 64], BF16, "WPOOL")
        PSC = sb([128, 256], F32, "PSC")
        GNR = sb([128, 256], F32, "GNR")
        WT = sb([128, 4, 128], BF16, "WT")
        BST = sb([128, 4], F32, "BST")
        WS00 = sb([128, 4], F32, "WS00")
        BS0 = sb([128, 4], F32, "BS0")
        WSTG0 = sb([128, 1032], F32, "WSTG0")
        WSTG = [WSTG0, WSTG0]
        SMALLF = WSTG0

        def load_weights(l):
            load(GP[:], gpost_rep[l], [GP.b], "wsmall")
            load(PETOK[:], pe_tok[l], [PETOK.b], "wsmall")
            load(PSC[:], pscale_rep[l], [PSC.b], "wsmall")
            load(GNR[:], gnorm_rep[l], [GNR.b], "wsmall")
            load(BST[:], bsT[l], [BST.b], "wsmall")
            load(WS00[:], ws00_rep[l], [WS00.b], "wsmall")
            load(BS0[:], bs0_rep[l], [BS0.b], "wsmall")
            load(SMALLF[:, 0:256], w1bd[l].rearrange("p r e -> p (r e)"), [SMALLF.b], "wsmall2")
            S.op("dve", lambda e: e.tensor_copy(out=W1BD[:].rearrange("p r e -> p (r e)"), in_=SMALLF[:, 0:256]),
                 reads=[SMALLF.b], writes=[W1BD.b])
            load(SMALLF[:, 0:256], w2kv[l].rearrange("p r e -> p (r e)"), [SMALLF.b], "wsmall2")
            S.op("dve", lambda e: e.tensor_scalar(out=W2KV[:].rearrange("p r e -> p (r e)"), in0=SMALLF[:, 0:256],
                                                  scalar1=1.0 / 128.0, scalar2=None, op0=ALU.mult),
                 reads=[SMALLF.b], writes=[W2KV.b])
            load(SMALLF[0:64, 0:256], wpool[l].rearrange("p r e -> p (r e)"), [SMALLF.b], "wsmall2")
            S.op("dve", lambda e: e.tensor_copy(out=WPOOL[:].rearrange("p r e -> p (r e)"), in_=SMALLF[0:64, 0:256]),
                 reads=[SMALLF.b], writes=[WPOOL.b])
            load(SMALLF[:, 0:512], wsT[l].rearrange("p g i -> p (g i)"), [SMALLF.b], "wsmall2")
            S.op("dve", lambda e: e.tensor_tensor(out=WT[:], in0=SMALLF[:, 0:512].rearrange("p (g i) -> p g i", g=4),
                                                  in1=trilT[:].unsqueeze(1).to_broadcast([128, 4, 128]), op=ALU.mult),
                 reads=[SMALLF.b, trilT.b], writes=[WT.b])
            if l not in win_loaded:
                load_win(l)
            for c in range(8):
                stg = WSTG0
                load(stg[:, 0:1024], w_out[l, c * 128:(c + 1) * 128, :], [stg.b])
                S.op("act", lambda e, stg=stg, c=c: e.mul(out=WOUT[:, c, :], in_=stg[:, 0:1024], mul=0.5),
                     reads=[stg.b], writes=[WOUT.b])

        win_loaded = set()

        def load_win(l):
            win_loaded.add(l)
            load(NPRE[:], npre_t[l], [NPRE.b])
            for c in range(8):
                for (c0, c1) in [(0, 1032), (1032, 2064), (2064, 3096)]:
                    stg = WSTG0
                    load(stg[:, 0:c1 - c0], w_in[l, c * 128:(c + 1) * 128, c0:c1], [stg.b])
                    if c0 == 0:
                        S.op("dve", lambda e, stg=stg, c=c: e.tensor_scalar(
                            out=WIN[:, c, 0:512], in0=stg[:, 0:512], scalar1=NPRE[:, c:c + 1], scalar2=0.125,
                            op0=ALU.mult, op1=ALU.mult), reads=[stg.b, NPRE.b], writes=[WIN.b])
                        S.op("dve", lambda e, stg=stg, c=c: e.tensor_scalar(
                            out=WIN[:, c, 512:1032], in0=stg[:, 512:1032], scalar1=NPRE[:, c:c + 1], scalar2=None,
                            op0=ALU.mult), reads=[stg.b, NPRE.b], writes=[WIN.b])
                    else:
                        S.op("dve", lambda e, stg=stg, c=c, c0=c0, c1=c1: e.tensor_scalar(
                            out=WIN[:, c, c0:c1], in0=stg[:, 0:c1 - c0], scalar1=NPRE[:, c:c + 1], scalar2=None,
                            op0=ALU.mult), reads=[stg.b, NPRE.b], writes=[WIN.b])

        X = [sb([128, D], F32, "X0"), None, None]
        XS = sb([128, D], BF16, "XS")
        XT = sb([128, 8, 128], BF16, "XT")
        SMALL = sb([128, 64], F32, "SMALL")
        QB = sb([128, 512], BF16, "QB")
        KVF = sb([128, 768], F32, "KVF")
        GL = sb([128, 24], F32, "GL")
        GTL = [sb([128, 24], F32, "GT0"), None]
        SZL = [sb([128, D], F32, "SZ0"), None]
        PIN = sb([128, 256], F32, "PIN")
        UV = sb([128, 512], F32, "UV")
        MIXL = [sb([128, D], BF16, "MIX0"), None]
        MT = sb([128, 8, 128], BF16, "MT")
        Y = sb([128, D], F32, "Y")
        YAL = [sb([128, 512], F32, "YA0"), None]
        VN = sb([128, 256], F32, "VN")
        VNB = sb([128, 256], BF16, "VNB")
        YC = sb([128, 256], F32, "YC")
        TMPB = sb([128, 256], F32, "TMPB")
        SQ = TMPB
        JUNK = sb([128, D], BF16, "JUNK")

        def run(gen):
            n = 0
            for _ in gen:
                n += 1
            return n

        def run2(a, na, b, nb):
            live = [a, b]
            cnt = {id(a): 0, id(b): 0}
            while live:
                for g_ in list(live):
                    try:
                        next(g_)
                        cnt[id(g_)] += 1
                    except StopIteration:
                        live.remove(g_)
            return cnt[id(a)], cnt[id(b)]

        def rstd_from_ss(ss_ap, out_ap, n, M, bufs_r, bufs_w, scratch):
            S.op("dve", lambda e: e.tensor_scalar(out=scratch, in0=ss_ap, scalar1=1.0 / n, scalar2=EPS,
                                                  op0=ALU.mult, op1=ALU.add), reads=bufs_r, writes=bufs_w)
            w = scratch.shape[1]
            S.op("pool", lambda e: e.tensor_tensor(out=out_ap, in0=scratch, in1=cmisc[0:M, 2:3].to_broadcast([M, w]), op=ALU.pow),
                 reads=list(bufs_w) + [cmisc.b], writes=bufs_w)

        def project(M, xt, par=0):
            GT = GTL[par]
            SZ = SZL[par]
            S.op("act", lambda e: e.activation(out=JUNK[0:M, :], in_=xt[0:M, :], func=AF.Square),
                 reads=[xt.b], writes=[JUNK.b])
            S.op("dve", lambda e: e.tensor_reduce(out=SMALL[0:M, 0:1], in_=JUNK[0:M, :], axis=AX.X, op=ALU.add),
                 reads=[JUNK.b], writes=[SMALL.b])
            rstd_from_ss(SMALL[0:M, 0:1], SMALL[0:M, 2:3], float(D), M, [SMALL.b], [SMALL.b], SMALL[0:M, 1:2])
            S.op("dve", lambda e: e.tensor_scalar(out=XS[0:M, :], in0=xt[0:M, :], scalar1=SMALL[0:M, 2:3],
                                                  scalar2=None, op0=ALU.mult), reads=[xt.b, SMALL.b], writes=[XS.b])
            yield
            for c in range(8):
                S.op("pe", lambda e, c=c: e.transpose(out=TP[:, c * 128:c * 128 + M], in_=XS[0:M, c * 128:(c + 1) * 128],
                                                      identity=ident[0:M, 0:M]),
                     reads=[XS.b, ident.b], writes=[TP.b])
            S.op("act", lambda e: e.copy(out=XT[:, :, 0:M], in_=TP[:].rearrange("p (c m) -> p c m", c=8)[:, :, 0:M]),
                 reads=[TP.b], writes=[XT.b])
            yield
            for gi, (c0, c1) in enumerate(PG):
                if gi > 0:
                    yield
                pj = PJ[gi % 2]
                w = c1 - c0
                for c in range(8):
                    S.op("pe", lambda e, c=c, pj=pj, c0=c0, c1=c1, w=w: e.matmul(
                        pj[0:M, 0:w], lhsT=XT[:, c, 0:M], rhs=WIN[:, c, c0:c1], start=(c == 0), stop=(c == 7)),
                         reads=[XT.b, WIN.b], writes=[pj.b])
                if gi == 0:
                    S.op("act", lambda e, pj=pj: e.copy(out=QB[0:M, :], in_=pj[0:M, 0:512]), reads=[pj.b], writes=[QB.b])
                elif gi == 1:
                    S.op("dve", lambda e, pj=pj: e.tensor_copy(out=KVF[0:M, 0:512], in_=pj[0:M, 0:512]),
                         reads=[pj.b], writes=[KVF.b])
                elif gi == 2:
                    S.op("dve", lambda e, pj=pj: e.tensor_copy(out=KVF[0:M, 512:768], in_=pj[0:M, 0:256]),
                         reads=[pj.b], writes=[KVF.b])
                    S.op("dve", lambda e, pj=pj: e.tensor_copy(out=GL[0:M, :], in_=pj[0:M, 256:280]),
                         reads=[pj.b], writes=[GL.b])
                elif gi == 3:
                    S.op("act", lambda e, pj=pj: e.activation(out=SZ[0:M, 0:512], in_=pj[0:M, 0:512], func=AF.Tanh, scale=0.5),
                         reads=[pj.b], writes=[SZ.b])
                    S.op("dve", lambda e, pj=pj: e.scalar_tensor_tensor(out=SZ[0:M, 0:512], in0=SZ[0:M, 0:512], scalar=1.0,
                                                                        in1=pj[0:M, 0:512], op0=ALU.add, op1=ALU.mult),
                         reads=[pj.b, SZ.b], writes=[SZ.b])
                elif gi == 4:
                    S.op("dve", lambda e, pj=pj: e.tensor_copy(out=PIN[0:M, :], in_=pj[0:M, 0:256]),
                         reads=[pj.b], writes=[PIN.b])
                    S.op("act", lambda e, pj=pj: e.activation(out=SZ[0:M, 512:768], in_=pj[0:M, 256:512], func=AF.Tanh, scale=0.5),
                         reads=[pj.b], writes=[SZ.b])
                    S.op("dve", lambda e, pj=pj: e.scalar_tensor_tensor(out=SZ[0:M, 512:768], in0=SZ[0:M, 512:768], scalar=1.0,
                                                                        in1=pj[0:M, 256:512], op0=ALU.add, op1=ALU.mult),
                         reads=[pj.b, SZ.b], writes=[SZ.b])
                elif gi == 5:
                    S.op("dve", lambda e, pj=pj: e.tensor_copy(out=UV[0:M, :], in_=pj[0:M, 0:512]),
                         reads=[pj.b], writes=[UV.b])
                else:
                    S.op("act", lambda e, pj=pj: e.activation(out=SZ[0:M, 768:1024], in_=pj[0:M, 0:256], func=AF.Tanh, scale=0.5),
                         reads=[pj.b], writes=[SZ.b])
                    S.op("dve", lambda e, pj=pj: e.scalar_tensor_tensor(out=SZ[0:M, 768:1024], in0=SZ[0:M, 768:1024], scalar=1.0,
                                                                        in1=pj[0:M, 0:256], op0=ALU.add, op1=ALU.mult),
                         reads=[pj.b, SZ.b], writes=[SZ.b])
            yield
            S.op("act", lambda e: e.activation(out=GT[0:M, :], in_=GL[0:M, :], func=AF.Exp, scale=-1.0),
                 reads=[GL.b], writes=[GT.b])
            S.op("dve", lambda e: e.tensor_scalar(out=GT[0:M, :], in0=GT[0:M, :], scalar1=1.0, scalar2=None, op0=ALU.add),
                 reads=[GT.b], writes=[GT.b])
            S.op("dve", lambda e: e.reciprocal(out=GT[0:M, :], in_=GT[0:M, :]), reads=[GT.b], writes=[GT.b])

        def gmlp_norm(M):
            v = UV[0:M, 256:512]
            S.op("dve", lambda e: e.tensor_tensor(out=SQ[0:M, :], in0=v, in1=v, op=ALU.mult), reads=[UV.b], writes=[SQ.b])
            S.op("dve", lambda e: e.tensor_reduce(out=SMALL[0:M, 8:12], in_=SQ[0:M, :].rearrange("p (g c) -> p g c", g=4),
                                                  axis=AX.X, op=ALU.add), reads=[SQ.b], writes=[SMALL.b])
            rstd_from_ss(SMALL[0:M, 8:12], SMALL[0:M, 16:20], 64.0, M, [SMALL.b], [SMALL.b], SMALL[0:M, 12:16])
            S.op("dve", lambda e: e.tensor_tensor(out=VN[0:M, :].rearrange("p (g c) -> p g c", g=4),
                                                  in0=v.rearrange("p (g c) -> p g c", g=4),
                                                  in1=SMALL[0:M, 16:20].unsqueeze(2).to_broadcast([M, 4, 64]), op=ALU.mult),
                 reads=[UV.b, SMALL.b], writes=[VN.b])
            S.op("dve", lambda e: e.tensor_tensor(out=VN[0:M, :], in0=VN[0:M, :], in1=GNR[0:M, :], op=ALU.mult),
                 reads=[VN.b, GNR.b], writes=[VN.b])

        def merge_out(M, xt, dst_ap, dst_bufs, stream, par=0):
            MIX = MIXL[par]
            for c in range(8):
                S.op("pe", lambda e, c=c: e.transpose(out=TP[:, c * 128:c * 128 + M], in_=MIX[0:M, c * 128:(c + 1) * 128],
                                                      identity=ident[0:M, 0:M]),
                     reads=[MIX.b, ident.b], writes=[TP.b])
            S.op("act", lambda e: e.copy(out=MT[:, :, 0:M], in_=TP[:].rearrange("p (c m) -> p c m", c=8)[:, :, 0:M]),
                 reads=[TP.b], writes=[MT.b])
            yield
            for hf in range(2):
                for c in range(8):
                    S.op("pe", lambda e, c=c, hf=hf: e.matmul(PJ[hf][0:M, :], lhsT=MT[:, c, 0:M],
                                                               rhs=WOUT[:, c, hf * 512:(hf + 1) * 512],
                                                               start=(c == 0), stop=(c == 7)),
                         reads=[MT.b, WOUT.b], writes=[PJ[hf].b])
            for hf in range(2):
                S.op("act", lambda e, hf=hf: e.activation(out=JUNK[0:M, hf * 512:(hf + 1) * 512], in_=PJ[hf][0:M, :],
                                                          func=AF.Square), reads=[PJ[hf].b], writes=[JUNK.b])
            S.op("dve", lambda e: e.tensor_reduce(out=SMALL[0:M, 24:25], in_=JUNK[0:M, :], axis=AX.X, op=ALU.add),
                 reads=[JUNK.b], writes=[SMALL.b])
            rstd_from_ss(SMALL[0:M, 24:25], SMALL[0:M, 26:27], float(D), M, [SMALL.b], [SMALL.b], SMALL[0:M, 25:26])
            for hf in range(2):
                sl = slice(hf * 512, (hf + 1) * 512)
                S.op("dve", lambda e, hf=hf, sl=sl: e.scalar_tensor_tensor(
                    out=Y[0:M, sl], in0=PJ[hf][0:M, :], scalar=SMALL[0:M, 26:27], in1=GP[0:M, sl],
                    op0=ALU.mult, op1=ALU.mult), reads=[PJ[hf].b, SMALL.b, GP.b], writes=[Y.b])
            S.op("pool", lambda e: e.tensor_tensor(out=Y[0:M, :], in0=Y[0:M, :], in1=xt[0:M, :], op=ALU.add),
                 reads=[Y.b, xt.b], writes=[Y.b])
            store(dst_ap, Y[0:M, :], [Y.b], stream, writes=dst_bufs)
            yield

        XP1b = [Buf(f"xp1_{i}") for i in range(NT)]

        def prompt_layer(l, pstack):
            X[1] = sb([128, D], F32, "X1", pstack)
            X[2] = sb([128, D], F32, "X2", pstack)
            JUNK2 = sb([128, D], BF16, "JUNK2", pstack)
            GTL[1] = sb([128, 24], F32, "GT1", pstack)
            SZL[1] = sb([128, D], F32, "SZ1", pstack)
            MIXL[1] = sb([128, D], BF16, "MIX1", pstack)
            YAL[1] = sb([128, 512], F32, "YA1", pstack)
            KT = sb([100, 4, T], BF16, "KT", pstack)
            KTb = [Buf(f"KT{i}") for i in range(NT)]
            KTaug = Buf("KTaug")
            VS = sb([128, NT, 2, 65], BF16, "VS", pstack)
            VW = sb([128, 8, 2, 65], BF16, "VW", pstack)
            VSb = [Buf(f"VS{i}") for i in range(NT)]
            VWb = [Buf(f"VW{i}") for i in range(8)]
            NMP = sb([128, 2, 2, 96], BF16, "NMP", pstack)
            POOLB = sb([128, 3, 4, 128], BF16, "POOLB", pstack)
            HBT = sb([128, 2, 64], BF16, "HBT", pstack)
            KCc = sb([64, 2, 64], BF16, "KCc", pstack)
            VCc = sb([64, 2, 64], BF16, "VCc", pstack)
            QA = [sb([100, 2, 2, 512], BF16, "QA0", pstack), sb([100, 2, 2, 512], BF16, "QA1", pstack)]
            QAaug = [Buf("QAaug0"), Buf("QAaug1")]
            KBF = sb([128, 2, 128], BF16, "KBF", pstack)
            KCP = sb([128, 256], BF16, "KCP", pstack)
            KCT = sb([128, 2, 128], BF16, "KCT", pstack)
            HID = sb([128, 2, 128], F32, "HID", pstack)
            SC = sb([128, 8, 64], F32, "SC", pstack)
            PC = SC
            PCB = sb([128, 8, 64], BF16, "PCB", pstack)
            PCT = sb([64, 8, 128], BF16, "PCT", pstack)
            IMP = sb([128, 2, 64], F32, "IMP", pstack)
            SCR = sb([128, 2, 64], F32, "SCR", pstack)
            WK1 = sb([128, 64], F32, "WK1", pstack)
            WK2 = sb([128, 64], F32, "WK2", pstack)
            M8 = sb([128, 8], F32, "M8", pstack)
            SEL = sb([128, 2, 64], BF16, "SEL", pstack)
            SELTL = [sb([64, 2, 128], BF16, "SELT0", pstack), sb([64, 2, 128], BF16, "SELT1", pstack)]
            VIS = sb([128, 64], F32, "VIS", pstack)
            NVIS = sb([128, 64], F32, "NVIS", pstack)
            STT = sb([128, 64], F32, "STT", pstack)
            ADDT = sb([128, 64], F32, "ADDT", pstack)
            PT = [sb([128, 512], BF16, f"PT{j}", pstack) for j in range(3)]
            MKD = sb([128, 128], BF16, "MKD", pstack)
            OSL = [sb([128, 4, 65], F32, "OS0", pstack)]
            CSL = [sb([128, 8], F32, "CS0", pstack)]
            oac = [0]
            PINB = [sb([128, 256], BF16, "PINB0", pstack), sb([128, 256], BF16, "PINB1", pstack)]
            DT = sb([64, 4, 128], BF16, "DT", pstack)

            load(POOLB[:], c_poolb, [POOLB.b], "const")
            for j in range(4):
                load(KT[96:100, j, :], c_kaug, [KTaug], "const")
            load(KT[64:96, 0, :], c_kmask, [KTaug], "const")
            load(KT[64:96, 1, :], c_kmask, [KTaug], "const")
            S.op("dve", lambda e: e.memset(KT[64:96, 2:4, :], 0.0), writes=[KTaug])
            S.op("dve", lambda e: e.memset(NMP[:], 0.0), writes=[NMP.b])
            for qq in QA:
                S.op("dve", lambda e, qq=qq: e.memset(qq[64:96, :, :, :], 0.0), writes=[qq.b])
            S.op("dve", lambda e: e.memset(VS[:], 1.0), writes=VSb)
            S.op("dve", lambda e: e.memset(VW[:], 1.0), writes=VWb)
            S.op("dve", lambda e: e.memset(HBT[:], 0.0), writes=[HBT.b])
            ptc = [0]
            mkc = [0]
            stc = [0]

            def attend(i, kv, branch, qa, qab):
                par = i % 2
                GT, YA, SELT = GTL[par], YAL[par], SELTL[par]
                og = 0
                OS, CS = OSL[og], CSL[og]
                oa_ap = OA[:, :] if og == 0 else MK[:].rearrange("p a b -> p (a b)")
                oa_b = OA.b if og == 0 else MK.b
                if branch == 0:
                    kts = list(range(0, i + 1))
                    slot, Vt, Vb = kv, VS, VSb
                    vsl = lambda kt: kt
                else:
                    kts = list(range(max(0, i - 4), i + 1))
                    slot, Vt, Vb = 2 + kv, VW, VWb
                    vsl = lambda kt: kt % 8
                n = len(kts)
                pts = [None] * n

                def s1a(idx):
                    kt = kts[idx]
                    st = ST[stc[0] % 2]
                    stc[0] += 1
                    hf = (kt // 16) if branch == 0 else 0
                    S.op("pe", lambda e, st=st, kt=kt, hf=hf: e.matmul(st[:, :], lhsT=KT[0:100, slot, kt * 128:(kt + 1) * 128],
                                                                        rhs=qa[0:100, kv, hf, :], start=True, stop=True),
                         reads=[KTb[kt], KTaug, qa.b, qab], writes=[st.b])
                    pt = PT[ptc[0] % len(PT)]
                    ptc[0] += 1
                    pts[idx] = pt
                    S.op("act", lambda e, st=st, pt=pt: e.activation(out=pt[:], in_=st[:, :], func=AF.Exp),
                         reads=[st.b], writes=[pt.b])

                def s1b(idx):
                    kt = kts[idx]
                    pt = pts[idx]
                    mask_ap = None
                    mreads = []
                    if branch == 0:
                        if kt == i:
                            mask_ap = trile[:]
                            mreads = [trile.b]
                    else:
                        if kt == i:
                            mask_ap = trile[:]
                            mreads = [trile.b]
                        elif kt == i - 4:
                            mask_ap = trigt[:]
                            mreads = [trigt.b]
                    if mask_ap is not None:
                        S.op("dve", lambda e, pt=pt, mask_ap=mask_ap: e.tensor_tensor(
                            out=pt[:].rearrange("p (g q) -> p g q", g=4), in0=pt[:].rearrange("p (g q) -> p g q", g=4),
                            in1=mask_ap.unsqueeze(1).to_broadcast([128, 4, 128]), op=ALU.mult),
                             reads=[pt.b] + mreads, writes=[pt.b])

                def s2(idx):
                    kt = kts[idx]
                    pt = pts[idx]
                    for g in range(4):
                        S.op("pe", lambda e, pt=pt, g=g, kt=kt, idx=idx: e.matmul(
                            oa_ap[:, g * 65:(g + 1) * 65], lhsT=pt[:, g * 128:(g + 1) * 128], rhs=Vt[:, vsl(kt), kv, :],
                            start=(idx == 0 and g == 0), stop=(idx == n - 1 and g == 3)),
                             reads=[pt.b, Vb[vsl(kt)]], writes=[oa_b])

                s1a(0)
                if n > 1:
                    s1a(1)
                s1b(0)
                for idx in range(n):
                    if idx + 2 < n:
                        s1a(idx + 2)
                    if idx + 1 < n:
                        s1b(idx + 1)
                    s2(idx)
                    yield
                S.op("act", lambda e: e.copy(out=OS[:].rearrange("p g d -> p (g d)"), in_=oa_ap[:, 0:260]),
                     reads=[oa_b], writes=[OS.b])
                gcol = 1 + branch
                S.op("dve", lambda e: e.reciprocal(out=CS[:, 0:4], in_=OS[:, :, 64]), reads=[OS.b], writes=[CS.b])
                S.op("dve", lambda e: e.tensor_tensor(
                    out=CS[:, 4:8], in0=CS[:, 0:4],
                    in1=GT[:, :].rearrange("p (h t) -> p h t", t=3)[:, kv * 4:(kv + 1) * 4, gcol], op=ALU.mult),
                     reads=[CS.b, GT.b], writes=[CS.b])
                for g in range(4):
                    h = kv * 4 + g
                    S.op("dve", lambda e, g=g, h=h: e.scalar_tensor_tensor(
                        out=YA[:, h * 64:(h + 1) * 64], in0=OS[:, g, 0:64], scalar=CS[:, 4 + g:5 + g],
                        in1=YA[:, h * 64:(h + 1) * 64], op0=ALU.mult, op1=ALU.add),
                         reads=[OS.b, CS.b, YA.b], writes=[YA.b])

            def front(i):
                par = i % 2
                GT, SZ, MIX, YA, SELT = GTL[par], SZL[par], MIXL[par], YAL[par], SELTL[par]
                xt = X[i % 3]
                src = xp if l == 0 else xp1
                load(xt[:], src[i * 128:(i + 1) * 128, :], [xt.b], f"x{i % 2}", reads=([XP1b[i]] if l == 1 else []))
                yield from project(128, xt, par)
                yield
                store(kvc_p[l, i * 128:(i + 1) * 128, :], KVF[:, 0:256], [KVF.b], "o_kvf")
                store(kvs_p[l, i * 128:(i + 1) * 128, :], KVF[:, 256:512], [KVF.b], "o_kvf")
                if i >= NT - 4:
                    j = i - (NT - 4)
                    store(kvw_p[l, j * 128:(j + 1) * 128, :], KVF[:, 512:768], [KVF.b], "o_kvf")
                if i == NT - 1:
                    store(pool_p[l], PIN[113:128, :], [PIN.b], "o_pin")
                yield
                S.op("pool", lambda e: e.tensor_copy(out=KBF[:, 0, :], in_=KVF[:, 256:384]), reads=[KVF.b], writes=[KBF.b])
                S.op("pool", lambda e: e.tensor_copy(out=KBF[:, 1, :], in_=KVF[:, 512:640]), reads=[KVF.b], writes=[KBF.b])
                S.op("pool", lambda e, i=i: e.tensor_copy(out=VS[:, i, :, 0:64],
                                                           in_=KVF[:, 384:512].rearrange("p (k d) -> p k d", k=2)),
                     reads=[KVF.b], writes=[VSb[i]])
                S.op("pool", lambda e, i=i: e.tensor_copy(out=VW[:, i % 8, :, 0:64],
                                                           in_=KVF[:, 640:768].rearrange("p (k d) -> p k d", k=2)),
                     reads=[KVF.b], writes=[VWb[i % 8]])
                for j in range(4):
                    S.op("pe", lambda e, j=j: e.transpose(out=TP[0:64, j * 128:(j + 1) * 128],
                                                          in_=KBF[:, j // 2, (j % 2) * 64:(j % 2) * 64 + 64], identity=ident[:]),
                         reads=[KBF.b, ident.b], writes=[TP.b])
                S.op("act", lambda e, i=i: e.copy(out=KT[0:64, :, i * 128:(i + 1) * 128],
                                                  in_=TP[0:64, 0:512].rearrange("p (j t) -> p j t", j=4)),
                     reads=[TP.b], writes=[KTb[i]])
                yield
                qa = QA[i % 2]
                qab = QAaug[i % 2]
                load(qa[96:100, :, :, :].rearrange("p k h n -> p (k h n)"), c_qaug[i], [qab], f"qaug{i % 2}", reads=[qa.b])
                for h in range(8):
                    S.op("pe", lambda e, h=h: e.transpose(out=TP[0:64, h * 128:(h + 1) * 128], in_=QB[:, h * 64:(h + 1) * 64],
                                                          identity=ident[:]), reads=[QB.b, ident.b], writes=[TP.b])
                S.op("act", lambda e, qa=qa: e.copy(
                    out=qa[0:64, :, :, :], in_=TP[0:64, :].rearrange("p (k n) -> p k n", k=2).unsqueeze(2).to_broadcast([64, 2, 2, 512])),
                     reads=[TP.b, qab], writes=[qa.b])
                yield
                S.op("dve", lambda e: e.tensor_tensor(out=KCP[:], in0=KVF[:, 0:256], in1=PETOK[:], op=ALU.add),
                     reads=[KVF.b, PETOK.b], writes=[KCP.b])
                for r in range(2):
                    S.op("pe", lambda e, r=r: e.transpose(out=TP[:, r * 128:(r + 1) * 128], in_=KCP[:, r * 128:(r + 1) * 128],
                                                          identity=ident[:]), reads=[KCP.b, ident.b], writes=[TP.b])
                S.op("act", lambda e: e.copy(out=KCT[:].rearrange("p r t -> p (r t)"), in_=TP[:, 0:256]),
                     reads=[TP.b], writes=[KCT.b])
                for r in range(2):
                    S.op("pe", lambda e, r=r: e.matmul(MS[:, r * 128:(r + 1) * 128], lhsT=W1BD[:, r, :], rhs=KCT[:, r, :],
                                                       start=True, stop=True), reads=[W1BD.b, KCT.b], writes=[MS.b])
                S.op("act", lambda e: e.activation(out=HID[:].rearrange("p r t -> p (r t)"), in_=MS[:, 0:256], func=AF.Tanh, scale=0.5),
                     reads=[MS.b], writes=[HID.b])
                S.op("dve", lambda e: e.scalar_tensor_tensor(out=HID[:].rearrange("p r t -> p (r t)"),
                                                             in0=HID[:].rearrange("p r t -> p (r t)"), scalar=1.0, in1=MS[:, 0:256],
                                                             op0=ALU.add, op1=ALU.mult), reads=[MS.b, HID.b], writes=[HID.b])
                S.op("dve", lambda e: e.tensor_reduce(out=SMALL[:, 34:38].rearrange("p (r b) -> p r b", r=2),
                                                      in_=HID[:].rearrange("p r (b c) -> p r b c", b=2),
                                                      axis=AX.X, op=ALU.add), reads=[HID.b], writes=[SMALL.b])
                S.op("dve", lambda e, i=i: e.tensor_copy(out=HBT[:, :, 2 * i:2 * i + 2],
                                                         in_=SMALL[:, 34:38].rearrange("p (r b) -> p r b", r=2)),
                     reads=[SMALL.b], writes=[HBT.b])
                for kv in range(2):
                    S.op("pe", lambda e, kv=kv: e.matmul(MS[0:64, kv * 64:(kv + 1) * 64], lhsT=W2KV[:, kv, :], rhs=HBT[:, 0, :],
                                                         start=True, stop=True), reads=[W2KV.b, HBT.b], writes=[MS.b])
                    S.op("pe", lambda e, kv=kv: e.matmul(MS[0:64, 128 + kv * 64:128 + (kv + 1) * 64], lhsT=HBT[:, 1, :],
                                                         rhs=W2KV[:, 2 + kv, :], start=True, stop=True),
                         reads=[W2KV.b, HBT.b], writes=[MS.b])
                S.op("act", lambda e: e.copy(out=KCc[:].rearrange("p k n -> p (k n)"), in_=MS[0:64, 0:128]),
                     reads=[MS.b], writes=[KCc.b])
                S.op("act", lambda e: e.copy(out=VCc[:].rearrange("p k n -> p (k n)"), in_=MS[0:64, 128:256]),
                     reads=[MS.b], writes=[VCc.b])
                yield
                for h in range(8):
                    kv, g = h // 4, h % 4
                    S.op("pe", lambda e, h=h, kv=kv, g=g, qa=qa: e.matmul(MS[:, h * 64:(h + 1) * 64],
                                                                   lhsT=qa[0:64, kv, 0, g * 128:(g + 1) * 128],
                                                                   rhs=KCc[:, kv, :], start=True, stop=True),
                         reads=[qa.b, KCc.b], writes=[MS.b])
                S.op("dve", lambda e, i=i: e.tensor_scalar(out=SMALL[:, 32:33], in0=cmisc[:, 0:1], scalar1=float(2 * i),
                                                           scalar2=None, op0=ALU.add), reads=[cmisc.b], writes=[SMALL.b])
                S.op("dve", lambda e: e.tensor_scalar(out=VIS[:], in0=iota64[:], scalar1=SMALL[:, 32:33], scalar2=None,
                                                      op0=ALU.is_le), reads=[iota64.b, SMALL.b], writes=[VIS.b])
                S.op("dve", lambda e: e.tensor_scalar(out=NVIS[:], in0=VIS[:], scalar1=-1.0, scalar2=-NEG,
                                                      op0=ALU.add, op1=ALU.mult), reads=[VIS.b], writes=[NVIS.b])
                S.op("dve", lambda e: e.tensor_tensor(out=SC[:].rearrange("p h n -> p (h n)"), in0=MS[:, :], in1=alb[:], op=ALU.add),
                     reads=[MS.b, alb.b], writes=[SC.b])
                yield
                S.op("dve", lambda e: e.tensor_tensor(out=SC[:], in0=SC[:], in1=NVIS[:].unsqueeze(1).to_broadcast([128, 8, 64]),
                                                      op=ALU.add), reads=[SC.b, NVIS.b], writes=[SC.b])
                yield
                S.op("dve", lambda e: e.tensor_reduce(out=SMALL[:, 40:48], in_=SC[:], axis=AX.X, op=ALU.max),
                     reads=[SC.b], writes=[SMALL.b])
                yield
                S.op("dve", lambda e: e.tensor_tensor(out=SC[:], in0=SC[:],
                                                      in1=SMALL[:, 40:48].unsqueeze(2).to_broadcast([128, 8, 64]),
                                                      op=ALU.subtract), reads=[SC.b, SMALL.b], writes=[SC.b])
                yield
                S.op("act", lambda e: e.activation(out=PC[:].rearrange("p h n -> p (h n)"),
                                                   in_=SC[:].rearrange("p h n -> p (h n)"), func=AF.Exp),
                     reads=[SC.b], writes=[PC.b])
                yield
                S.op("dve", lambda e: e.tensor_tensor(out=PC[:], in0=PC[:], in1=VIS[:].unsqueeze(1).to_broadcast([128, 8, 64]),
                                                      op=ALU.mult), reads=[PC.b, VIS.b], writes=[PC.b])
                yield
                S.op("dve", lambda e: e.tensor_reduce(out=SMALL[:, 48:56], in_=PC[:], axis=AX.X, op=ALU.add),
                     reads=[PC.b], writes=[SMALL.b])
                yield
                S.op("dve", lambda e: e.tensor_scalar(out=SMALL[:, 48:56], in0=SMALL[:, 48:56], scalar1=1e-30, scalar2=None,
                                                      op0=ALU.max), reads=[SMALL.b], writes=[SMALL.b])
                yield
                S.op("dve", lambda e: e.reciprocal(out=SMALL[:, 56:64], in_=SMALL[:, 48:56]), reads=[SMALL.b], writes=[SMALL.b])
                yield
                S.op("dve", lambda e: e.tensor_tensor(out=PC[:], in0=PC[:],
                                                      in1=SMALL[:, 56:64].unsqueeze(2).to_broadcast([128, 8, 64]), op=ALU.mult),
                     reads=[PC.b, SMALL.b], writes=[PC.b])
                yield
                S.op("act", lambda e: e.copy(out=PCB[:].rearrange("p h n -> p (h n)"), in_=PC[:].rearrange("p h n -> p (h n)")),
                     reads=[PC.b], writes=[PCB.b])
                yield
                S.op("dve", lambda e: e.tensor_reduce(out=IMP[:], in_=PC[:].rearrange("p (k g) n -> p k n g", k=2),
                                                      axis=AX.X, op=ALU.add), reads=[PC.b], writes=[IMP.b])
                yield
                S.op("dve", lambda e, i=i: e.tensor_scalar(out=SMALL[:, 33:34], in0=cmisc[:, 1:2], scalar1=float(2 * i),
                                                           scalar2=None, op0=ALU.add), reads=[cmisc.b], writes=[SMALL.b])
                yield
                S.op("dve", lambda e: e.tensor_scalar(out=STT[:], in0=iota64[:], scalar1=SMALL[:, 33:34], scalar2=None,
                                                      op0=ALU.is_le), reads=[iota64.b, SMALL.b], writes=[STT.b])
                yield
                S.op("dve", lambda e: e.tensor_scalar(out=ADDT[:], in0=iota64[:], scalar1=SMALL[:, 33:34], scalar2=None,
                                                      op0=ALU.is_equal), reads=[iota64.b, SMALL.b], writes=[ADDT.b])
                yield
                S.op("dve", lambda e: e.tensor_tensor(out=ADDT[:], in0=ADDT[:], in1=STT[:], op=ALU.add),
                     reads=[ADDT.b, STT.b], writes=[ADDT.b])
                yield
                S.op("dve", lambda e: e.tensor_scalar(out=ADDT[:], in0=ADDT[:], scalar1=1e4, scalar2=-1e4,
                                                      op0=ALU.mult, op1=ALU.add), reads=[ADDT.b], writes=[ADDT.b])
                yield
                S.op("dve", lambda e: e.tensor_tensor(out=ADDT[:], in0=ADDT[:], in1=force0[:], op=ALU.add),
                     reads=[ADDT.b, force0.b], writes=[ADDT.b])
                yield
                S.op("dve", lambda e: e.tensor_tensor(out=SCR[:], in0=IMP[:], in1=STT[:].unsqueeze(1).to_broadcast([128, 2, 64]),
                                                      op=ALU.mult), reads=[IMP.b, STT.b], writes=[SCR.b])
                yield
                S.op("dve", lambda e: e.tensor_tensor(out=SCR[:], in0=SCR[:], in1=ADDT[:].unsqueeze(1).to_broadcast([128, 2, 64]),
                                                      op=ALU.add), reads=[SCR.b, ADDT.b], writes=[SCR.b])
                yield
                for kv in range(2):
                    S.op("dve", lambda e, kv=kv: e.max(out=M8[:], in_=SCR[:, kv, :]), reads=[SCR.b], writes=[M8.b])
                    S.op("dve", lambda e, kv=kv: e.match_replace(out=WK1[:], in_to_replace=M8[:], in_values=SCR[:, kv, :],
                                                                 imm_value=NEG), reads=[SCR.b, M8.b], writes=[WK1.b])
                    S.op("dve", lambda e: e.max(out=M8[:], in_=WK1[:]), reads=[WK1.b], writes=[M8.b])
                    S.op("dve", lambda e: e.match_replace(out=WK2[:], in_to_replace=M8[:], in_values=WK1[:], imm_value=NEG),
                         reads=[WK1.b, M8.b], writes=[WK2.b])
                    S.op("dve", lambda e, kv=kv: e.tensor_tensor(out=SEL[:, kv, :], in0=WK2[:], in1=SCR[:, kv, :],
                                                                 op=ALU.not_equal), reads=[WK2.b, SCR.b], writes=[SEL.b])
                yield
                S.op("dve", lambda e: e.tensor_scalar(out=NMP[:, :, :, 64:96], in0=SEL[:].rearrange("p k (h n) -> p k h n", h=2),
                                                      scalar1=-1.0, scalar2=-NEG, op0=ALU.add, op1=ALU.mult),
                     reads=[SEL.b], writes=[NMP.b])
                nhalf = 2 if i >= 16 else 1
                for kv in range(2):
                    for hf in range(nhalf):
                        S.op("pe", lambda e, kv=kv, hf=hf: e.matmul(MS[0:96, (kv * 2 + hf) * 128:(kv * 2 + hf + 1) * 128],
                                                                     lhsT=NMP[:, kv, hf, :], rhs=ident[:], start=True, stop=True),
                             reads=[NMP.b, ident.b], writes=[MS.b])
                for kv in range(2):
                    for hf in range(nhalf):
                        S.op("act", lambda e, kv=kv, hf=hf, qa=qa: e.copy(
                            out=qa[64:96, kv, hf, :].rearrange("p (g q) -> p g q", g=4),
                            in_=MS[64:96, (kv * 2 + hf) * 128:(kv * 2 + hf + 1) * 128].unsqueeze(1).to_broadcast([32, 4, 128])),
                             reads=[MS.b], writes=[qa.b])
                for h in range(8):
                    S.op("pe", lambda e, h=h: e.transpose(out=TP[0:64, h * 128:(h + 1) * 128], in_=PCB[:, h, :],
                                                          identity=ident[:]), reads=[PCB.b, ident.b], writes=[TP.b])
                S.op("act", lambda e: e.copy(out=PCT[:].rearrange("p h q -> p (h q)"), in_=TP[0:64, :]),
                     reads=[TP.b], writes=[PCT.b])
                for h in range(8):
                    S.op("pe", lambda e, h=h: e.matmul(MS[:, h * 64:(h + 1) * 64], lhsT=PCT[:, h, :], rhs=VCc[:, h // 4, :],
                                                       start=True, stop=True), reads=[PCT.b, VCc.b], writes=[MS.b])
                S.op("dve", lambda e: e.tensor_tensor(
                    out=YA[:].rearrange("p (h d) -> p h d", h=8), in0=MS[:, :].rearrange("p (h d) -> p h d", h=8),
                    in1=GT[:, :].rearrange("p (h t) -> p h t", t=3)[:, :, 0:1].to_broadcast([128, 8, 64]), op=ALU.mult),
                     reads=[MS.b, GT.b], writes=[YA.b])
                pb = PINB[i % 2]
                pbp = PINB[(i + 1) % 2]
                S.op("pool", lambda e, pb=pb: e.tensor_copy(out=pb[:], in_=PIN[:]), reads=[PIN.b], writes=[pb.b])
                for g in range(4):
                    var = 2 if i == 0 else 0
                    S.op("pe", lambda e, g=g, pb=pb, var=var, i=i: e.matmul(MS[0:64, g * 128:(g + 1) * 128],
                                                                             lhsT=pb[:, g * 64:(g + 1) * 64],
                                                                             rhs=POOLB[:, var, g, :], start=True, stop=(i == 0)),
                         reads=[pb.b, POOLB.b], writes=[MS.b])
                    if i > 0:
                        S.op("pe", lambda e, g=g, pbp=pbp: e.matmul(MS[0:64, g * 128:(g + 1) * 128],
                                                                     lhsT=pbp[:, g * 64:(g + 1) * 64], rhs=POOLB[:, 1, g, :],
                                                                     start=False, stop=True),
                             reads=[pbp.b, POOLB.b], writes=[MS.b])
                S.op("act", lambda e: e.copy(out=DT[:].rearrange("p g t -> p (g t)"), in_=MS[0:64, :]), reads=[MS.b], writes=[DT.b])
                for g in range(4):
                    S.op("pe", lambda e, g=g: e.matmul(MS[:, g * 64:(g + 1) * 64], lhsT=DT[:, g, :], rhs=WPOOL[:, g, :],
                                                       start=True, stop=True), reads=[DT.b, WPOOL.b], writes=[MS.b])
                S.op("dve", lambda e: e.tensor_tensor(out=TMPB[:], in0=MS[:, 0:256], in1=PSC[:], op=ALU.mult),
                     reads=[MS.b, PSC.b], writes=[TMPB.b])
                S.op("pool", lambda e: e.tensor_tensor(out=MIX[:, 512:768], in0=TMPB[:], in1=SZ[:, 512:768], op=ALU.mult),
                     reads=[TMPB.b, SZ.b], writes=[MIX.b])
                yield
                gmlp_norm(128)
                yield
                if i == NT - 1:
                    store(gv_p[l], VN[:], [VN.b], "o_vn")
                S.op("act", lambda e: e.copy(out=VNB[:], in_=VN[:]), reads=[VN.b], writes=[VNB.b])
                for g in range(4):
                    S.op("pe", lambda e, g=g: e.matmul(MS[:, g * 64:(g + 1) * 64], lhsT=WT[:, g, :], rhs=VNB[:, g * 64:(g + 1) * 64],
                                                       start=True, stop=True), reads=[WT.b, VNB.b], writes=[MS.b])
                for g in range(4):
                    S.op("dve", lambda e, g=g: e.scalar_tensor_tensor(
                        out=YC[:, g * 64:(g + 1) * 64], in0=MS[:, g * 64:(g + 1) * 64], scalar=BST[:, g:g + 1],
                        in1=UV[:, g * 64:(g + 1) * 64], op0=ALU.add, op1=ALU.mult),
                         reads=[MS.b, BST.b, UV.b], writes=[YC.b])
                S.op("pool", lambda e: e.tensor_tensor(out=MIX[:, 768:1024], in0=YC[:], in1=SZ[:, 768:1024], op=ALU.mult),
                     reads=[YC.b, SZ.b], writes=[MIX.b])

            def back(i):
                par = i % 2
                GT, SZ, MIX, YA, SELT = GTL[par], SZL[par], MIXL[par], YAL[par], SELTL[par]
                qa = QA[i % 2]
                qab = QAaug[i % 2]
                for kv in range(2):
                    yield from attend(i, kv, 0, qa, qab)
                    yield from attend(i, kv, 1, qa, qab)
                S.op("dve", lambda e: e.tensor_tensor(out=MIX[:, 0:512], in0=YA[:], in1=SZ[:, 0:512], op=ALU.mult),
                     reads=[YA.b, SZ.b], writes=[MIX.b])

            OB = MK[:].rearrange("p a b -> p (a b)")

            def tail(i):
                par = i % 2
                MIX = MIXL[par]
                xt = X[i % 3]
                for c in range(8):
                    S.op("pe", lambda e, c=c: e.transpose(out=TP[:, c * 128:(c + 1) * 128], in_=MIX[:, c * 128:(c + 1) * 128],
                                                          identity=ident[:]), reads=[MIX.b, ident.b], writes=[TP.b])
                S.op("act", lambda e: e.copy(out=MT[:], in_=TP[:].rearrange("p (c m) -> p c m", c=8)), reads=[TP.b], writes=[MT.b])
                yield
                for hf in range(2):
                    sl = slice(hf * 512, (hf + 1) * 512)
                    for c in range(8):
                        S.op("pe", lambda e, c=c, sl=sl: e.matmul(OB[:, :], lhsT=MT[:, c, :], rhs=WOUT[:, c, sl],
                                                                   start=(c == 0), stop=(c == 7)),
                             reads=[MT.b, WOUT.b], writes=[MK.b])
                    S.op("act", lambda e, sl=sl: e.activation(out=JUNK2[:, sl], in_=OB[:, :], func=AF.Square),
                         reads=[MK.b], writes=[JUNK2.b])
                    S.op("dve", lambda e, sl=sl: e.tensor_copy(out=Y[:, sl], in_=OB[:, :]), reads=[MK.b], writes=[Y.b])
                    yield
                S.op("dve", lambda e: e.tensor_reduce(out=SMALL[:, 24:25], in_=JUNK2[:], axis=AX.X, op=ALU.add),
                     reads=[JUNK2.b], writes=[SMALL.b])
                rstd_from_ss(SMALL[:, 24:25], SMALL[:, 26:27], float(D), 128, [SMALL.b], [SMALL.b], SMALL[:, 25:26])
                yield
                S.op("dve", lambda e: e.scalar_tensor_tensor(out=Y[:], in0=Y[:], scalar=SMALL[:, 26:27], in1=GP[:],
                                                             op0=ALU.mult, op1=ALU.mult), reads=[Y.b, SMALL.b, GP.b], writes=[Y.b])
                yield
                S.op("dve", lambda e: e.tensor_tensor(out=Y[:], in0=Y[:], in1=xt[:], op=ALU.add), reads=[Y.b, xt.b], writes=[Y.b])
                if l == 0 and not dbg_y0:
                    store(xp1[i * 128:(i + 1) * 128, :], Y[:], [Y.b], "o_y", writes=[XP1b[i]])
                else:
                    store(y_p[i * 128:(i + 1) * 128, :], Y[:], [Y.b], "o_y")
                yield

            def run3(gens):
                live = [g_ for g_ in gens if g_ is not None]
                while live:
                    for g_ in list(live):
                        try:
                            next(g_)
                        except StopIteration:
                            live.remove(g_)

            ntiles = min(NT, nt_limit)
            for slot in range(ntiles + 2):
                gens = []
                if 0 <= slot - 2 < ntiles:
                    gens.append(tail(slot - 2))
                if 0 <= slot - 1 < ntiles:
                    gens.append(back(slot - 1))
                if slot < ntiles:
                    gens.append(front(slot))
                run3(gens)

        XS1b = Buf("xs1")
        BNC = Buf("bounce")

        def sample_layer(l, ss):
            PTI = sb([128, NS * NPG], I32, "PTI", ss)
            IDXF = sb([128, NS * NPG], F32, "IDXF", ss)
            IDX = sb([128, NS * NPG], I32, "IDX", ss)
            PGL = [sb([128, NPG, 256], F32, "PGa", ss), sb([128, NPG, 256], F32, "PGb", ss)]
            PGbL = [[Buf(f"PGa{j}") for j in range(NPG)], [Buf(f"PGb{j}") for j in range(NPG)]]
            PG = PGL[0]
            PGb = PGbL[0]
            KCS = sb([64, NS, 2, 32], BF16, "KCS", ss)
            VCS = sb([32, NS, 2, 64], BF16, "VCS", ss)
            QTS = sb([64, 8, NS], BF16, "QTS", ss)
            KNB = sb([NS, 2, 128], BF16, "KNB", ss)
            VNB2 = sb([NS, 2, 2, 64], BF16, "VNB2", ss)
            KNT = sb([64, 4, NS], BF16, "KNT", ss)
            SCs = sb([4, 32, 32], F32, "SCs", ss)
            SM4 = sb([4, 96], F32, "SM4", ss)
            PCT2 = sb([32, 4, 32], F32, "PCT2", ss)
            PCT2B = sb([32, 4, 32], BF16, "PCT2B", ss)
            PCTg = sb([32, 4, 32], BF16, "PCTg", ss)
            IMPs = sb([32, 32], F32, "IMPs", ss)
            SCRs = sb([32, 40], F32, "SCRs", ss)
            WK1s = sb([32, 40], F32, "WK1s", ss)
            WK2s = sb([32, 40], F32, "WK2s", ss)
            M8s = sb([32, 8], F32, "M8s", ss)
            SELs = sb([32, 33], BF16, "SELs", ss)
            SELTs = sb([33, 32], BF16, "SELTs", ss)
            SFORCE = sb([32, 33], F32, "SFORCE", ss)
            SEXPE = sb([33, 17 * 128], BF16, "SEXPE", ss)
            MTS = sb([128, 17, 32], BF16, "MTS", ss)
            SALB = sb([128, 17, 8], F32, "SALB", ss)
            SWALB = sb([128, 5, 8], F32, "SWALB", ss)
            SWMASK = sb([128, 5], F32, "SWMASK", ss)
            SCALB = sb([4, 2, 32], F32, "SCALB", ss)
            OSS = sb([4, 32, 65], F32, "OSS", ss)
            OT = sb([NS, 3, 8, 65], F32, "OT", ss)
            xt = X[0]

            for tb, src in [(SFORCE, c_sforce), (SEXPE, c_sexpe), (SALB, c_salb), (SWALB, c_swalb),
                            (SWMASK, c_swmask), (SCALB, c_scalb)]:
                load(tb[:], src, [tb.b])
            S.op("sp", lambda e: e.dma_start(out=kvw_s[l, :, 0:511, :], in_=ckw[l, :, 1:512, :]), dma="cp_kvw")
            S.op("sp", lambda e: e.dma_start(out=pool_s[l, :, 0:14, :], in_=spool[l, :, 1:15, :]), dma="cp_pool")
            load(PTI[:], ptab, [PTI.b])
            S.op("dve", lambda e: e.tensor_copy(out=IDXF[:], in_=PTI[:]), reads=[PTI.b], writes=[IDXF.b])
            S.op("dve", lambda e: e.tensor_scalar(out=IDXF[:], in0=IDXF[:], scalar1=128.0, scalar2=cmisc[:, 3:4],
                                                  op0=ALU.mult, op1=ALU.add), reads=[IDXF.b, cmisc.b], writes=[IDXF.b])
            S.op("dve", lambda e: e.tensor_scalar(out=IDXF[:], in0=IDXF[:], scalar1=float(l * NPHYS * 128), scalar2=None,
                                                  op0=ALU.add), reads=[IDXF.b], writes=[IDXF.b])
            S.op("dve", lambda e: e.tensor_copy(out=IDX[:], in_=IDXF[:]), reads=[IDXF.b], writes=[IDX.b])
            S.op("dve", lambda e: e.memset(SCRs[:], 0.0), writes=[SCRs.b])
            S.op("dve", lambda e: e.memset(KCS[:], 0.0), writes=[KCS.b])
            S.op("dve", lambda e: e.memset(VCS[:], 0.0), writes=[VCS.b])
            S.op("dve", lambda e: e.memset(OSS[:], 0.0), writes=[OSS.b])

            src = xs if l == 0 else xs1
            load(xt[0:NS, :], src, [xt.b], reads=([XS1b] if l == 1 else []))
            run(project(NS, xt, 0))
            if (l + 1) in layers:
                load_win(l + 1)
            GT, SZ, MIX, YA = GTL[0], SZL[0], MIXL[0], YAL[0]
            store(kvc_s[l], KVF[0:NS, 0:256], [KVF.b], "o_kvf")
            store(kvs_s[l], KVF[0:NS, 256:512], [KVF.b], "o_kvf")
            store(kvw_s[l, :, 511, :], KVF[0:NS, 512:768], [KVF.b], "o_kvf")
            store(pool_s[l, :, 14, :], PIN[0:NS, :], [PIN.b], "o_pin")

            def gather(cache, s_, sl):
                for j in range(NPG):
                    col = s_ * NPG + j
                    S.op("pool", lambda e, j=j, col=col: e.indirect_dma_start(
                        out=PGL[sl][:, j, :], out_offset=None, in_=cache[:, :],
                        in_offset=bass.IndirectOffsetOnAxis(ap=IDX[:, col:col + 1], axis=0)),
                         reads=[IDX.b], writes=[PGbL[sl][j]], dma=f"L_PG{sl}")

            for h in range(8):
                S.op("pe", lambda e, h=h: e.transpose(out=TP[0:64, h * NS:(h + 1) * NS], in_=QB[0:NS, h * 64:(h + 1) * 64],
                                                      identity=ident[0:NS, 0:NS]), reads=[QB.b, ident.b], writes=[TP.b])
            S.op("act", lambda e: e.copy(out=QTS[:].rearrange("p h s -> p (h s)"), in_=TP[0:64, 0:8 * NS]),
                 reads=[TP.b], writes=[QTS.b])
            S.op("dve", lambda e: e.tensor_copy(out=KNB[:, 0, :], in_=KVF[0:NS, 256:384]), reads=[KVF.b], writes=[KNB.b])
            S.op("dve", lambda e: e.tensor_copy(out=KNB[:, 1, :], in_=KVF[0:NS, 512:640]), reads=[KVF.b], writes=[KNB.b])
            S.op("dve", lambda e: e.tensor_copy(out=VNB2[:, 0, :, :], in_=KVF[0:NS, 384:512].rearrange("p (k d) -> p k d", k=2)),
                 reads=[KVF.b], writes=[VNB2.b])
            S.op("dve", lambda e: e.tensor_copy(out=VNB2[:, 1, :, :], in_=KVF[0:NS, 640:768].rearrange("p (k d) -> p k d", k=2)),
                 reads=[KVF.b], writes=[VNB2.b])
            for j in range(4):
                S.op("pe", lambda e, j=j: e.transpose(out=TP[0:64, j * NS:(j + 1) * NS],
                                                      in_=KNB[:, j // 2, (j % 2) * 64:(j % 2) * 64 + 64],
                                                      identity=ident[0:NS, 0:NS]), reads=[KNB.b, ident.b], writes=[TP.b])
            S.op("act", lambda e: e.copy(out=KNT[:].rearrange("p j s -> p (j s)"), in_=TP[0:64, 0:4 * NS]),
                 reads=[TP.b], writes=[KNT.b])

            sa = ss.enter_context(ExitStack())
            KCPs = sb([128, NPG, 256], BF16, "KCPs", sa)
            KCTs = sb([128, 2, NPG, 128], BF16, "KCTs", sa)
            HIDs = sb([128, 512], F32, "HIDs", sa)
            HBS = sb([128, 2, 32], F32, "HBS", sa)
            HBTs = sb([128, 2, 32], BF16, "HBTs", sa)
            nsq = min(NS, ns_limit)
            gather(ckc, 0, 0)
            for s_ in range(nsq):
                if s_ + 1 < nsq:
                    gather(ckc, s_ + 1, (s_ + 1) % 2)
                else:
                    gather(cks, 0, (s_ + 1) % 2)
                PG = PGL[s_ % 2]
                PGb = PGbL[s_ % 2]
                S.op("dve", lambda e, PG=PG: e.tensor_tensor(out=KCPs[:], in0=PG[:],
                                                      in1=PETOK[:].unsqueeze(1).to_broadcast([128, NPG, 256]), op=ALU.add),
                     reads=PGb + [PETOK.b], writes=[KCPs.b])
                for q4 in range(4):
                    for jl in range(4):
                        for r in range(2):
                            S.op("pe", lambda e, q4=q4, jl=jl, r=r: e.transpose(
                                out=TP[:, (jl * 2 + r) * 128:(jl * 2 + r + 1) * 128],
                                in_=KCPs[:, q4 * 4 + jl, r * 128:(r + 1) * 128], identity=ident[:]),
                                 reads=[KCPs.b, ident.b], writes=[TP.b])
                    S.op("act", lambda e, q4=q4: e.copy(
                        out=KCTs[:, :, q4 * 4:(q4 + 1) * 4, :].rearrange("p r j t -> p j r t"),
                        in_=TP[:].rearrange("p (j r t) -> p j r t", j=4, r=2)), reads=[TP.b], writes=[KCTs.b])
                for r in range(2):
                    for ch in range(4):
                        pj = PJ[(r * 4 + ch) % 2]
                        S.op("pe", lambda e, r=r, ch=ch, pj=pj: e.matmul(
                            pj[:, :], lhsT=W1BD[:, r, :],
                            rhs=KCTs[:, r, ch * 4:(ch + 1) * 4, :].rearrange("p j t -> p (j t)"), start=True, stop=True),
                             reads=[W1BD.b, KCTs.b], writes=[pj.b])
                        S.op("act", lambda e, pj=pj: e.activation(out=HIDs[:], in_=pj[:, :], func=AF.Tanh, scale=0.5),
                             reads=[pj.b], writes=[HIDs.b])
                        S.op("dve", lambda e, pj=pj: e.scalar_tensor_tensor(out=HIDs[:], in0=HIDs[:], scalar=1.0, in1=pj[:, :],
                                                                            op0=ALU.add, op1=ALU.mult), reads=[pj.b, HIDs.b], writes=[HIDs.b])
                        S.op("dve", lambda e, r=r, ch=ch: e.tensor_reduce(
                            out=HBS[:, r, ch * 8:(ch + 1) * 8], in_=HIDs[:].rearrange("p (b c) -> p b c", c=64),
                            axis=AX.X, op=ALU.add), reads=[HIDs.b], writes=[HBS.b])
                S.op("dve", lambda e: e.tensor_copy(out=HBTs[:], in_=HBS[:]), reads=[HBS.b], writes=[HBTs.b])
                for kv in range(2):
                    S.op("pe", lambda e, kv=kv: e.matmul(MS[0:64, kv * 32:(kv + 1) * 32], lhsT=W2KV[:, kv, :], rhs=HBTs[:, 0, :],
                                                         start=True, stop=True), reads=[W2KV.b, HBTs.b], writes=[MS.b])
                    S.op("pe", lambda e, kv=kv: e.matmul(MS[0:32, 64 + kv * 64:64 + (kv + 1) * 64], lhsT=HBTs[:, 1, :],
                                                         rhs=W2KV[:, 2 + kv, :], start=True, stop=True),
                         reads=[W2KV.b, HBTs.b], writes=[MS.b])
                S.op("act", lambda e, s_=s_: e.copy(out=KCS[:, s_, :, :].rearrange("p k n -> p (k n)"), in_=MS[0:64, 0:64]),
                     reads=[MS.b], writes=[KCS.b])
                S.op("act", lambda e, s_=s_: e.copy(out=VCS[:, s_, :, :].rearrange("p k n -> p (k n)"), in_=MS[0:32, 64:192]),
                     reads=[MS.b], writes=[VCS.b])

            sa.close()
            S.fence()
            for s_ in range(NS):
                for kv in range(2):
                    j = s_ * 2 + kv
                    pj = PJ[j // 16]
                    S.op("pe", lambda e, s_=s_, kv=kv, j=j, pj=pj: e.matmul(
                        pj[0:4, (j % 16) * 32:(j % 16 + 1) * 32], lhsT=QTS[:, kv * 4:(kv + 1) * 4, s_],
                        rhs=KCS[:, s_, kv, :], start=True, stop=True), reads=[QTS.b, KCS.b], writes=[pj.b])
            for hf in range(2):
                S.op("dve", lambda e, hf=hf: e.tensor_tensor(
                    out=SCs[:, hf * 16:(hf + 1) * 16, :].rearrange("p (s k) n -> p s (k n)", k=2),
                    in0=PJ[hf][0:4, :].rearrange("p (s kn) -> p s kn", s=8),
                    in1=SCALB[:].rearrange("p k n -> p (k n)").unsqueeze(1).to_broadcast([4, 8, 64]), op=ALU.add),
                     reads=[PJ[hf].b, SCALB.b], writes=[SCs.b])
            S.op("dve", lambda e: e.tensor_reduce(out=SM4[:, 0:32], in_=SCs[:], axis=AX.X, op=ALU.max), reads=[SCs.b], writes=[SM4.b])
            S.op("dve", lambda e: e.tensor_tensor(out=SCs[:], in0=SCs[:], in1=SM4[:, 0:32].unsqueeze(2).to_broadcast([4, 32, 32]),
                                                  op=ALU.subtract), reads=[SCs.b, SM4.b], writes=[SCs.b])
            S.op("act", lambda e: e.activation(out=SCs[:].rearrange("p j n -> p (j n)"), in_=SCs[:].rearrange("p j n -> p (j n)"),
                                               func=AF.Exp), reads=[SCs.b], writes=[SCs.b])
            S.op("dve", lambda e: e.tensor_reduce(out=SM4[:, 32:64], in_=SCs[:], axis=AX.X, op=ALU.add), reads=[SCs.b], writes=[SM4.b])
            S.op("dve", lambda e: e.reciprocal(out=SM4[:, 64:96], in_=SM4[:, 32:64]), reads=[SM4.b], writes=[SM4.b])
            S.op("dve", lambda e: e.tensor_tensor(out=SCs[:], in0=SCs[:], in1=SM4[:, 64:96].unsqueeze(2).to_broadcast([4, 32, 32]),
                                                  op=ALU.mult), reads=[SCs.b, SM4.b], writes=[SCs.b])
            store(bounce[0:4, 0:1024], SCs[:].rearrange("p j n -> p (j n)"), [SCs.b], "o_bnc", writes=[BNC])
            load(PCT2[:], bounce[0:4, 0:1024].rearrange("g (j n) -> j g n", j=32), [PCT2.b], reads=[BNC])
            S.op("dve", lambda e: e.tensor_reduce(out=IMPs[:], in_=PCT2[:].rearrange("p g n -> p n g"), axis=AX.X, op=ALU.add),
                 reads=[PCT2.b], writes=[IMPs.b])
            S.op("dve", lambda e: e.tensor_tensor(out=SCRs[:, 0:32], in0=IMPs[:], in1=SFORCE[:, 0:32], op=ALU.add),
                 reads=[IMPs.b, SFORCE.b], writes=[SCRs.b])
            S.op("dve", lambda e: e.tensor_copy(out=SCRs[:, 32:33], in_=SFORCE[:, 32:33]), reads=[SFORCE.b], writes=[SCRs.b])
            S.op("dve", lambda e: e.max(out=M8s[:], in_=SCRs[:, 0:33]), reads=[SCRs.b], writes=[M8s.b])
            S.op("dve", lambda e: e.match_replace(out=WK1s[:, 0:33], in_to_replace=M8s[:], in_values=SCRs[:, 0:33], imm_value=NEG),
                 reads=[SCRs.b, M8s.b], writes=[WK1s.b])
            S.op("dve", lambda e: e.max(out=M8s[:], in_=WK1s[:, 0:33]), reads=[WK1s.b], writes=[M8s.b])
            S.op("dve", lambda e: e.match_replace(out=WK2s[:, 0:33], in_to_replace=M8s[:], in_values=WK1s[:, 0:33], imm_value=NEG),
                 reads=[WK1s.b, M8s.b], writes=[WK2s.b])
            S.op("dve", lambda e: e.tensor_tensor(out=SELs[:], in0=WK2s[:, 0:33], in1=SCRs[:, 0:33], op=ALU.not_equal),
                 reads=[WK2s.b, SCRs.b], writes=[SELs.b])
            S.op("pe", lambda e: e.transpose(out=TP[0:33, 0:32], in_=SELs[:], identity=ident[0:32, 0:32]),
                 reads=[SELs.b, ident.b], writes=[TP.b])
            S.op("act", lambda e: e.copy(out=SELTs[:], in_=TP[0:33, 0:32]), reads=[TP.b], writes=[SELTs.b])
            for kt in range(17):
                pj = PJ[0] if kt < 9 else PJ[1]
                k0 = kt if kt < 9 else kt - 9
                S.op("pe", lambda e, kt=kt, pj=pj, k0=k0: e.matmul(pj[:, k0 * 32:(k0 + 1) * 32], lhsT=SEXPE[:, kt * 128:(kt + 1) * 128],
                                                                   rhs=SELTs[:], start=True, stop=True),
                     reads=[SEXPE.b, SELTs.b], writes=[pj.b])
            S.op("act", lambda e: e.copy(out=MTS[:, 0:9, :].rearrange("p k j -> p (k j)"), in_=PJ[0][:, 0:288]),
                 reads=[PJ[0].b], writes=[MTS.b])
            S.op("act", lambda e: e.copy(out=MTS[:, 9:17, :].rearrange("p k j -> p (k j)"), in_=PJ[1][:, 0:256]),
                 reads=[PJ[1].b], writes=[MTS.b])
            S.op("dve", lambda e: e.tensor_copy(out=PCT2B[:], in_=PCT2[:]), reads=[PCT2.b], writes=[PCT2B.b])
            for g in range(4):
                S.op("pe", lambda e, g=g: e.transpose(out=TP[0:32, g * 32:(g + 1) * 32], in_=PCT2B[:, g, :], identity=ident[0:32, 0:32]),
                     reads=[PCT2B.b, ident.b], writes=[TP.b])
            S.op("act", lambda e: e.copy(out=PCTg[:].rearrange("p g j -> p (g j)"), in_=TP[0:32, 0:128]), reads=[TP.b], writes=[PCTg.b])
            for rnd in range(4):
                pj = PJ[rnd % 2]
                for jj in range(8):
                    j = rnd * 8 + jj
                    s_, kv = j // 2, j % 2
                    S.op("pe", lambda e, j=j, jj=jj, s_=s_, kv=kv, pj=pj: e.matmul(
                        pj[0:4, jj * 64:(jj + 1) * 64], lhsT=PCTg[:, :, j], rhs=VCS[:, s_, kv, :], start=True, stop=True),
                         reads=[PCTg.b, VCS.b], writes=[pj.b])
                S.op("act", lambda e, rnd=rnd, pj=pj: e.copy(out=OSS[:, rnd * 8:(rnd + 1) * 8, 0:64],
                                                            in_=pj[0:4, :].rearrange("p (j d) -> p j d", j=8)),
                     reads=[pj.b], writes=[OSS.b])

            def bounce_out(br):
                store(bounce[0:4, 0:2080], OSS[:].rearrange("p j d -> p (j d)"), [OSS.b], "o_bnc", writes=[BNC])
                load(OT[:, br, :, :].rearrange("s (k g) d -> s k g d", k=2),
                     bounce[0:4, 0:2080].rearrange("g (s k d) -> s k g d", s=NS, k=2), [OT.b], reads=[BNC])

            bounce_out(0)

            S.fence()
            sb2 = ss.enter_context(ExitStack())
            KSB = sb([128, NPG, 128], BF16, "KSB", sb2)
            KTSs = sb([64, 2, 17, 128], BF16, "KTSs", sb2)
            VSs = sb([128, 17, 2, 65], BF16, "VSs", sb2)
            KTWs = sb([64, 2, 5, 128], BF16, "KTWs", sb2)
            VWs = sb([128, 5, 2, 65], BF16, "VWs", sb2)
            SSs = sb([128, 17, 4], F32, "SSs", sb2)
            PTs = sb([128, 17, 4], BF16, "PTs", sb2)
            S.op("dve", lambda e: e.memset(KTSs[:], 0.0), writes=[KTSs.b])
            S.op("dve", lambda e: e.memset(KTWs[:], 0.0), writes=[KTWs.b])
            S.op("dve", lambda e: e.memset(VSs[:], 0.0), writes=[VSs.b])
            S.op("dve", lambda e: e.memset(VWs[:], 0.0), writes=[VWs.b])
            S.op("dve", lambda e: e.memset(VSs[:, :, :, 64:65], 1.0), writes=[VSs.b])
            S.op("dve", lambda e: e.memset(VWs[:, :, :, 64:65], 1.0), writes=[VWs.b])
            def attend_s(s_, br):
                nkt = 17 if br == 1 else 5
                KTt, Vt = (KTSs, VSs) if br == 1 else (KTWs, VWs)
                ALBt = SALB if br == 1 else SWALB
                for kv in range(2):
                    j = s_ * 2 + kv
                    st = ST[j % 2]
                    for kt in range(nkt):
                        S.op("pe", lambda e, kt=kt, st=st, kv=kv: e.matmul(
                            st[:, kt * 4:(kt + 1) * 4], lhsT=KTt[:, kv, kt, :], rhs=QTS[:, kv * 4:(kv + 1) * 4, s_],
                            start=True, stop=True), reads=[KTt.b, QTS.b], writes=[st.b])
                    S.op("dve", lambda e, st=st, kv=kv: e.tensor_tensor(
                        out=SSs[:, 0:nkt, :], in0=st[:, 0:nkt * 4].rearrange("p (k g) -> p k g", g=4),
                        in1=ALBt[:, :, kv * 4:(kv + 1) * 4], op=ALU.add), reads=[st.b, ALBt.b], writes=[SSs.b])
                    S.op("act", lambda e: e.activation(out=PTs[:, 0:nkt, :], in_=SSs[:, 0:nkt, :], func=AF.Exp),
                         reads=[SSs.b], writes=[PTs.b])
                    if br == 1:
                        m_ap = MTS[:, :, j:j + 1].to_broadcast([128, 17, 4])
                        mrd = [MTS.b]
                    else:
                        m_ap = SWMASK[:].unsqueeze(2).to_broadcast([128, 5, 4])
                        mrd = [SWMASK.b]
                    S.op("dve", lambda e, m_ap=m_ap: e.tensor_tensor(out=PTs[:, 0:nkt, :], in0=PTs[:, 0:nkt, :], in1=m_ap, op=ALU.mult),
                         reads=[PTs.b] + mrd, writes=[PTs.b])
                    for kt in range(nkt):
                        S.op("pe", lambda e, kt=kt, kv=kv: e.matmul(OA[0:4, 0:65], lhsT=PTs[:, kt, :], rhs=Vt[:, kt, kv, :],
                                                                     start=(kt == 0), stop=(kt == nkt - 1)),
                             reads=[PTs.b, Vt.b], writes=[OA.b])
                    S.op("act", lambda e, j=j: e.copy(out=OSS[:, j, :], in_=OA[0:4, 0:65]), reads=[OA.b], writes=[OSS.b])

            for s_ in range(nsq):
                slb = (nsq + s_) % 2
                if s_ + 1 < nsq:
                    gather(cks, s_ + 1, (nsq + s_ + 1) % 2)
                PG = PGL[slb]
                PGb = PGbL[slb]
                S.op("dve", lambda e, PG=PG: e.tensor_copy(out=KSB[:], in_=PG[:, :, 0:128]), reads=PGb, writes=[KSB.b])
                S.op("dve", lambda e, PG=PG: e.tensor_copy(out=VSs[:, 0:16, :, 0:64],
                                                    in_=PG[:, :, 128:256].rearrange("p j (k d) -> p j k d", k=2)),
                     reads=PGb, writes=[VSs.b])
                S.op("sp", lambda e, s_=s_: e.dma_start(out=VSs[0:1, 16, :, 0:64], in_=VNB2[s_:s_ + 1, 0, :, :]),
                     reads=[VNB2.b], writes=[VSs.b], dma="L_vnew")
                for q4 in range(4):
                    for jl in range(4):
                        for kv in range(2):
                            S.op("pe", lambda e, q4=q4, jl=jl, kv=kv: e.transpose(
                                out=TP[0:64, (kv * 4 + jl) * 128:(kv * 4 + jl + 1) * 128],
                                in_=KSB[:, q4 * 4 + jl, kv * 64:(kv + 1) * 64], identity=ident[:]),
                                 reads=[KSB.b, ident.b], writes=[TP.b])
                    S.op("act", lambda e, q4=q4: e.copy(out=KTSs[:, :, q4 * 4:(q4 + 1) * 4, :],
                                                        in_=TP[0:64, :].rearrange("p (k j t) -> p k j t", k=2, j=4)),
                         reads=[TP.b], writes=[KTSs.b])
                S.op("dve", lambda e, s_=s_: e.tensor_copy(out=KTSs[:, :, 16, 0:1], in_=KNT[:, 0:2, s_:s_ + 1]),
                     reads=[KNT.b], writes=[KTSs.b])
                attend_s(s_, 1)
            bounce_out(1)
            load(PGL[0][:, 0:4, :], ckw[l, 0].rearrange("(k p) f -> p k f", p=128), PGbL[0][0:4])
            for s_ in range(nsq):
                if s_ + 1 < nsq:
                    load(PGL[(s_ + 1) % 2][:, 0:4, :], ckw[l, s_ + 1].rearrange("(k p) f -> p k f", p=128),
                         PGbL[(s_ + 1) % 2][0:4])
                PG = PGL[s_ % 2]
                PGb = PGbL[s_ % 2]
                S.op("dve", lambda e, PG=PG: e.tensor_copy(out=KSB[:, 0:4, :], in_=PG[:, 0:4, 0:128]), reads=PGb[0:4], writes=[KSB.b])
                S.op("dve", lambda e, PG=PG: e.tensor_copy(out=VWs[:, 0:4, :, 0:64],
                                                    in_=PG[:, 0:4, 128:256].rearrange("p j (k d) -> p j k d", k=2)),
                     reads=PGb[0:4], writes=[VWs.b])
                S.op("sp", lambda e, s_=s_: e.dma_start(out=VWs[0:1, 4, :, 0:64], in_=VNB2[s_:s_ + 1, 1, :, :]),
                     reads=[VNB2.b], writes=[VWs.b], dma="L_vnew")
                for jl in range(4):
                    for kv in range(2):
                        S.op("pe", lambda e, jl=jl, kv=kv: e.transpose(
                            out=TP[0:64, (kv * 4 + jl) * 128:(kv * 4 + jl + 1) * 128],
                            in_=KSB[:, jl, kv * 64:(kv + 1) * 64], identity=ident[:]),
                             reads=[KSB.b, ident.b], writes=[TP.b])
                S.op("act", lambda e: e.copy(out=KTWs[:, :, 0:4, :], in_=TP[0:64, :].rearrange("p (k j t) -> p k j t", k=2, j=4)),
                     reads=[TP.b], writes=[KTWs.b])
                S.op("dve", lambda e, s_=s_: e.tensor_copy(out=KTWs[:, :, 4, 0:1], in_=KNT[:, 2:4, s_:s_ + 1]),
                     reads=[KNT.b], writes=[KTWs.b])
                attend_s(s_, 2)
            bounce_out(2)

            sb2.close()
            S.fence()
            sf = ss.enter_context(ExitStack())
            SPT = sb([NS, 15, 256], F32, "SPT", sf)
            PSUMS = sb([NS, 256], F32, "PSUMS", sf)
            DIFB = sb([NS, 256], BF16, "DIFB", sf)
            DTs = sb([64, 4, NS], BF16, "DTs", sf)
            CSs = sb([NS, 16], F32, "CSs", sf)
            M = NS
            S.op("dve", lambda e: e.tensor_tensor(
                out=YA[0:M, :].rearrange("p (h d) -> p h d", h=8), in0=OT[:, 0, :, 0:64],
                in1=GT[0:M, :].rearrange("p (h t) -> p h t", t=3)[:, :, 0:1].to_broadcast([M, 8, 64]), op=ALU.mult),
                 reads=[OT.b, GT.b], writes=[YA.b])
            for br in (1, 2):
                S.op("dve", lambda e, br=br: e.tensor_scalar(out=CSs[:, 0:8], in0=OT[:, br, :, 64], scalar1=1e-30, scalar2=None,
                                                             op0=ALU.max), reads=[OT.b], writes=[CSs.b])
                S.op("dve", lambda e, br=br: e.reciprocal(out=CSs[:, 0:8], in_=CSs[:, 0:8]), reads=[CSs.b], writes=[CSs.b])
                S.op("dve", lambda e, br=br: e.tensor_tensor(out=CSs[:, 8:16], in0=CSs[:, 0:8],
                                                             in1=GT[0:M, :].rearrange("p (h t) -> p h t", t=3)[:, :, br], op=ALU.mult),
                     reads=[CSs.b, GT.b], writes=[CSs.b])
                for h in range(8):
                    S.op("dve", lambda e, br=br, h=h: e.scalar_tensor_tensor(
                        out=YA[0:M, h * 64:(h + 1) * 64], in0=OT[:, br, h, 0:64], scalar=CSs[:, 8 + h:9 + h],
                        in1=YA[0:M, h * 64:(h + 1) * 64], op0=ALU.mult, op1=ALU.add),
                         reads=[OT.b, CSs.b, YA.b], writes=[YA.b])
            S.op("dve", lambda e: e.tensor_tensor(out=MIX[0:M, 0:512], in0=YA[0:M, :], in1=SZ[0:M, 0:512], op=ALU.mult),
                 reads=[YA.b, SZ.b], writes=[MIX.b])
            load(SPT[:], spool[l], [SPT.b])
            for g, w in enumerate((2, 4, 8, 16)):
                S.op("dve", lambda e, g=g, w=w: e.tensor_reduce(
                    out=PSUMS[:, g * 64:(g + 1) * 64],
                    in_=SPT[:, 15 - (w - 1):15, g * 64:(g + 1) * 64].rearrange("p r c -> p c r"), axis=AX.X, op=ALU.add),
                     reads=[SPT.b], writes=[PSUMS.b])
                S.op("dve", lambda e, g=g, w=w: e.tensor_scalar(out=PSUMS[:, g * 64:(g + 1) * 64], in0=PSUMS[:, g * 64:(g + 1) * 64],
                                                                scalar1=1.0 / w, scalar2=None, op0=ALU.mult),
                     reads=[PSUMS.b], writes=[PSUMS.b])
                S.op("dve", lambda e, g=g, w=w: e.scalar_tensor_tensor(
                    out=DIFB[:, g * 64:(g + 1) * 64], in0=PIN[0:M, g * 64:(g + 1) * 64], scalar=(1.0 / w - 1.0),
                    in1=PSUMS[:, g * 64:(g + 1) * 64], op0=ALU.mult, op1=ALU.add), reads=[PIN.b, PSUMS.b], writes=[DIFB.b])
            for g in range(4):
                S.op("pe", lambda e, g=g: e.transpose(out=TP[0:64, g * M:(g + 1) * M], in_=DIFB[:, g * 64:(g + 1) * 64],
                                                      identity=ident[0:M, 0:M]), reads=[DIFB.b, ident.b], writes=[TP.b])
            S.op("act", lambda e: e.copy(out=DTs[:].rearrange("p g s -> p (g s)"), in_=TP[0:64, 0:4 * M]), reads=[TP.b], writes=[DTs.b])
            for g in range(4):
                S.op("pe", lambda e, g=g: e.matmul(MS[0:M, g * 64:(g + 1) * 64], lhsT=DTs[:, g, :], rhs=WPOOL[:, g, :],
                                                   start=True, stop=True), reads=[DTs.b, WPOOL.b], writes=[MS.b])
            S.op("dve", lambda e: e.tensor_tensor(out=TMPB[0:M, :], in0=MS[0:M, 0:256], in1=PSC[0:M, :], op=ALU.mult),
                 reads=[MS.b, PSC.b], writes=[TMPB.b])
            S.op("dve", lambda e: e.tensor_tensor(out=MIX[0:M, 512:768], in0=TMPB[0:M, :], in1=SZ[0:M, 512:768], op=ALU.mult),
                 reads=[TMPB.b, SZ.b], writes=[MIX.b])
            gmlp_norm(M)
            store(gv_s[l], VN[0:M, :], [VN.b], "o_vn")
            for g in range(4):
                S.op("dve", lambda e, g=g: e.tensor_scalar(out=YC[0:M, g * 64:(g + 1) * 64], in0=VN[0:M, g * 64:(g + 1) * 64],
                                                           scalar1=WS00[0:M, g:g + 1], scalar2=BS0[0:M, g:g + 1],
                                                           op0=ALU.mult, op1=ALU.add), reads=[VN.b, WS00.b, BS0.b], writes=[YC.b])
            S.op("dve", lambda e: e.tensor_tensor(out=YC[0:M, :], in0=YC[0:M, :], in1=UV[0:M, 0:256], op=ALU.mult),
                 reads=[YC.b, UV.b], writes=[YC.b])
            S.op("dve", lambda e: e.tensor_tensor(out=MIX[0:M, 768:1024], in0=YC[0:M, :], in1=SZ[0:M, 768:1024], op=ALU.mult),
                 reads=[YC.b, SZ.b], writes=[MIX.b])
            if l == 0 and not dbg_y0:
                run(merge_out(M, xt, xs1, [XS1b], "o_y", 0))
            else:
                run(merge_out(M, xt, y_s, [], "o_y", 0))

        for l in layers:
            load_weights(l)
            if do_prompt:
                S.fence()
                with ExitStack() as pstack:
                    prompt_layer(l, pstack)
                S.fence()
            if do_sample:
                S.fence()
                with ExitStack() as sstack:
                    sample_layer(l, sstack)
                S.fence()
        S.emit()
    return nc


def _consts():
    bf = ml_dtypes.bfloat16
    c = {}
    c["c_ident"] = np.eye(128, dtype=np.float32).astype(bf)
    c["c_identf"] = np.eye(128, dtype=np.float32)
    kk = np.arange(128)[:, None]
    qq = np.arange(128)[None, :]
    c["c_trile"] = (kk <= qq).astype(np.float32).astype(bf)
    c["c_trigt"] = (kk > qq).astype(np.float32).astype(bf)
    tk = np.arange(T)
    c["c_kaug"] = np.stack([np.ones(T), np.ones(T), tk // 128, tk % 128]).astype(np.float32).astype(bf)
    qa = np.zeros((NT, 4, 2, 4, 128), np.float32)
    a = np.arange(128)
    for i in range(NT):
        for kv in range(2):
            for g in range(4):
                s = SLOPES[kv * 4 + g]
                qa[i, 0, kv, g, :] = -s * 128.0 * i
                qa[i, 1, kv, g, :] = -s * a
                qa[i, 2, kv, g, :] = s * 128.0
                qa[i, 3, kv, g, :] = s
    qa2 = np.broadcast_to(qa.reshape(NT, 4, 2, 1, 512), (NT, 4, 2, 2, 512))
    c["c_qaug"] = np.ascontiguousarray(qa2).reshape(NT, 4, 2048).astype(bf)
    jj = np.arange(32)[:, None]
    c["c_kmask"] = (((tk[None, :] // 64) % 32) == jj).astype(np.float32).astype(bf)
    n = np.arange(64)[:, None]
    c["c_expe"] = (n == (tk[None, :] // 64)).astype(np.float32).astype(bf)
    albm = np.zeros((128, 8, 64), np.float32)
    for h in range(8):
        albm[:, h, :] = SLOPES[h] * 64.0 * np.arange(64)[None, :]
    c["c_alb"] = albm.reshape(128, 512)
    c["c_iota"] = np.tile(np.arange(64, dtype=np.float32)[None, :], (128, 1))
    misc = np.zeros((128, 8), np.float32)
    misc[:, 0] = -1.0 + (a >= 63) + (a >= 127)
    misc[:, 1] = (a >= 64)
    misc[:, 2] = -0.5
    misc[:, 3] = a
    c["c_misc"] = misc
    f0 = np.zeros((128, 64), np.float32)
    f0[:, 0] = 1e4
    c["c_force0"] = f0
    pb = np.zeros((128, 3, 4, 128), np.float32)
    tt = np.arange(128)
    for g, w in enumerate((2, 4, 8, 16)):
        for tp in range(128):
            for s_ in range(w):
                t = tp - s_
                if t >= 0:
                    pb[t, 0, g, tp] += 1.0 / w
                    pb[t, 2, g, tp] += 1.0 / min(w, tp + 1)
                else:
                    pb[128 + t, 1, g, tp] += 1.0 / w
            pb[tp, 0, g, tp] -= 1.0
            pb[tp, 2, g, tp] -= 1.0
    c["c_poolb"] = pb.astype(bf)
    c["c_trilT"] = (kk <= qq).astype(np.float32)
    cc = np.arange(128)[:, None]
    salb = np.zeros((128, 17, 8), np.float32)
    swalb = np.zeros((128, 5, 8), np.float32)
    for h in range(8):
        for kt in range(17):
            salb[:, kt, h] = -SLOPES[h] * (2048.0 - (kt * 128 + np.arange(128)))
        for kt in range(5):
            swalb[:, kt, h] = -SLOPES[h] * (2048.0 - (1536 + kt * 128 + np.arange(128)))
    salb[1:, 16, :] = 0.0
    swalb[1:, 4, :] = 0.0
    c["c_salb"] = salb
    c["c_swalb"] = swalb
    swm = np.ones((128, 5), np.float32)
    swm[0, 0] = 0.0
    swm[1:, 4] = 0.0
    c["c_swmask"] = swm
    scalb = np.zeros((4, 2, 32), np.float32)
    for kv in range(2):
        for g in range(4):
            scalb[g, kv, :] = SLOPES[kv * 4 + g] * 64.0 * np.arange(32)
    c["c_scalb"] = scalb
    sexpe = np.zeros((33, 17 * 128), np.float32)
    for pos in range(2049):
        sexpe[pos // 64, pos] = 1.0
    c["c_sexpe"] = sexpe.astype(bf)
    sf = np.zeros((32, 33), np.float32)
    sf[:, 0] = 1e4
    sf[:, 32] = 1e4
    c["c_sforce"] = sf
    c["c_spool"] = np.zeros((16, 4, 16), np.float32)
    c["c_smask0"] = np.zeros((128, 17), np.float32)
    return c


_NC_CACHE = {}


def kernel(x_prompt, x_sample, cache_kv_cmp, cache_kv_sel, cache_kv_win, state_pool, page_table,
           norm_pre, w_in, cmp_pe, cmp_w1, cmp_w2, pool_w, pool_scale, gmlp_norm, gmlp_ws, gmlp_bs,
           w_out, norm_post, _build_kwargs=None):
    kw = _build_kwargs or {}
    key = tuple(sorted((k, str(v)) for k, v in kw.items()))
    if key not in _NC_CACHE:
        _NC_CACHE[key] = build_program(**kw)
    nc = _NC_CACHE[key]
    in_maps = _prepare(kw, x_prompt, x_sample, cache_kv_cmp, cache_kv_sel, cache_kv_win, state_pool, page_table,
                       norm_pre, w_in, cmp_pe, cmp_w1, cmp_w2, pool_w, pool_scale, gmlp_norm, gmlp_ws, gmlp_bs,
                       w_out, norm_post)
    res = run_bass_kernel_spmd(nc, in_maps, core_ids=list(range(NCORES)))
    return _assemble(res.results)


def _prepare(kw, x_prompt, x_sample, cache_kv_cmp, cache_kv_sel, cache_kv_win, state_pool, page_table,
             norm_pre, w_in, cmp_pe, cmp_w1, cmp_w2, pool_w, pool_scale, gmlp_norm, gmlp_ws, gmlp_bs,
             w_out, norm_post):
    f32 = np.float32
    consts = _consts()
    npre_t = np.ascontiguousarray(np.asarray(norm_pre, f32).reshape(2, 8, 128).transpose(0, 2, 1))
    gpost_rep = np.ascontiguousarray(np.broadcast_to(np.asarray(norm_post, f32)[:, None, :], (2, 128, D)))
    pe = np.asarray(cmp_pe, f32)
    pe_tok = np.zeros((2, 128, 2, 2, 64), f32)
    for kv in range(2):
        pe_tok[:, 0:64, :, kv, :] = pe.transpose(0, 2, 1, 3)
        pe_tok[:, 64:128, :, kv, :] = pe.transpose(0, 2, 1, 3)
    pe_tok = pe_tok.reshape(2, 128, 256)
    w1 = np.asarray(cmp_w1, f32)
    w1bd = np.zeros((2, 128, 2, 128), f32)
    for r in range(2):
        w1bd[:, 0:64, r, 0:64] = w1[:, r]
        w1bd[:, 64:128, r, 64:128] = w1[:, r]
    w2 = np.asarray(cmp_w2, f32)
    w2kv = np.zeros((2, 128, 4, 64), f32)
    for kv in range(2):
        w2kv[:, kv * 64:(kv + 1) * 64, kv, :] = w2[:, 0]
        w2kv[:, kv * 64:(kv + 1) * 64, 2 + kv, :] = w2[:, 1]
    wpool = np.ascontiguousarray(np.asarray(pool_w, f32).transpose(0, 2, 1, 3))
    pscale_rep = np.ascontiguousarray(np.broadcast_to(np.asarray(pool_scale, f32)[:, None, :], (2, 128, 256)))
    gnorm_rep = np.ascontiguousarray(np.broadcast_to(np.asarray(gmlp_norm, f32)[:, None, :], (2, 128, 256)))
    ws = np.asarray(gmlp_ws, f32)
    wsT = np.ascontiguousarray(ws.transpose(0, 3, 1, 2))
    bsT = np.ascontiguousarray(np.asarray(gmlp_bs, f32).transpose(0, 2, 1))
    ws00_rep = np.ascontiguousarray(np.broadcast_to(ws[:, None, :, 0, 0], (2, 128, 4)))
    bs0_rep = np.ascontiguousarray(np.broadcast_to(np.asarray(gmlp_bs, f32)[:, None, :, 0], (2, 128, 4)))
    if kw.get("do_sample", True):
        ckc = np.asarray(cache_kv_cmp, f32).reshape(2 * NPHYS * 128, 256)
        cks = np.asarray(cache_kv_sel, f32).reshape(2 * NPHYS * 128, 256)
    else:
        ckc = cks = np.zeros((128, 256), f32)
    ckw_all = np.asarray(cache_kv_win, f32).reshape(2, 128, 512, 256)
    sp_all = np.asarray(state_pool, f32)
    xs_all = np.asarray(x_sample, f32).reshape(128, D)
    pt_all = np.asarray(page_table, np.int32)
    shared = dict(w_in=np.asarray(w_in, f32), w_out=np.asarray(w_out, f32), npre_t=npre_t, gpost_rep=gpost_rep,
                  pe_tok=pe_tok, w1bd=w1bd, w2kv=w2kv, wpool=wpool, pscale_rep=pscale_rep, gnorm_rep=gnorm_rep,
                  wsT=wsT, bsT=bsT, ws00_rep=ws00_rep, bs0_rep=bs0_rep, ckc=ckc, cks=cks, **consts)
    in_maps = []
    for c in range(NCORES):
        m = dict(shared)
        m["xp"] = np.ascontiguousarray(np.asarray(x_prompt, f32)[c % 4])
        sl = slice(c * NS, (c + 1) * NS)
        m["xs"] = np.ascontiguousarray(xs_all[sl])
        m["ckw"] = np.ascontiguousarray(ckw_all[:, sl])
        m["spool"] = np.ascontiguousarray(sp_all[:, sl])
        m["ptab"] = np.ascontiguousarray(np.broadcast_to(pt_all[sl].reshape(1, NS * NPG), (128, NS * NPG)))
        in_maps.append(m)
    return in_maps


def _assemble(R):
    y_prompt = np.stack([R[b]["y_p"] for b in range(4)])
    y_sample = np.concatenate([R[c]["y_s"] for c in range(NCORES)], 0).reshape(128, 1, D)

    def pstk(name, shp):
        return np.stack([R[b][name] for b in range(4)], axis=1).reshape(shp)

    def sstk(name, shp):
        return np.concatenate([R[c][name] for c in range(NCORES)], axis=1).reshape(shp)

    outs = (
        y_prompt, y_sample,
        pstk("kvc_p", (2, 4, T, 2, 2, 64)), sstk("kvc_s", (2, 128, 1, 2, 2, 64)),
        pstk("kvs_p", (2, 4, T, 2, 2, 64)), sstk("kvs_s", (2, 128, 1, 2, 2, 64)),
        pstk("kvw_p", (2, 4, 512, 2, 2, 64)), sstk("kvw_s", (2, 128, 512, 2, 2, 64)),
        pstk("pool_p", (2, 4, 15, 256)), sstk("pool_s", (2, 128, 15, 256)),
        pstk("gv_p", (2, 4, 128, 256)), sstk("gv_s", (2, 128, 1, 256)),
    )
    return tuple(np.ascontiguousarray(o, dtype=np.float32) for o in outs)
```
